# Optimizing a Trainium2 kernel written in Bass

```python
import math
import jax, jax.numpy as jnp
from jax import lax
import numpy as np

D_MODEL = 1024
BATCH = 16
SEQ = 2048
DEPTH = 4

GRID_W = 64
CTX_LEN = 256
N_MIXERS = 3
N_LAYERS_A = (DEPTH + 2) // 3
N_LAYERS_B = (DEPTH + 1) // 3
N_LAYERS_C = DEPTH // 3
MLP_HIDDEN = 4 * D_MODEL
NORM_EPS = 1e-6
ROPE_THETA = 10000.0
Q_BLOCK = 128
N_MOD = 6
DA_HEAD_DIM = 64
DA_HEADS = D_MODEL // (2 * DA_HEAD_DIM)
DA_SCALE = DA_HEAD_DIM ** -0.5
MLA_HEADS = 16
MLA_Q_LORA = 256
MLA_KV_LORA = 128
MLA_NOPE = 64
MLA_ROPE = 32
MLA_V = 64
MLA_SCALE = (MLA_NOPE + MLA_ROPE) ** -0.5
SC_WIDTH = 3

kernel_name = "hybrid_diffattn_mla_shortconv_dit"


def rms_norm(x, g):
    xf = x.astype(jnp.float32)
    y = xf * lax.rsqrt(jnp.mean(xf * xf, axis=-1, keepdims=True) + NORM_EPS)
    return (y * g.astype(jnp.float32)).astype(x.dtype)


def modulate(x, shift, scale):
    return x * (1 + scale) + shift


def axial_rope_tables(row, col, rot_dim):
    quarter = rot_dim // 4
    inv_freq = ROPE_THETA ** (-jnp.arange(quarter, dtype=jnp.float32) / quarter)
    ang = jnp.concatenate([row[:, None] * inv_freq, col[:, None] * inv_freq], axis=-1)
    return jnp.cos(ang), jnp.sin(ang)


def apply_rope(x, cos, sin):
    half = x.shape[-1] // 2
    xf = x.astype(jnp.float32)
    x1, x2 = xf[..., :half], xf[..., half:]
    return jnp.concatenate([x1 * cos - x2 * sin, x2 * cos + x1 * sin], axis=-1).astype(x.dtype)


def sweep_query_blocks(fn, qs):
    b, s = qs[0].shape[:2]
    nblk = s // Q_BLOCK
    blocks = tuple(jnp.moveaxis(q.reshape((b, nblk, Q_BLOCK) + q.shape[2:]), 1, 0) for q in qs)
    out = lax.map(lambda qb: fn(*qb), blocks)
    return jnp.moveaxis(out, 0, 1).reshape(b, s, -1)


def _diff_attend(q, k, v, lam, subln_g, lambda_init):
    b, nq = q.shape[0], q.shape[1]
    s = jnp.einsum('bqhd,bkhd->bhqk', q, k, preferred_element_type=jnp.float32) * DA_SCALE
    p = jax.nn.softmax(s, axis=-1).reshape(b, DA_HEADS, 2, nq, -1)
    a = p[:, :, 0] - lam * p[:, :, 1]
    o = jnp.einsum('bhqk,bkhe->bqhe', a.astype(v.dtype), v)
    o = rms_norm(o, subln_g) * (1.0 - lambda_init)
    return o.reshape(b, nq, DA_HEADS * 2 * DA_HEAD_DIM)


def diff_attention(a_lat, a_ctx, w_qkv, lam_vecs, subln_g, w_out, lambda_init, cos, sin, ctx_out):
    def project(a):
        b, n, _ = a.shape
        q, k, v = jnp.split(a @ w_qkv, 3, axis=-1)
        return (q.reshape(b, n, 2 * DA_HEADS, DA_HEAD_DIM),
                k.reshape(b, n, 2 * DA_HEADS, DA_HEAD_DIM),
                v.reshape(b, n, DA_HEADS, 2 * DA_HEAD_DIM))

    q_l, k_l, v_l = project(a_lat)
    q_c, k_c, v_c = project(a_ctx)
    cs, sn = cos[:, None, :], sin[:, None, :]
    q_l = apply_rope(q_l, cs, sn)
    k_l = apply_rope(k_l, cs, sn)
    lv = lam_vecs.astype(jnp.float32)
    lam = jnp.exp(jnp.sum(lv[0] * lv[1])) - jnp.exp(jnp.sum(lv[2] * lv[3])) + lambda_init
    k_all = jnp.concatenate([k_c, k_l], axis=1)
    v_all = jnp.concatenate([v_c, v_l], axis=1)
    y_lat = sweep_query_blocks(
        lambda qb: _diff_attend(qb, k_all, v_all, lam, subln_g, lambda_init), (q_l,)) @ w_out
    y_ctx = _diff_attend(q_c, k_c, v_c, lam, subln_g, lambda_init) @ w_out if ctx_out else None
    return y_lat, y_ctx


def _mla_attend(qn, qr, kn, kr, v):
    s = (jnp.einsum('bqhd,bkhd->bhqk', qn, kn, preferred_element_type=jnp.float32)
         + jnp.einsum('bqhd,bkd->bhqk', qr, kr, preferred_element_type=jnp.float32)) * MLA_SCALE
    p = jax.nn.softmax(s, axis=-1).astype(v.dtype)
    o = jnp.einsum('bhqk,bkhe->bqhe', p, v)
    return o.reshape(o.shape[0], o.shape[1], MLA_HEADS * MLA_V)


def mla_attention(a_lat, a_ctx, w_down, q_norm_g, w_uq, kv_norm_g, w_ukv, w_out, cos, sin, ctx_out):
    def project(a):
        b, n, _ = a.shape
        cq, ckv, kr = jnp.split(a @ w_down, [MLA_Q_LORA, MLA_Q_LORA + MLA_KV_LORA], axis=-1)
        q = (rms_norm(cq, q_norm_g) @ w_uq).reshape(b, n, MLA_HEADS, MLA_NOPE + MLA_ROPE)
        kv = (rms_norm(ckv, kv_norm_g) @ w_ukv).reshape(b, n, MLA_HEADS, MLA_NOPE + MLA_V)
        return q[..., :MLA_NOPE], q[..., MLA_NOPE:], kv[..., :MLA_NOPE], kr, kv[..., MLA_NOPE:]

    qn_l, qr_l, kn_l, kr_l, v_l = project(a_lat)
    qn_c, qr_c, kn_c, kr_c, v_c = project(a_ctx)
    qr_l = apply_rope(qr_l, cos[:, None, :], sin[:, None, :])
    kr_l = apply_rope(kr_l, cos, sin)
    kn_all = jnp.concatenate([kn_c, kn_l], axis=1)
    kr_all = jnp.concatenate([kr_c, kr_l], axis=1)
    v_all = jnp.concatenate([v_c, v_l], axis=1)
    y_lat = sweep_query_blocks(
        lambda qn_b, qr_b: _mla_attend(qn_b, qr_b, kn_all, kr_all, v_all), (qn_l, qr_l)) @ w_out
    y_ctx = _mla_attend(qn_c, qr_c, kn_c, kr_c, v_c) @ w_out if ctx_out else None
    return y_lat, y_ctx


def short_conv(a, w_in, conv_w, w_out):
    d = a.shape[-1]
    b_gate, c_gate, h = jnp.split(a @ w_in, 3, axis=-1)
    u = lax.conv_general_dilated(c_gate * h, conv_w[:, None, :].astype(a.dtype),
                                 window_strides=(1,), padding=((SC_WIDTH // 2, SC_WIDTH // 2),),
                                 dimension_numbers=('NWC', 'WIO', 'NWC'), feature_group_count=d)
    return (b_gate * u) @ w_out


def squared_relu_mlp(a, w_in, w_out):
    return jnp.square(jax.nn.relu(a @ w_in)) @ w_out


def setup_inputs(seed: int = 0) -> dict:
    key = jax.random.key(seed)
    ks = iter(jax.random.split(key, 32))
    D = D_MODEL

    def nrm(shape, scale=1.0):
        return jax.random.normal(next(ks), shape, jnp.float32) * scale

    def w(shape, fan_in, scale=1.0):
        return nrm(shape, scale * fan_in ** -0.5)

    def gain(shape):
        return 1.0 + nrm(shape, 0.02)

    return {
        "x": nrm((BATCH, SEQ, D)),
        "c": nrm((BATCH, D)),
        "ctx": nrm((BATCH, CTX_LEN, D)),
        "c_ctx": nrm((D,)),
        "w_ada": w((DEPTH, D, N_MOD * D), D, 0.5),
        "b_ada": nrm((DEPTH, N_MOD * D), 0.01),
        "norm_g": gain((DEPTH, 4, D)),
        "w_mlp_in": w((DEPTH, D, MLP_HIDDEN), D),
        "w_mlp_out": w((DEPTH, MLP_HIDDEN, D), MLP_HIDDEN),
        "w_da_qkv": w((N_LAYERS_A, D, 3 * 2 * DA_HEADS * DA_HEAD_DIM), D),
        "da_lambda": nrm((N_LAYERS_A, 4, DA_HEAD_DIM), 0.1),
        "da_subln": gain((N_LAYERS_A, 2 * DA_HEAD_DIM)),
        "w_da_out": w((N_LAYERS_A, 2 * DA_HEADS * DA_HEAD_DIM, D), 2 * DA_HEADS * DA_HEAD_DIM),
        "w_mla_down": w((N_LAYERS_B, D, MLA_Q_LORA + MLA_KV_LORA + MLA_ROPE), D),
        "mla_q_norm": gain((N_LAYERS_B, MLA_Q_LORA)),
        "w_mla_uq": w((N_LAYERS_B, MLA_Q_LORA, MLA_HEADS * (MLA_NOPE + MLA_ROPE)), MLA_Q_LORA),
        "mla_kv_norm": gain((N_LAYERS_B, MLA_KV_LORA)),
        "w_mla_ukv": w((N_LAYERS_B, MLA_KV_LORA, MLA_HEADS * (MLA_NOPE + MLA_V)), MLA_KV_LORA),
        "w_mla_out": w((N_LAYERS_B, MLA_HEADS * MLA_V, D), MLA_HEADS * MLA_V),
        "w_sc_in": w((N_LAYERS_C, D, 3 * D), D),
        "sc_conv": w((N_LAYERS_C, SC_WIDTH, D), SC_WIDTH),
        "w_sc_out": w((N_LAYERS_C, D, D), D),
    }


def reference(x, c, ctx, c_ctx, w_ada, b_ada, norm_g, w_mlp_in, w_mlp_out,
              w_da_qkv, da_lambda, da_subln, w_da_out,
              w_mla_down, mla_q_norm, w_mla_uq, mla_kv_norm, w_mla_ukv, w_mla_out,
              w_sc_in, sc_conv, w_sc_out):
    n_lat = x.shape[1]
    rows = n_lat // GRID_W
    row = jnp.repeat(jnp.arange(rows, dtype=jnp.float32), GRID_W)
    col = jnp.tile(jnp.arange(GRID_W, dtype=jnp.float32), rows)
    cos_da, sin_da = axial_rope_tables(row, col, DA_HEAD_DIM)
    cos_mla, sin_mla = axial_rope_tables(row, col, MLA_ROPE)

    silu_c = jax.nn.silu(c)
    silu_cc = jax.nn.silu(c_ctx)
    h_lat, h_ctx = x, ctx
    for i in range(DEPTH):
        kind, j = i % N_MIXERS, i // N_MIXERS
        ctx_out = i < DEPTH - 1
        g = norm_g[i]
        m_lat = jnp.split((silu_c @ w_ada[i] + b_ada[i])[:, None, :], N_MOD, axis=-1)
        m_ctx = jnp.split(silu_cc @ w_ada[i] + b_ada[i], N_MOD, axis=-1)

        a_lat = modulate(rms_norm(h_lat, g[0]), m_lat[0], m_lat[1])
        if kind == 0:
            a_ctx = modulate(rms_norm(h_ctx, g[0]), m_ctx[0], m_ctx[1])
            lambda_init = 0.8 - 0.6 * math.exp(-0.3 * i)
            y_lat, y_ctx = diff_attention(a_lat, a_ctx, w_da_qkv[j], da_lambda[j], da_subln[j],
                                          w_da_out[j], lambda_init, cos_da, sin_da, ctx_out)
        elif kind == 1:
            a_ctx = modulate(rms_norm(h_ctx, g[0]), m_ctx[0], m_ctx[1])
            y_lat, y_ctx = mla_attention(a_lat, a_ctx, w_mla_down[j], mla_q_norm[j], w_mla_uq[j],
                                         mla_kv_norm[j], w_mla_ukv[j], w_mla_out[j],
                                         cos_mla, sin_mla, ctx_out)
        else:
            y_lat = short_conv(a_lat, w_sc_in[j], sc_conv[j], w_sc_out[j])
            if ctx_out:
                a_ctx = modulate(rms_norm(h_ctx, g[0]), m_ctx[0], m_ctx[1])
                y_ctx = short_conv(a_ctx, w_sc_in[j], sc_conv[j], w_sc_out[j])

        h_lat = h_lat + m_lat[2] * rms_norm(y_lat, g[1])
        f_lat = squared_relu_mlp(modulate(rms_norm(h_lat, g[2]), m_lat[3], m_lat[4]), w_mlp_in[i], w_mlp_out[i])
        h_lat = h_lat + m_lat[5] * rms_norm(f_lat, g[3])
        if ctx_out:
            h_ctx = h_ctx + m_ctx[2] * rms_norm(y_ctx, g[1])
            f_ctx = squared_relu_mlp(modulate(rms_norm(h_ctx, g[2]), m_ctx[3], m_ctx[4]), w_mlp_in[i], w_mlp_out[i])
            h_ctx = h_ctx + m_ctx[5] * rms_norm(f_ctx, g[3])
    return h_lat
```

```python
import math
import contextlib
import numpy as np
import concourse.bass as bass
import concourse.mybir as mybir
from concourse.bass_utils import run_bass_kernel_spmd

F32 = mybir.dt.float32
BF16 = mybir.dt.bfloat16
AF = mybir.ActivationFunctionType
ALU = mybir.AluOpType

D = 1024
SEQ = 2048
CTX = 256
T = SEQ + CTX
DEPTH = 4
HID = 4096
NCH = 8
EPS = 1e-6
GRID_W = 64
ROPE_THETA = 10000.0
DA_SCALE = 64 ** -0.5
MLA_SCALE = 96 ** -0.5
ENGS = ("pe", "act", "dve", "pool", "sp")

BLOCKS = [(0, 256, "ctx")] + [(256 + 512 * i, 512, "lat") for i in range(4)]


class Buf:
    __slots__ = ("name", "last_w", "readers", "excl", "sem", "dcount")

    def __init__(self, name, excl=False):
        self.name = name
        self.last_w = None
        self.readers = {}
        self.excl = excl
        self.sem = None
        self.dcount = 0


class Instr:
    __slots__ = ("eng", "fn", "deps", "signal", "val", "is_dma", "dsem", "dval", "small")

    def __init__(self, eng, fn, is_dma=False):
        self.small = False
        self.eng = eng
        self.fn = fn
        self.deps = []
        self.signal = False
        self.val = None
        self.is_dma = is_dma
        self.dsem = None
        self.dval = None


class Prog:
    def __init__(self, nc):
        self.nc = nc
        self.streams = {e: [] for e in ENGS}
        self.bufs = []
        self.pending = {e: [] for e in ENGS}
        self.last = {e: None for e in ENGS}
        self.dmas_since_barrier = []
        self.strict = ()

    def buf(self, name, excl=False):
        b = Buf(name, excl)
        self.bufs.append(b)
        return b

    def _dep(self, ins, other):
        if other is None or other is ins:
            return
        if (not other.is_dma) and (not ins.is_dma) and other.eng == ins.eng:
            if ins.eng == "pe" or not (other.small or (self.strict and ins.eng in self.strict)):
                return
        ins.deps.append(other)
        if not other.is_dma:
            other.signal = True

    def _all_readers(self, b):
        for k, r in b.readers.items():
            if k == "dma":
                for x in r:
                    yield x
            else:
                yield r

    def _track(self, ins, reads, writes):
        for d in self.pending[ins.eng]:
            self._dep(ins, d)
        self.pending[ins.eng] = []
        for b in reads:
            if b.excl:
                self._dep(ins, b.last_w)
                for r in self._all_readers(b):
                    self._dep(ins, r)
                b.readers = {}
                b.last_w = ins
            else:
                self._dep(ins, b.last_w)
                if ins.is_dma:
                    b.readers.setdefault("dma", []).append(ins)
                else:
                    b.readers[ins.eng] = ins
        for b in writes:
            self._dep(ins, b.last_w)
            for r in self._all_readers(b):
                self._dep(ins, r)
            b.readers = {}
            b.last_w = ins

    def op(self, eng, fn, reads=(), writes=(), small=False):
        ins = Instr(eng, fn)
        ins.small = small
        self._track(ins, reads, writes)
        self.streams[eng].append(ins)
        self.last[eng] = ins
        return ins

    def dma(self, queue, fn, reads=(), writes=(), sem_buf=None, in_barrier=True):
        ins = Instr(queue, fn, is_dma=True)
        if sem_buf is None:
            sem_buf = writes[0] if writes else reads[0]
        sem_buf.dcount += 1
        ins.dsem = sem_buf
        ins.dval = 16 * sem_buf.dcount
        self._track(ins, reads, writes)
        self.streams[queue].append(ins)
        if in_barrier:
            self.dmas_since_barrier.append(ins)
        return ins

    def barrier(self):
        deps = [self.last[e] for e in ENGS if self.last[e] is not None]
        deps += self.dmas_since_barrier
        self.dmas_since_barrier = []
        for e in ENGS:
            if e == "pool":
                continue
            self.pending[e] = list(self.pending[e]) + deps

    def emit(self, final_waits=()):
        nc = self.nc
        with contextlib.ExitStack() as es:
            esem = {e: es.enter_context(nc.semaphore("s_" + e)) for e in ENGS}
            for b in self.bufs:
                if b.dcount > 0:
                    b.sem = es.enter_context(nc.semaphore("d_" + b.name))
            for e in ENGS:
                c = 0
                for ins in self.streams[e]:
                    if (not ins.is_dma) and ins.signal:
                        c += 1
                        ins.val = c
            block = es.enter_context(nc.Block())

            def run(ename, e):
                waited = {}
                for ins in self.streams[ename]:
                    for d in ins.deps:
                        if d.is_dma:
                            sem, val = d.dsem.sem, d.dval
                        else:
                            sem, val = esem[d.eng], d.val
                        key = sem.num
                        if waited.get(key, 0) >= val:
                            continue
                        waited[key] = val
                        e.wait_ge(sem, val)
                    r = ins.fn(e)
                    if ins.is_dma:
                        r.then_inc(ins.dsem.sem, 16)
                    elif ins.signal:
                        r.then_inc(esem[ename], 1)
                if ename == "sp":
                    for d in final_waits:
                        e.wait_ge(d.dsem.sem, d.dval)

            @block.tensor
            def _(e):
                run("pe", e)

            @block.scalar
            def _(e):
                run("act", e)

            @block.vector
            def _(e):
                run("dve", e)

            @block.gpsimd
            def _(e):
                run("pool", e)

            @block.sync
            def _(e):
                run("sp", e)


class Builder:
    def __init__(self, nb, n_layers, dbg=""):
        self.dbg = dbg
        self.nb = nb
        self.n_layers = n_layers
        self.nc = bass.Bass("TRN2", target_bir_lowering=False)
        self.P = Prog(self.nc)
        if "strict" in dbg:
            self.P.strict = ("dve", "act", "pool")
        self.es = contextlib.ExitStack()
        self.out_dmas = []
        self.conv_gate = None
        self.rr = {}
        self.scr = {}

    @staticmethod
    def _small(ap):
        n = 1
        for d in ap.shape[1:]:
            n *= int(d)
        return n < 512

    def mm(self, out, lhsT, rhs, start, stop, rd, wr):
        self.P.op("pe", lambda e: e.matmul(out, lhsT=lhsT, rhs=rhs, start=start, stop=stop), reads=rd, writes=wr)

    def tr(self, out, in_, rd, wr):
        ident = self.ident
        self.P.op("pe", lambda e: e.transpose(out, in_, ident[:]), reads=list(rd) + [self.Bconst], writes=wr)

    def act(self, out, in_, func, rd, wr, scale=1.0, bias=None):
        if bias is None:
            self.P.op("act", lambda e: e.activation(out=out, in_=in_, func=func, scale=scale), reads=rd, writes=wr, small=self._small(out))
        else:
            self.P.op("act", lambda e: e.activation(out=out, in_=in_, func=func, scale=scale, bias=bias), reads=rd, writes=wr, small=self._small(out))

    def tt(self, eng, out, in0, in1, op, rd, wr):
        self.P.op(eng, lambda e: e.tensor_tensor(out=out, in0=in0, in1=in1, op=op), reads=rd, writes=wr, small=self._small(out))

    def stt(self, eng, out, in0, scalar, in1, op0, op1, rd, wr):
        self.P.op(eng, lambda e: e.scalar_tensor_tensor(out=out, in0=in0, scalar=scalar, in1=in1, op0=op0, op1=op1), reads=rd, writes=wr, small=self._small(out))

    def ts(self, eng, out, in0, s1, s2, op0, op1, rd, wr):
        if s2 is None:
            self.P.op(eng, lambda e: e.tensor_scalar(out=out, in0=in0, scalar1=s1, scalar2=None, op0=op0), reads=rd, writes=wr, small=self._small(out))
        else:
            self.P.op(eng, lambda e: e.tensor_scalar(out=out, in0=in0, scalar1=s1, scalar2=s2, op0=op0, op1=op1), reads=rd, writes=wr, small=self._small(out))

    def cp(self, eng, out, in_, rd, wr):
        self.P.op(eng, lambda e: e.tensor_copy(out=out, in_=in_), reads=rd, writes=wr, small=self._small(out))

    def memset(self, eng, ap, val, wr):
        self.P.op(eng, lambda e: e.memset(ap, val), writes=wr, small=True)

    def recip(self, out, in_, rd, wr):
        self.P.op("dve", lambda e: e.reciprocal(out=out, in_=in_), reads=rd, writes=wr, small=self._small(out))

    def dma(self, queue, out, in_, rd, wr, sem_buf=None, in_barrier=True):
        return self.P.dma(queue, lambda e: e.dma_start(out=out, in_=in_), reads=rd, writes=wr, sem_buf=sem_buf,
                          in_barrier=in_barrier)

    def dump(self, tag, ap, bufs, dt=None):
        if "dump" not in self.dbg:
            return
        if not hasattr(self, "dumps"):
            self.dumps = {}
        if tag in self.dumps:
            return
        dt = dt or ap.dtype
        d = self.nc.dram_tensor("dbg_" + tag, list(ap.shape), dt, kind="ExternalOutput").ap()
        self.dumps[tag] = d
        x = self.dma("sp", d, ap, list(bufs), [], sem_buf=bufs[0])
        self.out_dmas.append(x)

    def rot(self, key, n):
        v = self.rr.get(key, 0)
        self.rr[key] = v + 1
        return v % n

    def sb(self, name, shape, dt):
        return self.es.enter_context(self.nc.sbuf_tensor("sb_" + name, shape, dt))

    def dram_in(self, name, shape):
        return self.nc.dram_tensor(name, list(shape), F32, kind="ExternalInput").ap()

    def build(self):
        nc, P = self.nc, self.P
        nb = self.nb
        I = {}
        for name, shape in input_shapes(nb).items():
            I[name] = self.dram_in(name, shape)
        self.I = I
        self.y = nc.dram_tensor("y", [nb, SEQ, D], F32, kind="ExternalOutput").ap()

        self.h = self.sb("h", [128, NCH, T], F32)
        self.Bh = [[P.buf(f"h{c}_{k}") for k in range(len(BLOCKS))] for c in range(NCH)]
        self.arA = self.sb("arA", [128, NCH * T // 2], F32)
        self.arA_bf = self.arA.bitcast(BF16) if hasattr(self.arA, "bitcast") else None
        self.arB = self.sb("arB", [128, NCH * T // 2], F32)
        self.NSLOT = 3
        self.ring = [self.sb(f"ring{i}", [128, 4096], BF16) for i in range(self.NSLOT)]
        self.Bring = [P.buf(f"ring{i}") for i in range(self.NSLOT)]
        self.hk = self.sb("hk", [128, T], BF16)
        self.hv = self.sb("hv", [128, 18 * 128], BF16)
        self.hqs = [self.sb(f"hq{i}", [128, 512], BF16) for i in range(4)]
        self.Bhk, self.Bhv = P.buf("hk"), P.buf("hv")
        self.Bhqs = [P.buf(f"hq{i}") for i in range(4)]
        self.NROPE = 1
        self.rope = [self.sb(f"rope{i}", [128, 2, 512], F32) for i in range(self.NROPE)]
        self.Brope = [P.buf(f"rope{i}") for i in range(self.NROPE)]
        self.NBT = 4
        self.bt2 = [self.sb(f"bt2_{i}", [128, 1024], BF16) for i in range(2)]
        self.bt = [self.bt2[i // 2][:, (i % 2) * 512:(i % 2 + 1) * 512] for i in range(4)]
        self.Bbt = [P.buf(f"bt{i}") for i in range(self.NBT)]
        self.NFT = 3
        self.ft = [self.sb(f"ft{i}", [128, 512], F32) for i in range(self.NFT)]
        self.Bft = [P.buf(f"ft{i}") for i in range(self.NFT)]
        self.rs = self.sb("rs", [128, 512], F32)
        self.Brs = P.buf("rs")
        self.comb = self.sb("comb", [128, 512], F32)
        self.Bcomb = P.buf("comb")
        self.sqd = self.sb("sqd", [128, 512], BF16)
        self.Bsqd = P.buf("sqd")
        self.ident = self.sb("ident", [128, 128], F32)
        self.ones = {k: self.sb(f"ones{k}", [128, 128], BF16) for k in (1024, 256, 128, 1)}
        self.epsc = self.sb("epsc", [128, 1], F32)
        self.onesf = self.sb("onesf", [128, 64], F32)
        self.scT = self.sb("scT", [128, NCH, 3], BF16)
        self.cv = self.sb("cv", [128, NCH, 3], F32)
        self.mvec = self.sb("mvec", [128, DEPTH, 48, 3], F32)
        self.badaT = self.sb("badaT", [128, DEPTH, 48], F32)
        self.ngT = self.sb("ngT", [128, DEPTH, 4, NCH], F32)
        self.der = self.sb("der", [128, DEPTH, 4, NCH, 3], F32)
        self.small = self.sb("small", [128, 64], F32)
        self.Bconst = P.buf("consts")
        self.Bmvec_l = [P.buf(f"mvec{l}") for l in range(DEPTH)]
        self.Bder_l = [P.buf(f"der{l}") for l in range(DEPTH)]
        self.Bsmall = P.buf("small")
        self.Bgate = P.buf("gate")
        self.psall = self.es.enter_context(nc.psum_tensor("psall", [128, 8 * 512], F32))
        self.ps = [self.psall[:, i * 512:(i + 1) * 512] for i in range(8)]
        self.Bps = [P.buf(f"ps{i}", excl=True) for i in range(8)]

        self.prologue()
        for b in range(nb):
            self.load_inputs(b)
            for l in range(self.n_layers):
                self.layer(b, l)
            self.store_output(b)
        P.emit(final_waits=self.out_dmas)
        return nc

    def wload(self, src_ap, view):
        s = self.rot("ring", self.NSLOT)
        dst = view(self.ring[s])
        self.dma("pool", dst, src_ap, rd=[], wr=[self.Bring[s]], in_barrier=False)
        return self.ring[s], self.Bring[s]

    def scratch_tensor(self, key, n_slots, width):
        t = self.nc.dram_tensor("ws_" + key, [n_slots, 128, width], BF16).ap()
        self.scr[key] = (t, self.P.buf("ws_" + key))
        return t, self.scr[key][1]

    def conv(self, dst, src, Bdst):
        gate = []
        if self.conv_gate is not None:
            gate, self.conv_gate = [self.conv_gate], None
        self.dma("pool", dst, src, gate, [], sem_buf=Bdst, in_barrier=False)
        Bdst.last_w = self.P.streams["pool"][-1]

    def sload(self, key, idx, width=None):
        t, Bt = self.scr[key]
        s = self.rot("ring", self.NSLOT)
        w = t.shape[2] if width is None else width
        self.dma("sp", self.ring[s][:, 0:w], t[idx, :, 0:w], [Bt], [self.Bring[s]], in_barrier=False)
        return self.ring[s], self.Bring[s]

    def prep_weights(self, l):
        I = self.I
        kind, j = l % 3, l // 3
        if "force_kind=" in self.dbg:
            kind, j = int(self.dbg.split("force_kind=")[1][0]), 0
        def halves(key, w_ap):
            t, Bt = self.scratch_tensor(key, 2, 4096)
            wv = w_ap.rearrange("(k p) n -> p k n", p=128)
            for half in range(2):
                self.conv(t[half].rearrange("p (k n) -> p k n", k=8), wv[:, :, half * 512:(half + 1) * 512], Bt)
        if kind == 0:
            t, Bt = self.scratch_tensor(f"daqkv{j}", 8, 3072)
            wq = I["w_da_qkv"][j].rearrange("(k p) n -> p k n", p=128)
            for hd in range(8):
                tv = t[hd].rearrange("p (t k n) -> p t k n", t=3, k=8)
                for t3 in range(3):
                    self.conv(tv[:, t3], wq[:, :, t3 * 1024 + hd * 128: t3 * 1024 + (hd + 1) * 128], Bt)
            halves(f"daout{j}", I["w_da_out"][j])
        elif kind == 1:
            t, Bt = self.scratch_tensor("mladown", 1, 4096)
            wd = I["w_mla_down"][j].rearrange("(k p) n -> p k n", p=128)
            tv = t[0].rearrange("p (k n) -> p k n", k=8)
            self.conv(tv[:, :, 0:416], wd, Bt)
            self.conv(tv[:, :, 480:496], wd[:, :, 400:416], Bt)
            self.conv(tv[:, :, 496:512], wd[:, :, 384:400], Bt)
            t, Bt = self.scratch_tensor("mlahead", 16, 512)
            wuq = I["w_mla_uq"][j].rearrange("(k p) n -> p k n", p=128)
            wukv = I["w_mla_ukv"][j]
            for hd in range(16):
                wqn = t[hd][:, 0:192].rearrange("p (k n) -> p k n", k=2)
                wqs = t[hd][:, 192:384].rearrange("p (k n) -> p k n", k=2)
                self.conv(wqn, wuq[:, :, hd * 96:(hd + 1) * 96], Bt)
                self.conv(wqs[:, :, 64:80], wuq[:, :, hd * 96 + 80:hd * 96 + 96], Bt)
                self.conv(wqs[:, :, 80:96], wuq[:, :, hd * 96 + 64:hd * 96 + 80], Bt)
                self.conv(t[hd][:, 384:512], wukv[:, hd * 128:(hd + 1) * 128], Bt)
            halves("mlaout", I["w_mla_out"][j])
        else:
            t, Bt = self.scratch_tensor("scin", 8, 3072)
            wi = I["w_sc_in"][j].rearrange("(k p) n -> p k n", p=128)
            for c in range(8):
                tv = t[c].rearrange("p (t k n) -> p t k n", t=3, k=8)
                for t3 in range(3):
                    self.conv(tv[:, t3], wi[:, :, t3 * 1024 + c * 128: t3 * 1024 + (c + 1) * 128], Bt)
            halves("scout", I["w_sc_out"][j])
        if l + 1 < self.n_layers:
            t, Bt = self.scratch_tensor(f"ada{l + 1}", 12, 4096)
            wa = I["w_ada"][l + 1].rearrange("(k p) n -> p k n", p=128)
            for g in range(12):
                self.conv(t[g].rearrange("p (k n) -> p k n", k=8), wa[:, :, g * 512:(g + 1) * 512], Bt)
        t, Bt = self.scratch_tensor(f"win{l}", 8, 4096)
        w_in = I["w_mlp_in"][l].rearrange("(k p) n -> p k n", p=128)
        for jg in range(8):
            self.conv(t[jg].rearrange("p (k n) -> p k n", k=8), w_in[:, :, jg * 512:(jg + 1) * 512], Bt)
        t, Bt = self.scratch_tensor(f"wout{l}", 8, 4096)
        w_out = I["w_mlp_out"][l].rearrange("(j p) n -> p j n", p=128)
        for jg in range(8):
            self.conv(t[jg].rearrange("p (j n) -> p j n", j=4), w_out[:, jg * 4:(jg + 1) * 4, :], Bt)

    def ftmp(self):
        i = self.rot("ft", self.NFT)
        return self.ft[i], self.Bft[i]

    def btmp(self):
        i = self.rot("bt", self.NBT)
        return self.bt[i], self.Bbt[i]

    def bank(self, key, banks):
        i = banks[self.rot(key, len(banks))]
        return self.ps[i], self.Bps[i]

    def prologue(self):
        P, I = self.P, self.I
        Bc = self.Bconst
        self.dma("sp", self.ident[:], I["ident"], [], [Bc])
        self.dma("sp", self.cv[:], I["cvecT"].rearrange("(k p) r -> p k r", p=128), [], [Bc])
        self.dma("sp", self.badaT[:], I["b_adaT"].rearrange("l p c -> p l c"), [], [Bc])
        self.dma("sp", self.ngT[:], I["norm_gT"].rearrange("p (l s c) -> p l s c", l=DEPTH, s=4), [], [Bc])
        self.dma("sp", self.small[:, 0:2], I["da_sublnT"], [], [Bc])
        self.dma("sp", self.small[:, 2:4], I["mla_q_normT"], [], [Bc])
        self.dma("sp", self.small[:, 4:5], I["mla_kv_normT"], [], [Bc])
        self.dma("sp", self.small[:, 8:32], I["sc_convT"], [], [Bc])
        lam_src = bass.AP(I["da_lambda"].tensor, 0, [[0, 128], [1, 512]])
        self.dma("sp", self.rope[0][:, 0, :], lam_src, [], [self.Brope[0]])
        for k, t in self.ones.items():
            self.memset("pool", t[:], 1.0 / k, [Bc])
        self.memset("pool", self.epsc[:], EPS, [Bc])
        self.memset("pool", self.onesf[:], 1.0, [Bc])
        self.act(self.scT[:], self.cv[:], AF.Silu, [Bc], [Bc])
        self.adaln(0)
        self.prep_weights(0)
        for j in range(2):
            li = 3 * j
            if li >= self.n_layers:
                break
            lam_init = 0.8 - 0.6 * math.exp(-0.3 * li)
            lv = self.rope[0][:, 0, j * 256:(j + 1) * 256].rearrange("p (a d) -> p a d", a=4)
            t, Bt = self.ftmp()
            self.tt("dve", t[:, 0:64], lv[:, 0], lv[:, 1], ALU.mult, [self.Brope[0]], [Bt])
            self.tt("dve", t[:, 64:128], lv[:, 2], lv[:, 3], ALU.mult, [self.Brope[0]], [Bt])
            self.P.op("dve", lambda e, t=t: e.reduce_sum(out=t[:, 128:130], in_=t[:, 0:128].rearrange("p (a d) -> p a d", a=2),
                                                       axis=mybir.AxisListType.X), reads=[Bt], writes=[Bt], small=True)
            self.act(t[:, 130:132], t[:, 128:130], AF.Exp, [Bt], [Bt])
            self.stt("dve", self.small[:, 32 + j:33 + j], t[:, 131:132], -lam_init, t[:, 130:131], ALU.add, ALU.subtract,
                     [Bt], [self.Bsmall])
            self.ts("dve", self.small[:, 34 + j:35 + j], self.small[:, j:j + 1], 1.0 - lam_init, None, ALU.mult, None,
                    [Bc], [self.Bsmall])

    def adaln(self, l):
        I, Bc = self.I, self.Bconst
        if True:
            pm, Bpm = self.bank("gen", [6, 7])
            for g in range(12):
                if l == 0:
                    src = I["w_ada"][l].rearrange("(k p) n -> p k n", p=128)[:, :, g * 512:(g + 1) * 512]
                    slot, Bs = self.wload(src, lambda t: t[:].rearrange("p (k n) -> p k n", k=8))
                else:
                    slot, Bs = self.sload(f"ada{l}", g)
                sv = slot[:].rearrange("p (k n) -> p k n", k=8)
                for cc in range(4):
                    ch = g * 4 + cc
                    for kc in range(8):
                        self.mm(pm[:, ch * 3:ch * 3 + 3], sv[:, kc, cc * 128:(cc + 1) * 128], self.scT[:, kc, :],
                                kc == 0, kc == 7, [Bs, Bc], [Bpm])
            self.tt("dve", self.mvec[:, l], pm[:, 0:144].rearrange("p (c r) -> p c r", r=3),
                    self.badaT[:, l].unsqueeze(2).to_broadcast([128, 48, 3]), ALU.add, [Bpm, Bc], [self.Bmvec_l[l]])
            mv = self.mvec[:, l].rearrange("p (n c) r -> p n c r", n=6)
            for idx, (mi, gi, plus1) in enumerate([(1, 0, True), (2, 1, False), (4, 2, True), (5, 3, False)]):
                gb = self.ngT[:, l, gi].unsqueeze(2).to_broadcast([128, NCH, 3])
                dst = self.der[:, l, idx]
                if plus1:
                    self.stt("dve", dst, mv[:, mi], 1.0, gb, ALU.add, ALU.mult, [self.Bmvec_l[l], Bc], [self.Bder_l[l]])
                else:
                    self.tt("dve", dst, mv[:, mi], gb, ALU.mult, [self.Bmvec_l[l], Bc], [self.Bder_l[l]])

    def load_inputs(self, b):
        P, I = self.P, self.I
        P.barrier()
        stage = [self.arA[:, 0:1024], self.arA[:, 1024:2048]]
        Bst = getattr(self, "Bstage_in", None)
        if Bst is None:
            Bst = self.Bstage_in = [P.buf("stin0"), P.buf("stin1")]
        for tt_ in range(T // 128):
            s = tt_ % 2
            if tt_ < 2:
                src = I["ctx"][b, tt_ * 128:(tt_ + 1) * 128, :]
            else:
                src = I["x"][b, (tt_ - 2) * 128:(tt_ - 1) * 128, :]
            self.dma("sp", stage[s], src, [], [Bst[s]])
            blk = self.tok2blk(tt_ * 128)
            for half in range(2):
                pb, Bpb = self.bank("gen", [6, 7])
                for q in range(4):
                    c = half * 4 + q
                    self.tr(pb[:, q * 128:(q + 1) * 128], stage[s][:, c * 128:(c + 1) * 128], [Bst[s]], [Bpb])
                for q in range(4):
                    c = half * 4 + q
                    eng = "act" if q % 2 == 0 else "dve"
                    if eng == "act":
                        self.act(self.h[:, c, tt_ * 128:(tt_ + 1) * 128], pb[:, q * 128:(q + 1) * 128], AF.Identity,
                                 [Bpb], [self.Bh[c][blk]])
                    else:
                        self.cp("dve", self.h[:, c, tt_ * 128:(tt_ + 1) * 128], pb[:, q * 128:(q + 1) * 128],
                                [Bpb], [self.Bh[c][blk]])
        P.barrier()

    def tok2blk(self, tok):
        for i, (s, n, _) in enumerate(BLOCKS):
            if s <= tok < s + n:
                return i
        raise ValueError

    def store_output(self, b):
        P = self.P
        P.barrier()
        stage = [self.arA[:, 0:1024], self.arA[:, 1024:2048]]
        Bst = getattr(self, "Bstage_out", None)
        if Bst is None:
            Bst = self.Bstage_out = [P.buf("stout0"), P.buf("stout1")]
        for tt_ in range(2, T // 128):
            s = tt_ % 2
            blk = self.tok2blk(tt_ * 128)
            for half in range(2):
                pb, Bpb = self.bank("gen", [6, 7])
                for q in range(4):
                    c = half * 4 + q
                    self.tr(pb[:, q * 128:(q + 1) * 128], self.h[:, c, tt_ * 128:(tt_ + 1) * 128], [self.Bh[c][blk]], [Bpb])
                if half == 0:
                    self.act(stage[s][:, 0:512], pb[:], AF.Identity, [Bpb], [Bst[s]])
                else:
                    self.cp("dve", stage[s][:, 512:1024], pb[:], [Bpb], [Bst[s]])
            d = self.dma("sp", self.y[b, (tt_ - 2) * 128:(tt_ - 1) * 128, :], stage[s], [Bst[s]], [], sem_buf=Bst[s])
            self.out_dmas.append(d)
        P.barrier()

    def mrow(self, b, kind):
        return 2 if kind == "ctx" else b

    def rstd_from_psum(self, pm, Bpm, n):
        r, Br = self.rs, self.Brs
        self.act(r[:, 0:n], pm[:, 0:n], AF.Ln, [Bpm, self.Bconst], [Br], bias=self.epsc[:, 0:1])
        self.act(r[:, 0:n], r[:, 0:n], AF.Exp, [Br], [Br], scale=-0.5)
        return r, Br

    def prenorm(self, b, l, site, blk, dst_fn, Bdst):
        s0, n, kind = BLOCKS[blk]
        r_ = self.mrow(b, kind)
        pm, Bpm = self.bank("gen", [6, 7])
        for c in range(NCH):
            sq, Bsq = self.btmp()
            self.act(sq[:, 0:n], self.h[:, c, s0:s0 + n], AF.Square, [self.Bh[c][blk]], [Bsq])
            self.mm(pm[:, 0:n], self.ones[1024][:], sq[:, 0:n], c == 0, c == NCH - 1, [Bsq, self.Bconst], [Bpm])
        rstd, Br = self.rstd_from_psum(pm, Bpm, n)
        aidx = 0 if site == 0 else 2
        midx = 0 if site == 0 else 3
        for c in range(NCH):
            t, Bt = self.ftmp()
            self.stt("dve", t[:, 0:n], self.h[:, c, s0:s0 + n], self.der[:, l, aidx, c, r_:r_ + 1], rstd[:, 0:n],
                     ALU.mult, ALU.mult, [self.Bh[c][blk], self.Bder_l[l], Br], [Bt])
            self.act(dst_fn(c)[:, 0:n], t[:, 0:n], AF.Identity, [Bt, self.Bmvec_l[l]], [Bdst],
                     bias=self.mvec[:, l, midx * 8 + c, r_:r_ + 1])

    def postnorm_residual(self, b, l, site, blk, ybuf, By):
        s0, n, kind = BLOCKS[blk]
        r_ = self.mrow(b, kind)
        pm, Bpm = self.bank("gen", [6, 7])
        for c in range(NCH):
            sq, Bsq = self.btmp()
            self.act(sq[:, 0:n], ybuf[:, c, 0:n], AF.Square, [By], [Bsq])
            self.mm(pm[:, 0:n], self.ones[1024][:], sq[:, 0:n], c == 0, c == NCH - 1, [Bsq, self.Bconst], [Bpm])
        rstd, Br = self.rstd_from_psum(pm, Bpm, n)
        gidx = 1 if site == 0 else 3
        for c in range(NCH):
            t, Bt = self.ftmp()
            self.stt("dve", t[:, 0:n], ybuf[:, c, 0:n], self.der[:, l, gidx, c, r_:r_ + 1], rstd[:, 0:n],
                     ALU.mult, ALU.mult, [By, self.Bder_l[l], Br], [Bt])
            self.tt("dve", self.h[:, c, s0:s0 + n], self.h[:, c, s0:s0 + n], t[:, 0:n], ALU.add,
                    [Bt, self.Bh[c][blk]], [self.Bh[c][blk]])

    def layer(self, b, l):
        kind, j = l % 3, l // 3
        ctx_out = l < DEPTH - 1
        if b == 0 and l + 1 < self.n_layers:
            self.memset("dve", self.small[:, 63:64], 0.0, [self.Bgate])
        if "force_kind=" in self.dbg:
            kind, j = int(self.dbg.split("force_kind=")[1][0]), 0
        if kind == 0:
            self.mixer_da(b, l, j, ctx_out)
        elif kind == 1:
            self.mixer_mla(b, l, j, ctx_out)
        else:
            self.mixer_sc(b, l, j, ctx_out)
        if "nomlp" in self.dbg and l == self.n_layers - 1:
            return
        if b == 0 and l + 1 < self.n_layers:
            self.adaln(l + 1)
            self.conv_gate = self.Bgate
            self.prep_weights(l + 1)
        self.mlp(b, l, ctx_out)

    def a_view(self):
        return self.arA_bf[:].rearrange("p (c t) -> p c t", c=NCH)

    def compute_a_all(self, b, l):
        P = self.P
        if not hasattr(self, "Ba"):
            self.Ba = [P.buf(f"a_{k}") for k in range(len(BLOCKS))]
        av = self.a_view()
        for blk, (s0, n, kind) in enumerate(BLOCKS):
            self.prenorm(b, l, 0, blk, lambda c, s0=s0, n=n: av[:, c, s0:s0 + n], self.Ba[blk])
        return av

    def out_proj_residual(self, b, l, wkey, Oall, BO, ctx_out, kchunks=NCH):
        P = self.P
        P.barrier()
        if not hasattr(self, "Bystage"):
            self.Bystage = [P.buf("ystage0"), P.buf("ystage1")]
        ybufs = [self.arA[:, i * NCH * 512:(i + 1) * NCH * 512].rearrange("p (c n) -> p c n", c=NCH) for i in range(2)]
        blks = [k for k in range(len(BLOCKS)) if not (BLOCKS[k][2] == "ctx" and not ctx_out)]

        def proj(bi):
            s0, n, kind = BLOCKS[blks[bi]]
            ybuf, By = ybufs[bi % 2], self.Bystage[bi % 2]
            for half in range(2):
                slot, Bs = self.sload(wkey, half)
                sv = slot[:, 0:kchunks * 512].rearrange("p (k n) -> p k n", k=kchunks)
                for q in range(4):
                    fo = half * 4 + q
                    pb, Bpb = self.bank("proj", [0, 1, 2, 3])
                    for k in range(kchunks):
                        self.mm(pb[:, 0:n], sv[:, k, q * 128:(q + 1) * 128], Oall[:, k, s0:s0 + n],
                                k == 0, k == kchunks - 1, [Bs, BO], [Bpb])
                    if q % 2 == 0:
                        self.act(ybuf[:, fo, 0:n], pb[:, 0:n], AF.Identity, [Bpb], [By])
                    else:
                        self.cp("dve", ybuf[:, fo, 0:n], pb[:, 0:n], [Bpb], [By])

        proj(0)
        for bi, blk in enumerate(blks):
            if bi + 1 < len(blks):
                proj(bi + 1)
            self.postnorm_residual(b, l, 0, blk, ybufs[bi % 2], self.Bystage[bi % 2])
        P.barrier()

    def mixer_da(self, b, l, j, ctx_out):
        P, I = self.P, self.I
        P.barrier()
        av = self.compute_a_all(b, l)
        Ba_all = self.Ba
        self.dump("a", av, Ba_all)
        self.dump("mvec", self.mvec[:, l], [self.Bmvec_l[l]])
        self.dump("der", self.der[:, l], [self.Bder_l[l]])
        self.dump("small", self.small[:], [self.Bsmall])
        Oall = self.arB.bitcast(BF16)[:].rearrange("p (c t) -> p c t", c=NCH)
        if not hasattr(self, "BO"):
            self.BO = P.buf("Oall")
        wq = I["w_da_qkv"][j].rearrange("(k p) n -> p k n", p=128)
        nlam = self.small[:, 32 + j:33 + j]
        sg = self.small[:, 34 + j:35 + j]
        self.da_pending_tail = None
        for i4 in range(4):
            m_ = i4 % 2
            self.memset("dve", self.hqs[i4][(1 - m_) * 64:(2 - m_) * 64, :], 0.0, [self.Bhqs[i4]])
        for hd in range(8):
            slot, Bs = self.sload(f"daqkv{j}", hd)
            sv = slot[:, 0:3072].rearrange("p (t k n) -> p t k n", t=3, k=8)
            for blk, (s0, n, kind) in enumerate(BLOCKS):
                self.proj_rope(sv[:, 1], Bs, av, Ba_all[blk], s0, n, kind, self.hk[:, s0:s0 + n], self.Bhk, "rope_da", [4, 5, 6, 7])
            for tt_ in range(T // 128):
                blk = self.tok2blk(tt_ * 128)
                if tt_ % 4 == 0:
                    pb, Bpb = self.bank("pr4567", [4, 5, 6, 7])
                q4 = tt_ % 4
                for kc in range(8):
                    self.mm(pb[:, q4 * 128:(q4 + 1) * 128], av[:, kc, tt_ * 128:(tt_ + 1) * 128], sv[:, 2, kc, :],
                            kc == 0, kc == 7, [Bs, Ba_all[blk]], [Bpb])
                if q4 == 3 or tt_ == T // 128 - 1:
                    t0 = tt_ - q4
                    self.act(self.hv[:, t0 * 128:(tt_ + 1) * 128], pb[:, 0:(q4 + 1) * 128], AF.Identity, [Bpb], [self.Bhv])
            self.dump("hk", self.hk[:], [self.Bhk])
            self.dump("hv", self.hv[:], [self.Bhv])
            qblocks = [k for k in range(len(BLOCKS)) if not (BLOCKS[k][2] == "ctx" and not ctx_out)]

            def qproj(blk):
                s0, n, kind = BLOCKS[blk]
                qi = self.rot("hq", 2)
                hq2 = [self.hqs[2 * qi], self.hqs[2 * qi + 1]]
                Bhq2 = [self.Bhqs[2 * qi], self.Bhqs[2 * qi + 1]]
                self.proj_rope(sv[:, 0], Bs, av, Ba_all[blk], s0, n, kind, None, None, "rope_da", [7],
                               outs=[(0, 64, hq2[0][0:64, 0:n], Bhq2[0]), (64, 128, hq2[1][64:128, 0:n], Bhq2[1])])
                return hq2, Bhq2

            nxt = qproj(qblocks[0])
            for bi, blk in enumerate(qblocks):
                s0, n, kind = BLOCKS[blk]
                hq2, Bhq2 = nxt
                if bi + 1 < len(qblocks):
                    nxt = qproj(qblocks[bi + 1])
                kts = [0, 1] if kind == "ctx" else list(range(18))
                steps = [(kt, m) for kt in kts for m in range(2)]
                Sb = [4, 5, 6]
                pend = []

                def issue_S(step):
                    kt, m = step
                    pS, BpS = self.bank("S3", Sb)
                    self.mm(pS[:, 0:n], self.hk[:, kt * 128:(kt + 1) * 128],
                            hq2[m][:, 0:n], True, True, [self.Bhk, Bhq2[m]], [BpS])
                    E, BE = self.btmp()
                    self.act(E[:, 0:n], pS[:, 0:n], AF.Exp, [BpS], [BE], scale=DA_SCALE)
                    return (kt, m, E, BE)

                def issue_AV(item, first, last):
                    kt, m, E, BE = item
                    self.mm(self.ps[m][:, 0:n], self.hv[:, kt * 128:(kt + 1) * 128], E[:, 0:n], first, last,
                            [self.Bhv, BE], [self.Bps[m]])
                    self.mm(self.ps[2 + m][:, 0:n], self.ones[1][:], E[:, 0:n], first, last,
                            [self.Bconst, BE], [self.Bps[2 + m]])

                LAG = 2
                for i, st in enumerate(steps):
                    pend.append(issue_S(st))
                    if i >= LAG:
                        it = pend[i - LAG]
                        issue_AV(it, it[0] == kts[0], it[0] == kts[-1])
                    if i == min(10, len(steps) - 1) and self.da_pending_tail is not None:
                        self.da_pending_tail()
                        self.da_pending_tail = None
                for i in range(max(0, len(steps) - LAG), len(steps)):
                    it = pend[i]
                    issue_AV(it, it[0] == kts[0], it[0] == kts[-1])
                r0, Br0 = self.ftmp()
                r1, Br1 = self.ftmp()
                self.act(r0[:, 0:n], self.ps[2][:, 0:n], AF.Ln, [self.Bps[2]], [Br0])
                self.act(r1[:, 0:n], self.ps[3][:, 0:n], AF.Ln, [self.Bps[3]], [Br1])
                self.act(r0[:, 0:n], r0[:, 0:n], AF.Exp, [Br0], [Br0], scale=-1.0)
                self.act(r1[:, 0:n], r1[:, 0:n], AF.Exp, [Br1], [Br1], scale=-1.0)
                self.tt("dve", r0[:, 0:n], self.ps[0][:, 0:n], r0[:, 0:n], ALU.mult, [self.Bps[0], Br0], [Br0])
                self.tt("dve", r1[:, 0:n], self.ps[1][:, 0:n], r1[:, 0:n], ALU.mult, [self.Bps[1], Br1], [Br1])
                self.stt("dve", self.comb[:, 0:n], r1[:, 0:n], nlam, r0[:, 0:n], ALU.mult, ALU.add,
                         [Br0, Br1, self.Bsmall], [self.Bcomb])
                self.act(self.sqd[:, 0:n], self.comb[:, 0:n], AF.Square, [self.Bcomb], [self.Bsqd])

                def tail_b(hd=hd, s0=s0, n=n):
                    pm, Bpm = self.bank("S3", [4, 5, 6])
                    self.mm(pm[:, 0:n], self.ones[128][:], self.sqd[:, 0:n], True, True, [self.Bsqd, self.Bconst], [Bpm])
                    rstd, Brs = self.rstd_from_psum(pm, Bpm, n)
                    self.stt("dve", Oall[:, hd, s0:s0 + n], self.comb[:, 0:n], sg, rstd[:, 0:n], ALU.mult, ALU.mult,
                             [self.Bcomb, Brs, self.Bsmall], [self.BO])

                self.da_pending_tail = tail_b
        if self.da_pending_tail is not None:
            self.da_pending_tail()
            self.da_pending_tail = None
        self.dump("Oall", Oall, [self.BO])
        self.out_proj_residual(b, l, f"daout{j}", Oall, self.BO, ctx_out)

    def proj_rope(self, w, Bw, av, Ba, s0, n, kind, dst, Bdst, rope_name, banks, outs=None):
        pb, Bpb = self.bank("pr" + "".join(str(x) for x in banks), banks)
        for kc in range(8):
            self.mm(pb[:, 0:n], w[:, kc, :], av[:, kc, s0:s0 + n], kc == 0, kc == 7, [Bw, Ba], [Bpb])
        if outs is None:
            outs = [(0, 128, dst, Bdst)]
        if kind == "ctx":
            for (r0, r1, d_, Bd_) in outs:
                self.act(d_, pb[r0:r1, 0:n], AF.Identity, [Bpb], [Bd_])
            return
        ri = self.rot("rope", self.NROPE)
        rt, Brt = self.rope[ri], self.Brope[ri]
        l0 = s0 - CTX
        self.dma("sp", rt[:, 0, 0:n], self.I[rope_name][0, :, l0:l0 + n], [], [Brt], in_barrier=False)
        self.dma("sp", rt[:, 1, 0:n], self.I[rope_name][1, :, l0:l0 + n], [], [Brt], in_barrier=False)
        qf, Bqf = self.ftmp()
        self.act(qf[:, 0:n], pb[:, 0:n], AF.Identity, [Bpb], [Bqf])
        sw, Bsw = self.ftmp()
        for g in range(4):
            src = g ^ 1
            self.cp("dve", sw[g * 32:(g + 1) * 32, 0:n], qf[src * 32:(src + 1) * 32, 0:n], [Bqf], [Bsw])
        self.tt("dve", sw[:, 0:n], sw[:, 0:n], rt[:, 1, 0:n], ALU.mult, [Bsw, Brt], [Bsw])
        self.tt("dve", qf[:, 0:n], qf[:, 0:n], rt[:, 0, 0:n], ALU.mult, [Bqf, Brt], [Bqf])
        for (r0, r1, d_, Bd_) in outs:
            self.tt("dve", d_, qf[r0:r1, 0:n], sw[r0:r1, 0:n], ALU.add, [Bqf, Bsw], [Bd_])


    def mixer_mla(self, b, l, j, ctx_out):
        P, I = self.P, self.I
        P.barrier()
        arA_bf = self.arA_bf
        cqn = arA_bf[:, 0:2 * T].rearrange("p (c t) -> p c t", c=2)
        ckvn = arA_bf[:, 2 * T:3 * T]
        krT = arA_bf[:, 3 * T:4 * T]
        ablk = [arA_bf[:, 4 * T + i * 4096:4 * T + (i + 1) * 4096].rearrange("p (c n) -> p c n", c=NCH) for i in range(2)]
        if not hasattr(self, "Bmla"):
            self.Bmla = {k: P.buf("mla_" + k) for k in ("a0", "a1", "cqn", "ckvn", "kr")}
        if not hasattr(self, "BO"):
            self.BO = P.buf("Oall")
        Bm = self.Bmla
        Oall = self.arB.bitcast(BF16)[:].rearrange("p (c t) -> p c t", c=NCH)
        wd = I["w_mla_down"][j].rearrange("(k p) n -> p k n", p=128)
        qg = self.small[:, 2:4]
        kvg = self.small[:, 4:5]
        hvv = self.hv[:, 0:18 * 128].rearrange("p (t e) -> p t e", e=128)
        self.memset("dve", hvv[:, :, 64:128], 1.0, [self.Bhv])
        self.memset("dve", self.hk[96:128, :], 0.0, [self.Bhk])
        for i4 in range(2):
            self.memset("dve", self.hqs[i4][96:128, :], 0.0, [self.Bhqs[i4]])
        PB6 = [0, 1, 2, 3, 4, 5]
        for blk, (s0, n, kind) in enumerate(BLOCKS):
            par = blk % 2
            a2 = ablk[par]
            Ba2 = Bm["a%d" % par]
            self.prenorm(b, l, 0, blk, lambda c, a2=a2: a2[:, c, :], Ba2)
            slot, Bs = self.sload("mladown", 0)
            sv = slot[:].rearrange("p (k n) -> p k n", k=8)
            outs = []
            for (c0, c1) in [(0, 128), (128, 256), (256, 384), (320, 416), (416, 512)]:
                if c0 == 416 and kind == "ctx":
                    outs.append(None)
                    continue
                pb, Bpb = self.bank("proj6", PB6)
                M = c1 - c0
                for kc in range(8):
                    self.mm(pb[0:M, 0:n], sv[:, kc, c0:c1], a2[:, kc, 0:n], kc == 0, kc == 7, [Bs, Ba2], [Bpb])
                outs.append((pb, Bpb))
            (pA, BpA), (pB, BpB), (pC, BpC), (pD, BpD) = outs[0:4]
            pm, Bpm = self.bank("gen", [6, 7])
            for i_, (pp, Bpp) in enumerate([(pA, BpA), (pB, BpB)]):
                sq, Bsq = self.btmp()
                self.act(sq[:, 0:n], pp[:, 0:n], AF.Square, [Bpp], [Bsq])
                self.mm(pm[:, 0:n], self.ones[256][:], sq[:, 0:n], i_ == 0, i_ == 1, [Bsq, self.Bconst], [Bpm])
            rstd, Br = self.rstd_from_psum(pm, Bpm, n)
            for i_, (pp, Bpp) in enumerate([(pA, BpA), (pB, BpB)]):
                self.stt("dve", cqn[:, i_, s0:s0 + n], pp[:, 0:n], qg[:, i_:i_ + 1], rstd[:, 0:n], ALU.mult, ALU.mult,
                         [Bpp, Br, self.Bconst], [Bm["cqn"]])
            pm, Bpm = self.bank("gen", [6, 7])
            sq, Bsq = self.btmp()
            self.act(sq[:, 0:n], pC[:, 0:n], AF.Square, [BpC], [Bsq])
            self.mm(pm[:, 0:n], self.ones[128][:], sq[:, 0:n], True, True, [Bsq, self.Bconst], [Bpm])
            rstd, Br = self.rstd_from_psum(pm, Bpm, n)
            self.stt("dve", ckvn[:, s0:s0 + n], pC[:, 0:n], kvg, rstd[:, 0:n], ALU.mult, ALU.mult,
                     [BpC, Br, self.Bconst], [Bm["ckvn"]])
            if kind == "ctx":
                self.act(krT[64:96, s0:s0 + n], pD[64:96, 0:n], AF.Identity, [BpD], [Bm["kr"]])
            else:
                pE, BpE = outs[4]
                self.rope_rows(pD, BpD, pE, BpE, s0, n, krT[64:96, s0:s0 + n], Bm["kr"])
        wuq = I["w_mla_uq"][j].rearrange("(k p) n -> p k n", p=128)
        wukv = I["w_mla_ukv"][j]
        for hd in range(16):
            slot, Bs = self.sload("mlahead", hd)
            wqn = slot[:, 0:192].rearrange("p (k n) -> p k n", k=2)
            wqs = slot[:, 192:384].rearrange("p (k n) -> p k n", k=2)
            wkv = slot[:, 384:512]
            for blk, (s0, n, kind) in enumerate(BLOCKS):
                pb, Bpb = self.bank("S", [4, 5, 6, 7])
                self.mm(pb[0:64, 0:n], wkv[:, 0:64], ckvn[:, s0:s0 + n], True, True, [Bs, Bm["ckvn"]], [Bpb])
                self.cp("dve", self.hk[0:64, s0:s0 + n], pb[0:64, 0:n], [Bpb], [self.Bhk])
            self.cp("dve", self.hk[64:96, :], krT[64:96, :], [Bm["kr"]], [self.Bhk])
            for t0 in (0, 8, 16):
                k = min(8, 18 - t0)
                pb, Bpb = self.bank("S", [4, 5, 6, 7])
                for q in range(k):
                    tt_ = t0 + q
                    self.mm(pb[:, q * 64:(q + 1) * 64], ckvn[:, tt_ * 128:(tt_ + 1) * 128], wkv[:, 64:128], True, True,
                            [Bs, Bm["ckvn"]], [Bpb])
                self.cp("dve", hvv[:, t0:t0 + k, 0:64], pb[:, 0:k * 64].rearrange("p (t e) -> p t e", e=64),
                        [Bpb], [self.Bhv])
            qblocks = [k for k in range(len(BLOCKS)) if not (BLOCKS[k][2] == "ctx" and not ctx_out)]

            def qproj(blk):
                s0, n, kind = BLOCKS[blk]
                qi = self.rot("hq", 2)
                hq, Bhq = self.hqs[qi], self.Bhqs[qi]
                pA, BpA = self.ps[2], self.Bps[2]
                for kc in range(2):
                    self.mm(pA[0:96, 0:n], wqn[:, kc, :], cqn[:, kc, s0:s0 + n], kc == 0, kc == 1, [Bs, Bm["cqn"]], [BpA])
                self.cp("dve", hq[0:64, 0:n], pA[0:64, 0:n], [BpA], [Bhq])
                if kind == "ctx":
                    self.cp("dve", hq[64:96, 0:n], pA[64:96, 0:n], [BpA], [Bhq])
                else:
                    pB, BpB = self.ps[3], self.Bps[3]
                    for kc in range(2):
                        self.mm(pB[0:96, 0:n], wqs[:, kc, :], cqn[:, kc, s0:s0 + n], kc == 0, kc == 1, [Bs, Bm["cqn"]], [BpB])
                    self.rope_rows(pA, BpA, pB, BpB, s0, n, hq[64:96, 0:n], Bhq)
                return hq, Bhq

            nxt = qproj(qblocks[0])
            for bi, blk in enumerate(qblocks):
                s0, n, kind = BLOCKS[blk]
                hq, Bhq = nxt
                if bi + 1 < len(qblocks):
                    nxt = qproj(qblocks[bi + 1])
                kts = [0, 1] if kind == "ctx" else list(range(18))
                oi = self.rot("mlaO", 2)
                pO, BpO = self.ps[oi], self.Bps[oi]
                pairs = [(kts[2 * i], kts[2 * i + 1]) for i in range(len(kts) // 2)]
                pend = []

                def issue_S2(pr):
                    jb = self.rot("S2", 2)
                    b0 = 4 + 2 * jb
                    for q, kt in enumerate(pr):
                        self.mm(self.ps[b0 + q][:, 0:n], self.hk[:, kt * 128:(kt + 1) * 128], hq[:, 0:n], True, True,
                                [self.Bhk, Bhq], [self.Bps[b0 + q]])
                    je = self.rot("E2", 2)
                    Ep = self.bt2[je]
                    BE = [self.Bbt[2 * je], self.Bbt[2 * je + 1]]
                    src = self.psall[:, b0 * 512:(b0 + 2) * 512].rearrange("p (t c) -> p t c", t=2)[:, :, 0:n]
                    dst = Ep[:].rearrange("p (t c) -> p t c", t=2)[:, :, 0:n]
                    self.act(dst, src, AF.Exp, [self.Bps[b0], self.Bps[b0 + 1]], BE, scale=MLA_SCALE)
                    return (pr, Ep, BE)

                def issue_AV2(item):
                    pr, Ep, BE = item
                    for q, kt in enumerate(pr):
                        self.mm(pO[:, 0:n], hvv[:, kt, :], Ep[:, q * 512:q * 512 + n], kt == kts[0], kt == kts[-1],
                                [self.Bhv] + BE, [BpO])

                LAGP = 1
                for i, pr in enumerate(pairs):
                    pend.append(issue_S2(pr))
                    if i >= LAGP:
                        issue_AV2(pend[i - LAGP])
                for i in range(max(0, len(pairs) - LAGP), len(pairs)):
                    issue_AV2(pend[i])
                r, Brr = self.ftmp()
                self.recip(r[64:128, 0:n], pO[64:128, 0:n], [BpO], [Brr])
                rb, Brb = self.ftmp()
                self.cp("dve", rb[0:64, 0:n], r[64:128, 0:n], [Brr], [Brb])
                on, Bon = self.ftmp()
                self.tt("dve", on[0:64, 0:n], pO[0:64, 0:n], rb[0:64, 0:n], ALU.mult, [BpO, Brb], [Bon])
                po = (hd % 2) * 64
                self.cp("dve", Oall[po:po + 64, hd // 2, s0:s0 + n], on[0:64, 0:n], [Bon], [self.BO])
        self.out_proj_residual(b, l, "mlaout", Oall, self.BO, ctx_out)

    def rope_rows(self, pA, BpA, pB, BpB, s0, n, dst, Bdst):
        ri = self.rot("rope", self.NROPE)
        rt, Brt = self.rope[ri], self.Brope[ri]
        l0 = s0 - CTX
        self.dma("sp", rt[64:96, 0, 0:n], self.I["rope_mla"][0, 64:96, l0:l0 + n], [], [Brt], in_barrier=False)
        self.dma("sp", rt[64:96, 1, 0:n], self.I["rope_mla"][1, 64:96, l0:l0 + n], [], [Brt], in_barrier=False)
        t1, Bt1 = self.ftmp()
        t2, Bt2 = self.ftmp()
        self.tt("dve", t1[64:96, 0:n], pA[64:96, 0:n], rt[64:96, 0, 0:n], ALU.mult, [BpA, Brt], [Bt1])
        self.tt("dve", t2[64:96, 0:n], pB[64:96, 0:n], rt[64:96, 1, 0:n], ALU.mult, [BpB, Brt], [Bt2])
        self.tt("dve", dst, t1[64:96, 0:n], t2[64:96, 0:n], ALU.add, [Bt1, Bt2], [Bdst])


    def mixer_sc(self, b, l, j, ctx_out):
        P, I = self.P, self.I
        P.barrier()
        av = self.compute_a_all(b, l)
        zT = self.arB.bitcast(BF16)[:].rearrange("p (c t) -> p c t", c=NCH)
        if not hasattr(self, "BO"):
            self.BO = P.buf("Oall")
        wi = I["w_sc_in"][j].rearrange("(k p) n -> p k n", p=128)
        seqs = [(CTX, SEQ)]
        if ctx_out:
            seqs = [(0, CTX), (CTX, SEQ)]
        for c in range(NCH):
            slot, Bs = self.sload("scin", c)
            sv = slot[:, 0:3072].rearrange("p (t k n) -> p t k n", t=3, k=8)
            k0 = self.small[:, 8 + c:9 + c]
            k1 = self.small[:, 16 + c:17 + c]
            k2 = self.small[:, 24 + c:25 + c]
            for (q0, qlen) in seqs:
                t0 = 0
                while t0 < qlen:
                    m = min(510, qlen - t0)
                    lo, hi = max(t0 - 1, 0), min(t0 + m + 1, qlen)
                    N = hi - lo
                    reads_a = [self.Ba[k] for k in range(len(BLOCKS))
                               if BLOCKS[k][0] < q0 + hi and BLOCKS[k][0] + BLOCKS[k][1] > q0 + lo]
                    pbg, Bpbg = self.bank("proj", [0, 1, 2, 3])
                    pcg, Bpcg = self.bank("proj", [0, 1, 2, 3])
                    phh, Bphh = self.bank("proj", [0, 1, 2, 3])
                    for kc in range(8):
                        self.mm(pbg[:, 0:m], sv[:, 0, kc, :], av[:, kc, q0 + t0:q0 + t0 + m], kc == 0, kc == 7, [Bs] + reads_a, [Bpbg])
                    for kc in range(8):
                        self.mm(pcg[:, 0:N], sv[:, 1, kc, :], av[:, kc, q0 + lo:q0 + hi], kc == 0, kc == 7, [Bs] + reads_a, [Bpcg])
                    for kc in range(8):
                        self.mm(phh[:, 0:N], sv[:, 2, kc, :], av[:, kc, q0 + lo:q0 + hi], kc == 0, kc == 7, [Bs] + reads_a, [Bphh])
                    cg, Bcg = self.ftmp()
                    xt, Bxt = self.ftmp()
                    u, Bu = self.ftmp()
                    self.act(cg[:, 0:N], pcg[:, 0:N], AF.Identity, [Bpcg], [Bcg])
                    self.tt("dve", xt[:, 0:N], phh[:, 0:N], cg[:, 0:N], ALU.mult, [Bphh, Bcg], [Bxt])
                    cen = t0 - lo
                    self.ts("dve", u[:, 0:m], xt[:, cen:cen + m], k1, None, ALU.mult, None, [Bxt, self.Bconst], [Bu])
                    ta = max(t0, 1)
                    cnt = t0 + m - ta
                    if cnt > 0:
                        self.stt("dve", u[:, ta - t0:ta - t0 + cnt], xt[:, ta - 1 - lo:ta - 1 - lo + cnt], k0,
                                 u[:, ta - t0:ta - t0 + cnt], ALU.mult, ALU.add, [Bxt, Bu, self.Bconst], [Bu])
                    tb = min(t0 + m - 1, qlen - 2)
                    cnt = tb - t0 + 1
                    if cnt > 0:
                        self.stt("dve", u[:, 0:cnt], xt[:, t0 + 1 - lo:t0 + 1 - lo + cnt], k2,
                                 u[:, 0:cnt], ALU.mult, ALU.add, [Bxt, Bu, self.Bconst], [Bu])
                    self.tt("dve", zT[:, c, q0 + t0:q0 + t0 + m], pbg[:, 0:m], u[:, 0:m], ALU.mult, [Bpbg, Bu], [self.BO])
                    t0 += m
        self.out_proj_residual(b, l, "scout", zT, self.BO, ctx_out)

    def mlp(self, b, l, ctx_out):
        P, I = self.P, self.I
        P.barrier()
        if not hasattr(self, "Ba2"):
            self.Ba2 = [P.buf("a2_0"), P.buf("a2_1")]
            self.Bhid = [P.buf(f"hid{jg}") for jg in range(8)]
            self.Bfst = P.buf("fstage")
        arA_bf = self.arA_bf
        a2v = [arA_bf[:, 0:4096].rearrange("p (c n) -> p c n", c=NCH),
               arA_bf[:, 4096:8192].rearrange("p (c n) -> p c n", c=NCH)]
        fbuf = self.arA[:, 4096:4096 + NCH * 512].rearrange("p (c n) -> p c n", c=NCH)
        hid = self.arB.bitcast(BF16)[:, 0:32 * 512].rearrange("p (j n) -> p j n", j=32)
        w_in = I["w_mlp_in"][l].rearrange("(k p) n -> p k n", p=128)
        w_out = I["w_mlp_out"][l].rearrange("(j p) n -> p j n", p=128)
        blks = [k for k in range(len(BLOCKS)) if not (BLOCKS[k][2] == "ctx" and not ctx_out)]

        def pre(i):
            a2_ = a2v[i % 2]
            self.prenorm(b, l, 1, blks[i], lambda c, a2_=a2_: a2_[:, c, :], self.Ba2[i % 2])

        pre(0)
        pending_post = None
        for bi, blk in enumerate(blks):
            s0, n, kind = BLOCKS[blk]
            par = bi % 2
            a2 = a2v[par]
            for jg in range(8):
                if jg == 1 and pending_post is not None:
                    self.postnorm_residual(b, l, 1, pending_post, fbuf, self.Bfst)
                    pending_post = None
                if jg == 3 and bi + 1 < len(blks):
                    pre(bi + 1)
                slot, Bs = self.sload(f"win{l}", jg)
                sv = slot[:].rearrange("p (k n) -> p k n", k=8)
                for jj in range(4):
                    jx = jg * 4 + jj
                    pb, Bpb = self.bank("proj", [0, 1, 2, 3])
                    for kc in range(8):
                        self.mm(pb[:, 0:n], sv[:, kc, jj * 128:(jj + 1) * 128], a2[:, kc, 0:n], kc == 0, kc == 7,
                                [Bs, self.Ba2[par]], [Bpb])
                    t, Bt = self.ftmp()
                    self.act(t[:, 0:n], pb[:, 0:n], AF.Relu, [Bpb], [Bt])
                    self.tt("dve", hid[:, jx, 0:n], t[:, 0:n], t[:, 0:n], ALU.mult, [Bt], [self.Bhid[jg]])
            for jg in range(8):
                slot, Bs = self.sload(f"wout{l}", jg)
                sv = slot[:].rearrange("p (j n) -> p j n", j=4)
                for fo in range(8):
                    for jj in range(4):
                        self.mm(self.ps[fo][:, 0:n], sv[:, jj, fo * 128:(fo + 1) * 128], hid[:, jg * 4 + jj, 0:n],
                                jg == 0 and jj == 0, jg == 7 and jj == 3, [Bs, self.Bhid[jg]], [self.Bps[fo]])
            for fo in range(8):
                if fo % 2 == 0:
                    self.act(fbuf[:, fo, 0:n], self.ps[fo][:, 0:n], AF.Identity, [self.Bps[fo]], [self.Bfst])
                else:
                    self.cp("dve", fbuf[:, fo, 0:n], self.ps[fo][:, 0:n], [self.Bps[fo]], [self.Bfst])
            pending_post = blk
        self.postnorm_residual(b, l, 1, pending_post, fbuf, self.Bfst)
        P.barrier()


def input_shapes(nb):
    return {
        "x": (nb, SEQ, D), "ctx": (nb, CTX, D), "cvecT": (D, 3), "ident": (128, 128),
        "w_ada": (DEPTH, D, 6 * D), "b_adaT": (DEPTH, 128, 48), "norm_gT": (128, DEPTH * 4 * NCH),
        "w_mlp_in": (DEPTH, D, HID), "w_mlp_out": (DEPTH, HID, D),
        "w_da_qkv": (2, D, 3 * D), "da_lambda": (2, 4, 64), "da_sublnT": (128, 2), "w_da_out": (2, D, D),
        "w_mla_down": (1, D, 416), "mla_q_normT": (128, 2), "w_mla_uq": (1, 256, 1536),
        "mla_kv_normT": (128, 1), "w_mla_ukv": (1, 128, 2048), "w_mla_out": (1, D, D),
        "w_sc_in": (1, D, 3 * D), "sc_convT": (128, 24), "w_sc_out": (1, D, D),
        "rope_da": (2, 128, SEQ), "rope_mla": (2, 128, SEQ),
    }


def rope_tables():
    t = np.arange(SEQ)
    row = (t // GRID_W).astype(np.float32)
    col = (t % GRID_W).astype(np.float32)

    def tab(rot_dim):
        quarter = rot_dim // 4
        inv = (ROPE_THETA ** (-np.arange(quarter, dtype=np.float32) / quarter)).astype(np.float32)
        ang = np.concatenate([row[:, None] * inv, col[:, None] * inv], axis=-1).astype(np.float32)
        return np.cos(ang).T.astype(np.float32), np.sin(ang).T.astype(np.float32)

    c32, s32 = tab(64)
    da = np.zeros((2, 128, SEQ), np.float32)
    for g in range(4):
        da[0, g * 32:(g + 1) * 32] = c32
        da[1, g * 32:(g + 1) * 32] = -s32 if g % 2 == 0 else s32
    c16, s16 = tab(32)
    mla = np.zeros((2, 128, SEQ), np.float32)
    mla[0, 64:80] = c16
    mla[0, 80:96] = c16
    mla[1, 64:80] = -s16
    mla[1, 80:96] = s16
    return da, mla


def make_in_maps(inputs, nb, n_cores):
    f = lambda a: np.ascontiguousarray(np.asarray(a, dtype=np.float32))
    da, mla = rope_tables()
    shared = {
        "ident": np.eye(128, dtype=np.float32),
        "w_ada": f(inputs["w_ada"]),
        "b_adaT": f(np.asarray(inputs["b_ada"]).reshape(DEPTH, 48, 128).transpose(0, 2, 1)),
        "norm_gT": f(np.asarray(inputs["norm_g"]).reshape(DEPTH * 4 * NCH, 128).T),
        "w_mlp_in": f(inputs["w_mlp_in"]), "w_mlp_out": f(inputs["w_mlp_out"]),
        "w_da_qkv": f(inputs["w_da_qkv"]), "da_lambda": f(inputs["da_lambda"]),
        "da_sublnT": f(np.asarray(inputs["da_subln"]).T), "w_da_out": f(inputs["w_da_out"]),
        "w_mla_down": f(inputs["w_mla_down"]),
        "mla_q_normT": f(np.asarray(inputs["mla_q_norm"]).reshape(2, 128).T),
        "w_mla_uq": f(inputs["w_mla_uq"]),
        "mla_kv_normT": f(np.asarray(inputs["mla_kv_norm"]).reshape(1, 128).T),
        "w_mla_ukv": f(inputs["w_mla_ukv"]), "w_mla_out": f(inputs["w_mla_out"]),
        "w_sc_in": f(inputs["w_sc_in"]),
        "sc_convT": f(np.asarray(inputs["sc_conv"]).reshape(3 * NCH, 128).T),
        "w_sc_out": f(inputs["w_sc_out"]),
        "rope_da": da, "rope_mla": mla,
    }
    x, c, ctx, c_ctx = (np.asarray(inputs[k], dtype=np.float32) for k in ("x", "c", "ctx", "c_ctx"))
    maps = []
    for i in range(n_cores):
        sl = slice(i * nb, (i + 1) * nb)
        cols = [c[i * nb + r] for r in range(nb)]
        while len(cols) < 2:
            cols.append(np.zeros(D, np.float32))
        cv = np.stack(cols + [c_ctx], axis=1)
        m = dict(shared)
        m["x"] = f(x[sl])
        m["ctx"] = f(ctx[sl])
        m["cvecT"] = f(cv)
        maps.append(m)
    return maps


_CACHE = {}


def run(inputs, nb, n_cores, n_layers=DEPTH, trace=False, dbg=""):
    key = (nb, n_layers, dbg)
    if key not in _CACHE:
        _CACHE[key] = Builder(nb, n_layers, dbg).build()
    nc = _CACHE[key]
    maps = make_in_maps(inputs, nb, n_cores)
    res = run_bass_kernel_spmd(nc, maps, core_ids=list(range(n_cores)), trace=trace)
    out = np.concatenate([r["y"] for r in res.results], axis=0)
    if dbg:
        return out, res.results[0]
    return out, res


def kernel(**inputs):
    out, _ = run(inputs, nb=2, n_cores=8)
    return out.astype(np.float32)
```

```python
import math
import contextlib
import numpy as np
import concourse.bass as bass
import concourse.mybir as mybir
from concourse.bass_utils import run_bass_kernel_spmd

F32 = mybir.dt.float32
BF16 = mybir.dt.bfloat16
AF = mybir.ActivationFunctionType
ALU = mybir.AluOpType

D = 1024
SEQ = 2048
CTX = 256
T = SEQ + CTX
DEPTH = 4
HID = 4096
NCH = 8
EPS = 1e-6
GRID_W = 64
ROPE_THETA = 10000.0
DA_SCALE = 64 ** -0.5
MLA_SCALE = 96 ** -0.5
ENGS = ("pe", "act", "dve", "pool", "sp")

BLOCKS = [(0, 256, "ctx")] + [(256 + 512 * i, 512, "lat") for i in range(4)]


class Buf:
    __slots__ = ("name", "last_w", "readers", "excl", "sem", "dcount")

    def __init__(self, name, excl=False):
        self.name = name
        self.last_w = None
        self.readers = {}
        self.excl = excl
        self.sem = None
        self.dcount = 0


class Instr:
    __slots__ = ("eng", "fn", "deps", "signal", "val", "is_dma", "dsem", "dval", "small")

    def __init__(self, eng, fn, is_dma=False):
        self.small = False
        self.eng = eng
        self.fn = fn
        self.deps = []
        self.signal = False
        self.val = None
        self.is_dma = is_dma
        self.dsem = None
        self.dval = None


class Prog:
    def __init__(self, nc):
        self.nc = nc
        self.streams = {e: [] for e in ENGS}
        self.bufs = []
        self.pending = {e: [] for e in ENGS}
        self.last = {e: None for e in ENGS}
        self.dmas_since_barrier = []
        self.strict = ()

    def buf(self, name, excl=False):
        b = Buf(name, excl)
        self.bufs.append(b)
        return b

    def _dep(self, ins, other):
        if other is None or other is ins:
            return
        if (not other.is_dma) and (not ins.is_dma) and other.eng == ins.eng:
            if ins.eng == "pe" or not (other.small or (self.strict and ins.eng in self.strict)):
                return
        ins.deps.append(other)
        if not other.is_dma:
            other.signal = True

    def _all_readers(self, b):
        for k, r in b.readers.items():
            if k == "dma":
                for x in r:
                    yield x
            else:
                yield r

    def _track(self, ins, reads, writes):
        for d in self.pending[ins.eng]:
            self._dep(ins, d)
        self.pending[ins.eng] = []
        for b in reads:
            if b.excl:
                self._dep(ins, b.last_w)
                for r in self._all_readers(b):
                    self._dep(ins, r)
                b.readers = {}
                b.last_w = ins
            else:
                self._dep(ins, b.last_w)
                if ins.is_dma:
                    b.readers.setdefault("dma", []).append(ins)
                else:
                    b.readers[ins.eng] = ins
        for b in writes:
            self._dep(ins, b.last_w)
            for r in self._all_readers(b):
                self._dep(ins, r)
            b.readers = {}
            b.last_w = ins

    def op(self, eng, fn, reads=(), writes=(), small=False):
        ins = Instr(eng, fn)
        ins.small = small
        self._track(ins, reads, writes)
        self.streams[eng].append(ins)
        self.last[eng] = ins
        return ins

    def dma(self, queue, fn, reads=(), writes=(), sem_buf=None, in_barrier=True):
        ins = Instr(queue, fn, is_dma=True)
        if sem_buf is None:
            sem_buf = writes[0] if writes else reads[0]
        sem_buf.dcount += 1
        ins.dsem = sem_buf
        ins.dval = 16 * sem_buf.dcount
        self._track(ins, reads, writes)
        self.streams[queue].append(ins)
        if in_barrier:
            self.dmas_since_barrier.append(ins)
        return ins

    def barrier(self):
        deps = [self.last[e] for e in ENGS if self.last[e] is not None]
        deps += self.dmas_since_barrier
        self.dmas_since_barrier = []
        for e in ENGS:
            if e == "pool":
                continue
            self.pending[e] = list(self.pending[e]) + deps

    def emit(self, final_waits=()):
        nc = self.nc
        with contextlib.ExitStack() as es:
            esem = {e: es.enter_context(nc.semaphore("s_" + e)) for e in ENGS}
            for b in self.bufs:
                if b.dcount > 0:
                    b.sem = es.enter_context(nc.semaphore("d_" + b.name))
            for e in ENGS:
                c = 0
                for ins in self.streams[e]:
                    if (not ins.is_dma) and ins.signal:
                        c += 1
                        ins.val = c
            block = es.enter_context(nc.Block())

            def run(ename, e):
                waited = {}
                for ins in self.streams[ename]:
                    for d in ins.deps:
                        if d.is_dma:
                            sem, val = d.dsem.sem, d.dval
                        else:
                            sem, val = esem[d.eng], d.val
                        key = sem.num
                        if waited.get(key, 0) >= val:
                            continue
                        waited[key] = val
                        e.wait_ge(sem, val)
                    r = ins.fn(e)
                    if ins.is_dma:
                        r.then_inc(ins.dsem.sem, 16)
                    elif ins.signal:
                        r.then_inc(esem[ename], 1)
                if ename == "sp":
                    for d in final_waits:
                        e.wait_ge(d.dsem.sem, d.dval)

            @block.tensor
            def _(e):
                run("pe", e)

            @block.scalar
            def _(e):
                run("act", e)

            @block.vector
            def _(e):
                run("dve", e)

            @block.gpsimd
            def _(e):
                run("pool", e)

            @block.sync
            def _(e):
                run("sp", e)


class Builder:
    def __init__(self, nb, n_layers, dbg=""):
        self.dbg = dbg
        self.nb = nb
        self.n_layers = n_layers
        self.nc = bass.Bass("TRN2", target_bir_lowering=False)
        self.P = Prog(self.nc)
        if "strict" in dbg:
            self.P.strict = ("dve", "act", "pool")
        self.es = contextlib.ExitStack()
        self.out_dmas = []
        self.conv_gate = None
        self.rr = {}
        self.scr = {}

    @staticmethod
    def _small(ap):
        n = 1
        for d in ap.shape[1:]:
            n *= int(d)
        return n < 512

    def mm(self, out, lhsT, rhs, start, stop, rd, wr):
        self.P.op("pe", lambda e: e.matmul(out, lhsT=lhsT, rhs=rhs, start=start, stop=stop), reads=rd, writes=wr)

    def tr(self, out, in_, rd, wr):
        ident = self.ident
        self.P.op("pe", lambda e: e.transpose(out, in_, ident[:]), reads=list(rd) + [self.Bconst], writes=wr)

    def act(self, out, in_, func, rd, wr, scale=1.0, bias=None):
        if bias is None:
            self.P.op("act", lambda e: e.activation(out=out, in_=in_, func=func, scale=scale), reads=rd, writes=wr, small=self._small(out))
        else:
            self.P.op("act", lambda e: e.activation(out=out, in_=in_, func=func, scale=scale, bias=bias), reads=rd, writes=wr, small=self._small(out))

    def tt(self, eng, out, in0, in1, op, rd, wr):
        self.P.op(eng, lambda e: e.tensor_tensor(out=out, in0=in0, in1=in1, op=op), reads=rd, writes=wr, small=self._small(out))

    def stt(self, eng, out, in0, scalar, in1, op0, op1, rd, wr):
        self.P.op(eng, lambda e: e.scalar_tensor_tensor(out=out, in0=in0, scalar=scalar, in1=in1, op0=op0, op1=op1), reads=rd, writes=wr, small=self._small(out))

    def ts(self, eng, out, in0, s1, s2, op0, op1, rd, wr):
        if s2 is None:
            self.P.op(eng, lambda e: e.tensor_scalar(out=out, in0=in0, scalar1=s1, scalar2=None, op0=op0), reads=rd, writes=wr, small=self._small(out))
        else:
            self.P.op(eng, lambda e: e.tensor_scalar(out=out, in0=in0, scalar1=s1, scalar2=s2, op0=op0, op1=op1), reads=rd, writes=wr, small=self._small(out))

    def cp(self, eng, out, in_, rd, wr):
        self.P.op(eng, lambda e: e.tensor_copy(out=out, in_=in_), reads=rd, writes=wr, small=self._small(out))

    def memset(self, eng, ap, val, wr):
        self.P.op(eng, lambda e: e.memset(ap, val), writes=wr, small=True)

    def recip(self, out, in_, rd, wr):
        self.P.op("dve", lambda e: e.reciprocal(out=out, in_=in_), reads=rd, writes=wr, small=self._small(out))

    def dma(self, queue, out, in_, rd, wr, sem_buf=None, in_barrier=True):
        return self.P.dma(queue, lambda e: e.dma_start(out=out, in_=in_), reads=rd, writes=wr, sem_buf=sem_buf,
                          in_barrier=in_barrier)

    def dump(self, tag, ap, bufs, dt=None):
        if "dump" not in self.dbg:
            return
        if not hasattr(self, "dumps"):
            self.dumps = {}
        if tag in self.dumps:
            return
        dt = dt or ap.dtype
        d = self.nc.dram_tensor("dbg_" + tag, list(ap.shape), dt, kind="ExternalOutput").ap()
        self.dumps[tag] = d
        x = self.dma("sp", d, ap, list(bufs), [], sem_buf=bufs[0])
        self.out_dmas.append(x)

    def rot(self, key, n):
        v = self.rr.get(key, 0)
        self.rr[key] = v + 1
        return v % n

    def sb(self, name, shape, dt):
        return self.es.enter_context(self.nc.sbuf_tensor("sb_" + name, shape, dt))

    def dram_in(self, name, shape):
        return self.nc.dram_tensor(name, list(shape), F32, kind="ExternalInput").ap()

    def build(self):
        nc, P = self.nc, self.P
        nb = self.nb
        I = {}
        for name, shape in input_shapes(nb).items():
            I[name] = self.dram_in(name, shape)
        self.I = I
        self.y = nc.dram_tensor("y", [nb, SEQ, D], F32, kind="ExternalOutput").ap()

        self.h = self.sb("h", [128, NCH, T], F32)
        self.Bh = [[P.buf(f"h{c}_{k}") for k in range(len(BLOCKS))] for c in range(NCH)]
        self.arA = self.sb("arA", [128, NCH * T // 2], F32)
        self.arA_bf = self.arA.bitcast(BF16) if hasattr(self.arA, "bitcast") else None
        self.arB = self.sb("arB", [128, NCH * T // 2], F32)
        self.NSLOT = 3
        self.ring = [self.sb(f"ring{i}", [128, 4096], BF16) for i in range(self.NSLOT)]
        self.Bring = [P.buf(f"ring{i}") for i in range(self.NSLOT)]
        self.Bringsw = [P.buf(f"ringsw{i}") for i in range(self.NSLOT)]
        self.hk = self.sb("hk", [128, T], BF16)
        self.hv = self.sb("hv", [128, 18 * 128], BF16)
        self.hqs = [self.sb(f"hq{i}", [128, 512], BF16) for i in range(4)]
        self.Bhk, self.Bhv = P.buf("hk"), P.buf("hv")
        self.Bhqs = [P.buf(f"hq{i}") for i in range(4)]
        self.NROPE = 1
        self.rope = [self.sb(f"rope{i}", [128, 2, 512], F32) for i in range(self.NROPE)]
        self.Brope = [P.buf(f"rope{i}") for i in range(self.NROPE)]
        self.NBT = 4
        self.bt = [self.sb(f"bt{i}", [128, 512], BF16) for i in range(self.NBT)]
        self.Bbt = [P.buf(f"bt{i}") for i in range(self.NBT)]
        self.NFT = 3
        self.ft = [self.sb(f"ft{i}", [128, 512], F32) for i in range(self.NFT)]
        self.Bft = [P.buf(f"ft{i}") for i in range(self.NFT)]
        self.rs = self.sb("rs", [128, 512], F32)
        self.Brs = P.buf("rs")
        self.comb = self.sb("comb", [128, 512], F32)
        self.Bcomb = P.buf("comb")
        self.sqd = self.sb("sqd", [128, 512], BF16)
        self.Bsqd = P.buf("sqd")
        self.ident = self.sb("ident", [128, 128], F32)
        self.ones = {k: self.sb(f"ones{k}", [128, 128], BF16) for k in (1024, 256, 128, 1)}
        self.epsc = self.sb("epsc", [128, 1], F32)
        self.onesf = self.sb("onesf", [128, 64], F32)
        self.scT = self.sb("scT", [128, NCH, 3], BF16)
        self.cv = self.sb("cv", [128, NCH, 3], F32)
        self.mvec = self.sb("mvec", [128, DEPTH, 48, 3], F32)
        self.badaT = self.sb("badaT", [128, DEPTH, 48], F32)
        self.ngT = self.sb("ngT", [128, DEPTH, 4, NCH], F32)
        self.der = self.sb("der", [128, DEPTH, 4, NCH, 3], F32)
        self.small = self.sb("small", [128, 64], F32)
        self.Bconst = P.buf("consts")
        self.Bmvec_l = [P.buf(f"mvec{l}") for l in range(DEPTH)]
        self.Bder_l = [P.buf(f"der{l}") for l in range(DEPTH)]
        self.Bsmall = P.buf("small")
        self.Bgate = P.buf("gate")
        self.ps = [self.es.enter_context(nc.psum_tensor(f"ps{i}", [128, 512], F32)) for i in range(8)]
        self.Bps = [P.buf(f"ps{i}", excl=True) for i in range(8)]

        self.prologue()
        for b in range(nb):
            self.load_inputs(b)
            for l in range(self.n_layers):
                self.layer(b, l)
            self.store_output(b)
        P.emit(final_waits=self.out_dmas)
        return nc

    def wload(self, src_ap, view):
        s = self.rot("ring", self.NSLOT)
        dst = view(self.ring[s])
        self.dma("pool", dst, src_ap, rd=[], wr=[self.Bring[s]], sem_buf=self.Bringsw[s], in_barrier=False)
        return self.ring[s], self.Bring[s]

    def scratch_tensor(self, key, n_slots, width):
        t = self.nc.dram_tensor("ws_" + key, [n_slots, 128, width], BF16).ap()
        self.scr[key] = (t, self.P.buf("ws_" + key))
        return t, self.scr[key][1]

    def conv(self, dst, src, Bdst):
        gate = []
        if self.conv_gate is not None:
            gate, self.conv_gate = [self.conv_gate], None
        self.dma("pool", dst, src, gate, [], sem_buf=Bdst, in_barrier=False)
        Bdst.last_w = self.P.streams["pool"][-1]

    def sload(self, key, idx, width=None):
        t, Bt = self.scr[key]
        s = self.rot("ring", self.NSLOT)
        w = t.shape[2] if width is None else width
        self.dma("sp", self.ring[s][:, 0:w], t[idx, :, 0:w], [Bt], [self.Bring[s]], in_barrier=False)
        return self.ring[s], self.Bring[s]

    def prep_weights(self, l):
        I = self.I
        kind, j = l % 3, l // 3
        if "force_kind=" in self.dbg:
            kind, j = int(self.dbg.split("force_kind=")[1][0]), 0
        def halves(key, w_ap):
            t, Bt = self.scratch_tensor(key, 2, 4096)
            wv = w_ap.rearrange("(k p) n -> p k n", p=128)
            for half in range(2):
                self.conv(t[half].rearrange("p (k n) -> p k n", k=8), wv[:, :, half * 512:(half + 1) * 512], Bt)
        if kind == 0:
            t, Bt = self.scratch_tensor(f"daqkv{j}", 8, 3072)
            wq = I["w_da_qkv"][j].rearrange("(k p) n -> p k n", p=128)
            for hd in range(8):
                tv = t[hd].rearrange("p (t k n) -> p t k n", t=3, k=8)
                for t3 in range(3):
                    self.conv(tv[:, t3], wq[:, :, t3 * 1024 + hd * 128: t3 * 1024 + (hd + 1) * 128], Bt)
            halves(f"daout{j}", I["w_da_out"][j])
        elif kind == 1:
            t, Bt = self.scratch_tensor("mladown", 1, 4096)
            wd = I["w_mla_down"][j].rearrange("(k p) n -> p k n", p=128)
            tv = t[0].rearrange("p (k n) -> p k n", k=8)
            self.conv(tv[:, :, 0:416], wd, Bt)
            self.conv(tv[:, :, 480:496], wd[:, :, 400:416], Bt)
            self.conv(tv[:, :, 496:512], wd[:, :, 384:400], Bt)
            t, Bt = self.scratch_tensor("mlahead", 16, 512)
            wuq = I["w_mla_uq"][j].rearrange("(k p) n -> p k n", p=128)
            wukv = I["w_mla_ukv"][j]
            for hd in range(16):
                wqn = t[hd][:, 0:192].rearrange("p (k n) -> p k n", k=2)
                wqs = t[hd][:, 192:384].rearrange("p (k n) -> p k n", k=2)
                self.conv(wqn, wuq[:, :, hd * 96:(hd + 1) * 96], Bt)
                self.conv(wqs[:, :, 64:80], wuq[:, :, hd * 96 + 80:hd * 96 + 96], Bt)
                self.conv(wqs[:, :, 80:96], wuq[:, :, hd * 96 + 64:hd * 96 + 80], Bt)
                self.conv(t[hd][:, 384:512], wukv[:, hd * 128:(hd + 1) * 128], Bt)
            halves("mlaout", I["w_mla_out"][j])
        else:
            t, Bt = self.scratch_tensor("scin", 8, 3072)
            wi = I["w_sc_in"][j].rearrange("(k p) n -> p k n", p=128)
            for c in range(8):
                tv = t[c].rearrange("p (t k n) -> p t k n", t=3, k=8)
                for t3 in range(3):
                    self.conv(tv[:, t3], wi[:, :, t3 * 1024 + c * 128: t3 * 1024 + (c + 1) * 128], Bt)
            halves("scout", I["w_sc_out"][j])
        if l + 1 < self.n_layers:
            t, Bt = self.scratch_tensor(f"ada{l + 1}", 12, 4096)
            wa = I["w_ada"][l + 1].rearrange("(k p) n -> p k n", p=128)
            for g in range(12):
                self.conv(t[g].rearrange("p (k n) -> p k n", k=8), wa[:, :, g * 512:(g + 1) * 512], Bt)
        t, Bt = self.scratch_tensor(f"win{l}", 8, 4096)
        w_in = I["w_mlp_in"][l].rearrange("(k p) n -> p k n", p=128)
        for jg in range(8):
            self.conv(t[jg].rearrange("p (k n) -> p k n", k=8), w_in[:, :, jg * 512:(jg + 1) * 512], Bt)
        t, Bt = self.scratch_tensor(f"wout{l}", 8, 4096)
        w_out = I["w_mlp_out"][l].rearrange("(j p) n -> p j n", p=128)
        for jg in range(8):
            self.conv(t[jg].rearrange("p (j n) -> p j n", j=4), w_out[:, jg * 4:(jg + 1) * 4, :], Bt)

    def ftmp(self):
        i = self.rot("ft", self.NFT)
        return self.ft[i], self.Bft[i]

    def btmp(self):
        i = self.rot("bt", self.NBT)
        return self.bt[i], self.Bbt[i]

    def bank(self, key, banks):
        i = banks[self.rot(key, len(banks))]
        return self.ps[i], self.Bps[i]

    def prologue(self):
        P, I = self.P, self.I
        Bc = self.Bconst
        self.dma("sp", self.ident[:], I["ident"], [], [Bc])
        self.dma("sp", self.cv[:], I["cvecT"].rearrange("(k p) r -> p k r", p=128), [], [Bc])
        self.dma("sp", self.badaT[:], I["b_adaT"].rearrange("l p c -> p l c"), [], [Bc])
        self.dma("sp", self.ngT[:], I["norm_gT"].rearrange("p (l s c) -> p l s c", l=DEPTH, s=4), [], [Bc])
        self.dma("sp", self.small[:, 0:2], I["da_sublnT"], [], [Bc])
        self.dma("sp", self.small[:, 2:4], I["mla_q_normT"], [], [Bc])
        self.dma("sp", self.small[:, 4:5], I["mla_kv_normT"], [], [Bc])
        self.dma("sp", self.small[:, 8:32], I["sc_convT"], [], [Bc])
        lam_src = bass.AP(I["da_lambda"].tensor, 0, [[0, 128], [1, 512]])
        self.dma("sp", self.rope[0][:, 0, :], lam_src, [], [self.Brope[0]])
        for k, t in self.ones.items():
            self.memset("pool", t[:], 1.0 / k, [Bc])
        self.memset("pool", self.epsc[:], EPS, [Bc])
        self.memset("pool", self.onesf[:], 1.0, [Bc])
        self.act(self.scT[:], self.cv[:], AF.Silu, [Bc], [Bc])
        self.adaln(0)
        self.prep_weights(0)
        for j in range(2):
            li = 3 * j
            if li >= self.n_layers:
                break
            lam_init = 0.8 - 0.6 * math.exp(-0.3 * li)
            lv = self.rope[0][:, 0, j * 256:(j + 1) * 256].rearrange("p (a d) -> p a d", a=4)
            t, Bt = self.ftmp()
            self.tt("dve", t[:, 0:64], lv[:, 0], lv[:, 1], ALU.mult, [self.Brope[0]], [Bt])
            self.tt("dve", t[:, 64:128], lv[:, 2], lv[:, 3], ALU.mult, [self.Brope[0]], [Bt])
            self.P.op("dve", lambda e, t=t: e.reduce_sum(out=t[:, 128:130], in_=t[:, 0:128].rearrange("p (a d) -> p a d", a=2),
                                                       axis=mybir.AxisListType.X), reads=[Bt], writes=[Bt], small=True)
            self.act(t[:, 130:132], t[:, 128:130], AF.Exp, [Bt], [Bt])
            self.stt("dve", self.small[:, 32 + j:33 + j], t[:, 131:132], -lam_init, t[:, 130:131], ALU.add, ALU.subtract,
                     [Bt], [self.Bsmall])
            self.ts("dve", self.small[:, 34 + j:35 + j], self.small[:, j:j + 1], 1.0 - lam_init, None, ALU.mult, None,
                    [Bc], [self.Bsmall])

    def adaln(self, l):
        I, Bc = self.I, self.Bconst
        if True:
            pm, Bpm = self.bank("gen", [6, 7])
            for g in range(12):
                if l == 0:
                    src = I["w_ada"][l].rearrange("(k p) n -> p k n", p=128)[:, :, g * 512:(g + 1) * 512]
                    slot, Bs = self.wload(src, lambda t: t[:].rearrange("p (k n) -> p k n", k=8))
                else:
                    slot, Bs = self.sload(f"ada{l}", g)
                sv = slot[:].rearrange("p (k n) -> p k n", k=8)
                for cc in range(4):
                    ch = g * 4 + cc
                    for kc in range(8):
                        self.mm(pm[:, ch * 3:ch * 3 + 3], sv[:, kc, cc * 128:(cc + 1) * 128], self.scT[:, kc, :],
                                kc == 0, kc == 7, [Bs, Bc], [Bpm])
            self.tt("dve", self.mvec[:, l], pm[:, 0:144].rearrange("p (c r) -> p c r", r=3),
                    self.badaT[:, l].unsqueeze(2).to_broadcast([128, 48, 3]), ALU.add, [Bpm, Bc], [self.Bmvec_l[l]])
            mv = self.mvec[:, l].rearrange("p (n c) r -> p n c r", n=6)
            for idx, (mi, gi, plus1) in enumerate([(1, 0, True), (2, 1, False), (4, 2, True), (5, 3, False)]):
                gb = self.ngT[:, l, gi].unsqueeze(2).to_broadcast([128, NCH, 3])
                dst = self.der[:, l, idx]
                if plus1:
                    self.stt("dve", dst, mv[:, mi], 1.0, gb, ALU.add, ALU.mult, [self.Bmvec_l[l], Bc], [self.Bder_l[l]])
                else:
                    self.tt("dve", dst, mv[:, mi], gb, ALU.mult, [self.Bmvec_l[l], Bc], [self.Bder_l[l]])

    def load_inputs(self, b):
        P, I = self.P, self.I
        P.barrier()
        stage = [self.arA[:, 0:1024], self.arA[:, 1024:2048]]
        Bst = getattr(self, "Bstage_in", None)
        if Bst is None:
            Bst = self.Bstage_in = [P.buf("stin0"), P.buf("stin1")]
        for tt_ in range(T // 128):
            s = tt_ % 2
            if tt_ < 2:
                src = I["ctx"][b, tt_ * 128:(tt_ + 1) * 128, :]
            else:
                src = I["x"][b, (tt_ - 2) * 128:(tt_ - 1) * 128, :]
            self.dma("sp", stage[s], src, [], [Bst[s]])
            blk = self.tok2blk(tt_ * 128)
            for half in range(2):
                pb, Bpb = self.bank("gen", [6, 7])
                for q in range(4):
                    c = half * 4 + q
                    self.tr(pb[:, q * 128:(q + 1) * 128], stage[s][:, c * 128:(c + 1) * 128], [Bst[s]], [Bpb])
                for q in range(4):
                    c = half * 4 + q
                    eng = "act" if q % 2 == 0 else "dve"
                    if eng == "act":
                        self.act(self.h[:, c, tt_ * 128:(tt_ + 1) * 128], pb[:, q * 128:(q + 1) * 128], AF.Identity,
                                 [Bpb], [self.Bh[c][blk]])
                    else:
                        self.cp("dve", self.h[:, c, tt_ * 128:(tt_ + 1) * 128], pb[:, q * 128:(q + 1) * 128],
                                [Bpb], [self.Bh[c][blk]])
        P.barrier()

    def tok2blk(self, tok):
        for i, (s, n, _) in enumerate(BLOCKS):
            if s <= tok < s + n:
                return i
        raise ValueError

    def store_output(self, b):
        P = self.P
        P.barrier()
        stage = [self.arA[:, 0:1024], self.arA[:, 1024:2048]]
        Bst = getattr(self, "Bstage_out", None)
        if Bst is None:
            Bst = self.Bstage_out = [P.buf("stout0"), P.buf("stout1")]
        for tt_ in range(2, T // 128):
            s = tt_ % 2
            blk = self.tok2blk(tt_ * 128)
            for half in range(2):
                pb, Bpb = self.bank("gen", [6, 7])
                for q in range(4):
                    c = half * 4 + q
                    self.tr(pb[:, q * 128:(q + 1) * 128], self.h[:, c, tt_ * 128:(tt_ + 1) * 128], [self.Bh[c][blk]], [Bpb])
                if half == 0:
                    self.act(stage[s][:, 0:512], pb[:], AF.Identity, [Bpb], [Bst[s]])
                else:
                    self.cp("dve", stage[s][:, 512:1024], pb[:], [Bpb], [Bst[s]])
            d = self.dma("sp", self.y[b, (tt_ - 2) * 128:(tt_ - 1) * 128, :], stage[s], [Bst[s]], [], sem_buf=Bst[s])
            self.out_dmas.append(d)
        P.barrier()

    def mrow(self, b, kind):
        return 2 if kind == "ctx" else b

    def rstd_from_psum(self, pm, Bpm, n):
        r, Br = self.rs, self.Brs
        self.act(r[:, 0:n], pm[:, 0:n], AF.Ln, [Bpm, self.Bconst], [Br], bias=self.epsc[:, 0:1])
        self.act(r[:, 0:n], r[:, 0:n], AF.Exp, [Br], [Br], scale=-0.5)
        return r, Br

    def prenorm(self, b, l, site, blk, dst_fn, Bdst):
        s0, n, kind = BLOCKS[blk]
        r_ = self.mrow(b, kind)
        pm, Bpm = self.bank("gen", [6, 7])
        for c in range(NCH):
            sq, Bsq = self.btmp()
            self.act(sq[:, 0:n], self.h[:, c, s0:s0 + n], AF.Square, [self.Bh[c][blk]], [Bsq])
            self.mm(pm[:, 0:n], self.ones[1024][:], sq[:, 0:n], c == 0, c == NCH - 1, [Bsq, self.Bconst], [Bpm])
        rstd, Br = self.rstd_from_psum(pm, Bpm, n)
        aidx = 0 if site == 0 else 2
        midx = 0 if site == 0 else 3
        for c in range(NCH):
            t, Bt = self.ftmp()
            self.stt("dve", t[:, 0:n], self.h[:, c, s0:s0 + n], self.der[:, l, aidx, c, r_:r_ + 1], rstd[:, 0:n],
                     ALU.mult, ALU.mult, [self.Bh[c][blk], self.Bder_l[l], Br], [Bt])
            self.act(dst_fn(c)[:, 0:n], t[:, 0:n], AF.Identity, [Bt, self.Bmvec_l[l]], [Bdst],
                     bias=self.mvec[:, l, midx * 8 + c, r_:r_ + 1])

    def postnorm_residual(self, b, l, site, blk, ybuf, By):
        s0, n, kind = BLOCKS[blk]
        r_ = self.mrow(b, kind)
        pm, Bpm = self.bank("gen", [6, 7])
        for c in range(NCH):
            sq, Bsq = self.btmp()
            self.act(sq[:, 0:n], ybuf[:, c, 0:n], AF.Square, [By], [Bsq])
            self.mm(pm[:, 0:n], self.ones[1024][:], sq[:, 0:n], c == 0, c == NCH - 1, [Bsq, self.Bconst], [Bpm])
        rstd, Br = self.rstd_from_psum(pm, Bpm, n)
        gidx = 1 if site == 0 else 3
        for c in range(NCH):
            t, Bt = self.ftmp()
            self.stt("dve", t[:, 0:n], ybuf[:, c, 0:n], self.der[:, l, gidx, c, r_:r_ + 1], rstd[:, 0:n],
                     ALU.mult, ALU.mult, [By, self.Bder_l[l], Br], [Bt])
            self.tt("dve", self.h[:, c, s0:s0 + n], self.h[:, c, s0:s0 + n], t[:, 0:n], ALU.add,
                    [Bt, self.Bh[c][blk]], [self.Bh[c][blk]])

    def layer(self, b, l):
        kind, j = l % 3, l // 3
        ctx_out = l < DEPTH - 1
        if b == 0 and l + 1 < self.n_layers:
            self.memset("dve", self.small[:, 63:64], 0.0, [self.Bgate])
        if "force_kind=" in self.dbg:
            kind, j = int(self.dbg.split("force_kind=")[1][0]), 0
        if kind == 0:
            self.mixer_da(b, l, j, ctx_out)
        elif kind == 1:
            self.mixer_mla(b, l, j, ctx_out)
        else:
            self.mixer_sc(b, l, j, ctx_out)
        if "nomlp" in self.dbg and l == self.n_layers - 1:
            return
        if b == 0 and l + 1 < self.n_layers:
            self.adaln(l + 1)
            self.conv_gate = self.Bgate
            self.prep_weights(l + 1)
        self.mlp(b, l, ctx_out)

    def a_view(self):
        return self.arA_bf[:].rearrange("p (c t) -> p c t", c=NCH)

    def compute_a_all(self, b, l):
        P = self.P
        if not hasattr(self, "Ba"):
            self.Ba = [P.buf(f"a_{k}") for k in range(len(BLOCKS))]
        av = self.a_view()
        for blk, (s0, n, kind) in enumerate(BLOCKS):
            self.prenorm(b, l, 0, blk, lambda c, s0=s0, n=n: av[:, c, s0:s0 + n], self.Ba[blk])
        return av

    def out_proj_residual(self, b, l, wkey, Oall, BO, ctx_out, kchunks=NCH):
        P = self.P
        P.barrier()
        if not hasattr(self, "Bystage"):
            self.Bystage = [P.buf("ystage0"), P.buf("ystage1")]
        ybufs = [self.arA[:, i * NCH * 512:(i + 1) * NCH * 512].rearrange("p (c n) -> p c n", c=NCH) for i in range(2)]
        blks = [k for k in range(len(BLOCKS)) if not (BLOCKS[k][2] == "ctx" and not ctx_out)]

        def proj(bi):
            s0, n, kind = BLOCKS[blks[bi]]
            ybuf, By = ybufs[bi % 2], self.Bystage[bi % 2]
            for half in range(2):
                slot, Bs = self.sload(wkey, half)
                sv = slot[:, 0:kchunks * 512].rearrange("p (k n) -> p k n", k=kchunks)
                for q in range(4):
                    fo = half * 4 + q
                    pb, Bpb = self.bank("proj", [0, 1, 2, 3])
                    for k in range(kchunks):
                        self.mm(pb[:, 0:n], sv[:, k, q * 128:(q + 1) * 128], Oall[:, k, s0:s0 + n],
                                k == 0, k == kchunks - 1, [Bs, BO], [Bpb])
                    if q % 2 == 0:
                        self.act(ybuf[:, fo, 0:n], pb[:, 0:n], AF.Identity, [Bpb], [By])
                    else:
                        self.cp("dve", ybuf[:, fo, 0:n], pb[:, 0:n], [Bpb], [By])

        proj(0)
        for bi, blk in enumerate(blks):
            if bi + 1 < len(blks):
                proj(bi + 1)
            self.postnorm_residual(b, l, 0, blk, ybufs[bi % 2], self.Bystage[bi % 2])
        P.barrier()

    def mixer_da(self, b, l, j, ctx_out):
        P, I = self.P, self.I
        P.barrier()
        av = self.compute_a_all(b, l)
        Ba_all = self.Ba
        self.dump("a", av, Ba_all)
        self.dump("mvec", self.mvec[:, l], [self.Bmvec_l[l]])
        self.dump("der", self.der[:, l], [self.Bder_l[l]])
        self.dump("small", self.small[:], [self.Bsmall])
        Oall = self.arB.bitcast(BF16)[:].rearrange("p (c t) -> p c t", c=NCH)
        if not hasattr(self, "BO"):
            self.BO = P.buf("Oall")
        wq = I["w_da_qkv"][j].rearrange("(k p) n -> p k n", p=128)
        nlam = self.small[:, 32 + j:33 + j]
        sg = self.small[:, 34 + j:35 + j]
        self.da_pending_tail = None
        for i4 in range(4):
            m_ = i4 % 2
            self.memset("dve", self.hqs[i4][(1 - m_) * 64:(2 - m_) * 64, :], 0.0, [self.Bhqs[i4]])
        for hd in range(8):
            slot, Bs = self.sload(f"daqkv{j}", hd)
            sv = slot[:, 0:3072].rearrange("p (t k n) -> p t k n", t=3, k=8)
            for blk, (s0, n, kind) in enumerate(BLOCKS):
                self.proj_rope(sv[:, 1], Bs, av, Ba_all[blk], s0, n, kind, self.hk[:, s0:s0 + n], self.Bhk, "rope_da", [4, 5, 6, 7])
            for tt_ in range(T // 128):
                blk = self.tok2blk(tt_ * 128)
                if tt_ % 4 == 0:
                    pb, Bpb = self.bank("pr4567", [4, 5, 6, 7])
                q4 = tt_ % 4
                for kc in range(8):
                    self.mm(pb[:, q4 * 128:(q4 + 1) * 128], av[:, kc, tt_ * 128:(tt_ + 1) * 128], sv[:, 2, kc, :],
                            kc == 0, kc == 7, [Bs, Ba_all[blk]], [Bpb])
                if q4 == 3 or tt_ == T // 128 - 1:
                    t0 = tt_ - q4
                    self.act(self.hv[:, t0 * 128:(tt_ + 1) * 128], pb[:, 0:(q4 + 1) * 128], AF.Identity, [Bpb], [self.Bhv])
            self.dump("hk", self.hk[:], [self.Bhk])
            self.dump("hv", self.hv[:], [self.Bhv])
            qblocks = [k for k in range(len(BLOCKS)) if not (BLOCKS[k][2] == "ctx" and not ctx_out)]

            def qproj(blk):
                s0, n, kind = BLOCKS[blk]
                qi = self.rot("hq", 2)
                hq2 = [self.hqs[2 * qi], self.hqs[2 * qi + 1]]
                Bhq2 = [self.Bhqs[2 * qi], self.Bhqs[2 * qi + 1]]
                self.proj_rope(sv[:, 0], Bs, av, Ba_all[blk], s0, n, kind, None, None, "rope_da", [7],
                               outs=[(0, 64, hq2[0][0:64, 0:n], Bhq2[0]), (64, 128, hq2[1][64:128, 0:n], Bhq2[1])])
                return hq2, Bhq2

            nxt = qproj(qblocks[0])
            for bi, blk in enumerate(qblocks):
                s0, n, kind = BLOCKS[blk]
                hq2, Bhq2 = nxt
                if bi + 1 < len(qblocks):
                    nxt = qproj(qblocks[bi + 1])
                kts = [0, 1] if kind == "ctx" else list(range(18))
                steps = [(kt, m) for kt in kts for m in range(2)]
                Sb = [4, 5, 6]
                pend = []

                def issue_S(step):
                    kt, m = step
                    pS, BpS = self.bank("S3", Sb)
                    self.mm(pS[:, 0:n], self.hk[:, kt * 128:(kt + 1) * 128],
                            hq2[m][:, 0:n], True, True, [self.Bhk, Bhq2[m]], [BpS])
                    E, BE = self.btmp()
                    self.act(E[:, 0:n], pS[:, 0:n], AF.Exp, [BpS], [BE], scale=DA_SCALE)
                    return (kt, m, E, BE)

                def issue_AV(item, first, last):
                    kt, m, E, BE = item
                    self.mm(self.ps[m][:, 0:n], self.hv[:, kt * 128:(kt + 1) * 128], E[:, 0:n], first, last,
                            [self.Bhv, BE], [self.Bps[m]])
                    self.mm(self.ps[2 + m][:, 0:n], self.ones[1][:], E[:, 0:n], first, last,
                            [self.Bconst, BE], [self.Bps[2 + m]])

                LAG = 2
                for i, st in enumerate(steps):
                    pend.append(issue_S(st))
                    if i >= LAG:
                        it = pend[i - LAG]
                        issue_AV(it, it[0] == kts[0], it[0] == kts[-1])
                    if i == min(10, len(steps) - 1) and self.da_pending_tail is not None:
                        self.da_pending_tail()
                        self.da_pending_tail = None
                for i in range(max(0, len(steps) - LAG), len(steps)):
                    it = pend[i]
                    issue_AV(it, it[0] == kts[0], it[0] == kts[-1])
                r0, Br0 = self.ftmp()
                r1, Br1 = self.ftmp()
                self.act(r0[:, 0:n], self.ps[2][:, 0:n], AF.Ln, [self.Bps[2]], [Br0])
                self.act(r1[:, 0:n], self.ps[3][:, 0:n], AF.Ln, [self.Bps[3]], [Br1])
                self.act(r0[:, 0:n], r0[:, 0:n], AF.Exp, [Br0], [Br0], scale=-1.0)
                self.act(r1[:, 0:n], r1[:, 0:n], AF.Exp, [Br1], [Br1], scale=-1.0)
                self.tt("dve", r0[:, 0:n], self.ps[0][:, 0:n], r0[:, 0:n], ALU.mult, [self.Bps[0], Br0], [Br0])
                self.tt("dve", r1[:, 0:n], self.ps[1][:, 0:n], r1[:, 0:n], ALU.mult, [self.Bps[1], Br1], [Br1])
                self.stt("dve", self.comb[:, 0:n], r1[:, 0:n], nlam, r0[:, 0:n], ALU.mult, ALU.add,
                         [Br0, Br1, self.Bsmall], [self.Bcomb])
                self.act(self.sqd[:, 0:n], self.comb[:, 0:n], AF.Square, [self.Bcomb], [self.Bsqd])

                def tail_b(hd=hd, s0=s0, n=n):
                    pm, Bpm = self.bank("S3", [4, 5, 6])
                    self.mm(pm[:, 0:n], self.ones[128][:], self.sqd[:, 0:n], True, True, [self.Bsqd, self.Bconst], [Bpm])
                    rstd, Brs = self.rstd_from_psum(pm, Bpm, n)
                    self.stt("dve", Oall[:, hd, s0:s0 + n], self.comb[:, 0:n], sg, rstd[:, 0:n], ALU.mult, ALU.mult,
                             [self.Bcomb, Brs, self.Bsmall], [self.BO])

                self.da_pending_tail = tail_b
        if self.da_pending_tail is not None:
            self.da_pending_tail()
            self.da_pending_tail = None
        self.dump("Oall", Oall, [self.BO])
        self.out_proj_residual(b, l, f"daout{j}", Oall, self.BO, ctx_out)

    def proj_rope(self, w, Bw, av, Ba, s0, n, kind, dst, Bdst, rope_name, banks, outs=None):
        pb, Bpb = self.bank("pr" + "".join(str(x) for x in banks), banks)
        for kc in range(8):
            self.mm(pb[:, 0:n], w[:, kc, :], av[:, kc, s0:s0 + n], kc == 0, kc == 7, [Bw, Ba], [Bpb])
        if outs is None:
            outs = [(0, 128, dst, Bdst)]
        if kind == "ctx":
            for (r0, r1, d_, Bd_) in outs:
                self.act(d_, pb[r0:r1, 0:n], AF.Identity, [Bpb], [Bd_])
            return
        ri = self.rot("rope", self.NROPE)
        rt, Brt = self.rope[ri], self.Brope[ri]
        l0 = s0 - CTX
        self.dma("sp", rt[:, 0, 0:n], self.I[rope_name][0, :, l0:l0 + n], [], [Brt], in_barrier=False)
        self.dma("sp", rt[:, 1, 0:n], self.I[rope_name][1, :, l0:l0 + n], [], [Brt], in_barrier=False)
        qf, Bqf = self.ftmp()
        self.act(qf[:, 0:n], pb[:, 0:n], AF.Identity, [Bpb], [Bqf])
        sw, Bsw = self.ftmp()
        for g in range(4):
            src = g ^ 1
            self.cp("dve", sw[g * 32:(g + 1) * 32, 0:n], qf[src * 32:(src + 1) * 32, 0:n], [Bqf], [Bsw])
        self.tt("dve", sw[:, 0:n], sw[:, 0:n], rt[:, 1, 0:n], ALU.mult, [Bsw, Brt], [Bsw])
        self.tt("dve", qf[:, 0:n], qf[:, 0:n], rt[:, 0, 0:n], ALU.mult, [Bqf, Brt], [Bqf])
        for (r0, r1, d_, Bd_) in outs:
            self.tt("dve", d_, qf[r0:r1, 0:n], sw[r0:r1, 0:n], ALU.add, [Bqf, Bsw], [Bd_])


    def mixer_mla(self, b, l, j, ctx_out):
        P, I = self.P, self.I
        P.barrier()
        arA_bf = self.arA_bf
        cqn = arA_bf[:, 0:2 * T].rearrange("p (c t) -> p c t", c=2)
        ckvn = arA_bf[:, 2 * T:3 * T]
        krT = arA_bf[:, 3 * T:4 * T]
        ablk = [arA_bf[:, 4 * T + i * 4096:4 * T + (i + 1) * 4096].rearrange("p (c n) -> p c n", c=NCH) for i in range(2)]
        if not hasattr(self, "Bmla"):
            self.Bmla = {k: P.buf("mla_" + k) for k in ("a0", "a1", "cqn", "ckvn", "kr")}
        if not hasattr(self, "BO"):
            self.BO = P.buf("Oall")
        Bm = self.Bmla
        Oall = self.arB.bitcast(BF16)[:].rearrange("p (c t) -> p c t", c=NCH)
        wd = I["w_mla_down"][j].rearrange("(k p) n -> p k n", p=128)
        qg = self.small[:, 2:4]
        kvg = self.small[:, 4:5]
        hvv = self.hv[:, 0:18 * 128].rearrange("p (t e) -> p t e", e=128)
        self.memset("dve", hvv[:, :, 64:128], 1.0, [self.Bhv])
        self.memset("dve", self.hk[96:128, :], 0.0, [self.Bhk])
        for i4 in range(2):
            self.memset("dve", self.hqs[i4][96:128, :], 0.0, [self.Bhqs[i4]])
        PB6 = [0, 1, 2, 3, 4, 5]
        for blk, (s0, n, kind) in enumerate(BLOCKS):
            par = blk % 2
            a2 = ablk[par]
            Ba2 = Bm["a%d" % par]
            self.prenorm(b, l, 0, blk, lambda c, a2=a2: a2[:, c, :], Ba2)
            slot, Bs = self.sload("mladown", 0)
            sv = slot[:].rearrange("p (k n) -> p k n", k=8)
            outs = []
            for (c0, c1) in [(0, 128), (128, 256), (256, 384), (320, 416), (416, 512)]:
                if c0 == 416 and kind == "ctx":
                    outs.append(None)
                    continue
                pb, Bpb = self.bank("proj6", PB6)
                M = c1 - c0
                for kc in range(8):
                    self.mm(pb[0:M, 0:n], sv[:, kc, c0:c1], a2[:, kc, 0:n], kc == 0, kc == 7, [Bs, Ba2], [Bpb])
                outs.append((pb, Bpb))
            (pA, BpA), (pB, BpB), (pC, BpC), (pD, BpD) = outs[0:4]
            pm, Bpm = self.bank("gen", [6, 7])
            for i_, (pp, Bpp) in enumerate([(pA, BpA), (pB, BpB)]):
                sq, Bsq = self.btmp()
                self.act(sq[:, 0:n], pp[:, 0:n], AF.Square, [Bpp], [Bsq])
                self.mm(pm[:, 0:n], self.ones[256][:], sq[:, 0:n], i_ == 0, i_ == 1, [Bsq, self.Bconst], [Bpm])
            rstd, Br = self.rstd_from_psum(pm, Bpm, n)
            for i_, (pp, Bpp) in enumerate([(pA, BpA), (pB, BpB)]):
                self.stt("dve", cqn[:, i_, s0:s0 + n], pp[:, 0:n], qg[:, i_:i_ + 1], rstd[:, 0:n], ALU.mult, ALU.mult,
                         [Bpp, Br, self.Bconst], [Bm["cqn"]])
            pm, Bpm = self.bank("gen", [6, 7])
            sq, Bsq = self.btmp()
            self.act(sq[:, 0:n], pC[:, 0:n], AF.Square, [BpC], [Bsq])
            self.mm(pm[:, 0:n], self.ones[128][:], sq[:, 0:n], True, True, [Bsq, self.Bconst], [Bpm])
            rstd, Br = self.rstd_from_psum(pm, Bpm, n)
            self.stt("dve", ckvn[:, s0:s0 + n], pC[:, 0:n], kvg, rstd[:, 0:n], ALU.mult, ALU.mult,
                     [BpC, Br, self.Bconst], [Bm["ckvn"]])
            if kind == "ctx":
                self.act(krT[64:96, s0:s0 + n], pD[64:96, 0:n], AF.Identity, [BpD], [Bm["kr"]])
            else:
                pE, BpE = outs[4]
                self.rope_rows(pD, BpD, pE, BpE, s0, n, krT[64:96, s0:s0 + n], Bm["kr"])
        wuq = I["w_mla_uq"][j].rearrange("(k p) n -> p k n", p=128)
        wukv = I["w_mla_ukv"][j]
        for hd in range(16):
            slot, Bs = self.sload("mlahead", hd)
            wqn = slot[:, 0:192].rearrange("p (k n) -> p k n", k=2)
            wqs = slot[:, 192:384].rearrange("p (k n) -> p k n", k=2)
            wkv = slot[:, 384:512]
            for blk, (s0, n, kind) in enumerate(BLOCKS):
                pb, Bpb = self.bank("S", [4, 5, 6, 7])
                self.mm(pb[0:64, 0:n], wkv[:, 0:64], ckvn[:, s0:s0 + n], True, True, [Bs, Bm["ckvn"]], [Bpb])
                self.cp("dve", self.hk[0:64, s0:s0 + n], pb[0:64, 0:n], [Bpb], [self.Bhk])
            self.cp("dve", self.hk[64:96, :], krT[64:96, :], [Bm["kr"]], [self.Bhk])
            for t0 in (0, 8, 16):
                k = min(8, 18 - t0)
                pb, Bpb = self.bank("S", [4, 5, 6, 7])
                for q in range(k):
                    tt_ = t0 + q
                    self.mm(pb[:, q * 64:(q + 1) * 64], ckvn[:, tt_ * 128:(tt_ + 1) * 128], wkv[:, 64:128], True, True,
                            [Bs, Bm["ckvn"]], [Bpb])
                self.cp("dve", hvv[:, t0:t0 + k, 0:64], pb[:, 0:k * 64].rearrange("p (t e) -> p t e", e=64),
                        [Bpb], [self.Bhv])
            qblocks = [k for k in range(len(BLOCKS)) if not (BLOCKS[k][2] == "ctx" and not ctx_out)]

            def qproj(blk):
                s0, n, kind = BLOCKS[blk]
                qi = self.rot("hq", 2)
                hq, Bhq = self.hqs[qi], self.Bhqs[qi]
                pA, BpA = self.ps[2], self.Bps[2]
                for kc in range(2):
                    self.mm(pA[0:96, 0:n], wqn[:, kc, :], cqn[:, kc, s0:s0 + n], kc == 0, kc == 1, [Bs, Bm["cqn"]], [BpA])
                self.cp("dve", hq[0:64, 0:n], pA[0:64, 0:n], [BpA], [Bhq])
                if kind == "ctx":
                    self.cp("dve", hq[64:96, 0:n], pA[64:96, 0:n], [BpA], [Bhq])
                else:
                    pB, BpB = self.ps[3], self.Bps[3]
                    for kc in range(2):
                        self.mm(pB[0:96, 0:n], wqs[:, kc, :], cqn[:, kc, s0:s0 + n], kc == 0, kc == 1, [Bs, Bm["cqn"]], [BpB])
                    self.rope_rows(pA, BpA, pB, BpB, s0, n, hq[64:96, 0:n], Bhq)
                return hq, Bhq

            nxt = qproj(qblocks[0])
            for bi, blk in enumerate(qblocks):
                s0, n, kind = BLOCKS[blk]
                hq, Bhq = nxt
                if bi + 1 < len(qblocks):
                    nxt = qproj(qblocks[bi + 1])
                kts = [0, 1] if kind == "ctx" else list(range(18))
                oi = self.rot("mlaO", 2)
                pO, BpO = self.ps[oi], self.Bps[oi]
                pend = []
                LAG = 2

                def issue_S(kt):
                    pS, BpS = self.bank("S", [4, 5, 6, 7])
                    self.mm(pS[:, 0:n], self.hk[:, kt * 128:(kt + 1) * 128], hq[:, 0:n], True, True,
                            [self.Bhk, Bhq], [BpS])
                    E, BE = self.btmp()
                    self.act(E[:, 0:n], pS[:, 0:n], AF.Exp, [BpS], [BE], scale=MLA_SCALE)
                    return (kt, E, BE)

                def issue_AV(item):
                    kt, E, BE = item
                    self.mm(pO[:, 0:n], hvv[:, kt, :], E[:, 0:n], kt == kts[0], kt == kts[-1], [self.Bhv, BE], [BpO])

                for i, kt in enumerate(kts):
                    pend.append(issue_S(kt))
                    if i >= LAG:
                        issue_AV(pend[i - LAG])
                for i in range(max(0, len(kts) - LAG), len(kts)):
                    issue_AV(pend[i])
                r, Brr = self.ftmp()
                self.recip(r[64:128, 0:n], pO[64:128, 0:n], [BpO], [Brr])
                rb, Brb = self.ftmp()
                self.cp("dve", rb[0:64, 0:n], r[64:128, 0:n], [Brr], [Brb])
                on, Bon = self.ftmp()
                self.tt("dve", on[0:64, 0:n], pO[0:64, 0:n], rb[0:64, 0:n], ALU.mult, [BpO, Brb], [Bon])
                po = (hd % 2) * 64
                self.cp("dve", Oall[po:po + 64, hd // 2, s0:s0 + n], on[0:64, 0:n], [Bon], [self.BO])
        self.out_proj_residual(b, l, "mlaout", Oall, self.BO, ctx_out)

    def rope_rows(self, pA, BpA, pB, BpB, s0, n, dst, Bdst):
        ri = self.rot("rope", self.NROPE)
        rt, Brt = self.rope[ri], self.Brope[ri]
        l0 = s0 - CTX
        self.dma("sp", rt[64:96, 0, 0:n], self.I["rope_mla"][0, 64:96, l0:l0 + n], [], [Brt], in_barrier=False)
        self.dma("sp", rt[64:96, 1, 0:n], self.I["rope_mla"][1, 64:96, l0:l0 + n], [], [Brt], in_barrier=False)
        t1, Bt1 = self.ftmp()
        t2, Bt2 = self.ftmp()
        self.tt("dve", t1[64:96, 0:n], pA[64:96, 0:n], rt[64:96, 0, 0:n], ALU.mult, [BpA, Brt], [Bt1])
        self.tt("dve", t2[64:96, 0:n], pB[64:96, 0:n], rt[64:96, 1, 0:n], ALU.mult, [BpB, Brt], [Bt2])
        self.tt("dve", dst, t1[64:96, 0:n], t2[64:96, 0:n], ALU.add, [Bt1, Bt2], [Bdst])


    def mixer_sc(self, b, l, j, ctx_out):
        P, I = self.P, self.I
        P.barrier()
        av = self.compute_a_all(b, l)
        zT = self.arB.bitcast(BF16)[:].rearrange("p (c t) -> p c t", c=NCH)
        if not hasattr(self, "BO"):
            self.BO = P.buf("Oall")
        wi = I["w_sc_in"][j].rearrange("(k p) n -> p k n", p=128)
        seqs = [(CTX, SEQ)]
        if ctx_out:
            seqs = [(0, CTX), (CTX, SEQ)]
        for c in range(NCH):
            slot, Bs = self.sload("scin", c)
            sv = slot[:, 0:3072].rearrange("p (t k n) -> p t k n", t=3, k=8)
            k0 = self.small[:, 8 + c:9 + c]
            k1 = self.small[:, 16 + c:17 + c]
            k2 = self.small[:, 24 + c:25 + c]
            for (q0, qlen) in seqs:
                t0 = 0
                while t0 < qlen:
                    m = min(510, qlen - t0)
                    lo, hi = max(t0 - 1, 0), min(t0 + m + 1, qlen)
                    N = hi - lo
                    reads_a = [self.Ba[k] for k in range(len(BLOCKS))
                               if BLOCKS[k][0] < q0 + hi and BLOCKS[k][0] + BLOCKS[k][1] > q0 + lo]
                    pbg, Bpbg = self.bank("proj", [0, 1, 2, 3])
                    pcg, Bpcg = self.bank("proj", [0, 1, 2, 3])
                    phh, Bphh = self.bank("proj", [0, 1, 2, 3])
                    for kc in range(8):
                        self.mm(pbg[:, 0:m], sv[:, 0, kc, :], av[:, kc, q0 + t0:q0 + t0 + m], kc == 0, kc == 7, [Bs] + reads_a, [Bpbg])
                    for kc in range(8):
                        self.mm(pcg[:, 0:N], sv[:, 1, kc, :], av[:, kc, q0 + lo:q0 + hi], kc == 0, kc == 7, [Bs] + reads_a, [Bpcg])
                    for kc in range(8):
                        self.mm(phh[:, 0:N], sv[:, 2, kc, :], av[:, kc, q0 + lo:q0 + hi], kc == 0, kc == 7, [Bs] + reads_a, [Bphh])
                    cg, Bcg = self.ftmp()
                    xt, Bxt = self.ftmp()
                    u, Bu = self.ftmp()
                    self.act(cg[:, 0:N], pcg[:, 0:N], AF.Identity, [Bpcg], [Bcg])
                    self.tt("dve", xt[:, 0:N], phh[:, 0:N], cg[:, 0:N], ALU.mult, [Bphh, Bcg], [Bxt])
                    cen = t0 - lo
                    self.ts("dve", u[:, 0:m], xt[:, cen:cen + m], k1, None, ALU.mult, None, [Bxt, self.Bconst], [Bu])
                    ta = max(t0, 1)
                    cnt = t0 + m - ta
                    if cnt > 0:
                        self.stt("dve", u[:, ta - t0:ta - t0 + cnt], xt[:, ta - 1 - lo:ta - 1 - lo + cnt], k0,
                                 u[:, ta - t0:ta - t0 + cnt], ALU.mult, ALU.add, [Bxt, Bu, self.Bconst], [Bu])
                    tb = min(t0 + m - 1, qlen - 2)
                    cnt = tb - t0 + 1
                    if cnt > 0:
                        self.stt("dve", u[:, 0:cnt], xt[:, t0 + 1 - lo:t0 + 1 - lo + cnt], k2,
                                 u[:, 0:cnt], ALU.mult, ALU.add, [Bxt, Bu, self.Bconst], [Bu])
                    self.tt("dve", zT[:, c, q0 + t0:q0 + t0 + m], pbg[:, 0:m], u[:, 0:m], ALU.mult, [Bpbg, Bu], [self.BO])
                    t0 += m
        self.out_proj_residual(b, l, "scout", zT, self.BO, ctx_out)

    def mlp(self, b, l, ctx_out):
        P, I = self.P, self.I
        P.barrier()
        if not hasattr(self, "Ba2"):
            self.Ba2 = [P.buf("a2_0"), P.buf("a2_1")]
            self.Bhid = [P.buf(f"hid{jg}") for jg in range(8)]
            self.Bfst = P.buf("fstage")
        arA_bf = self.arA_bf
        a2v = [arA_bf[:, 0:4096].rearrange("p (c n) -> p c n", c=NCH),
               arA_bf[:, 4096:8192].rearrange("p (c n) -> p c n", c=NCH)]
        fbuf = self.arA[:, 4096:4096 + NCH * 512].rearrange("p (c n) -> p c n", c=NCH)
        hid = self.arB.bitcast(BF16)[:, 0:32 * 512].rearrange("p (j n) -> p j n", j=32)
        w_in = I["w_mlp_in"][l].rearrange("(k p) n -> p k n", p=128)
        w_out = I["w_mlp_out"][l].rearrange("(j p) n -> p j n", p=128)
        blks = [k for k in range(len(BLOCKS)) if not (BLOCKS[k][2] == "ctx" and not ctx_out)]

        def pre(i):
            a2_ = a2v[i % 2]
            self.prenorm(b, l, 1, blks[i], lambda c, a2_=a2_: a2_[:, c, :], self.Ba2[i % 2])

        pre(0)
        pending_post = None
        for bi, blk in enumerate(blks):
            s0, n, kind = BLOCKS[blk]
            par = bi % 2
            a2 = a2v[par]
            for jg in range(8):
                if jg == 1 and pending_post is not None:
                    self.postnorm_residual(b, l, 1, pending_post, fbuf, self.Bfst)
                    pending_post = None
                if jg == 3 and bi + 1 < len(blks):
                    pre(bi + 1)
                slot, Bs = self.sload(f"win{l}", jg)
                sv = slot[:].rearrange("p (k n) -> p k n", k=8)
                for jj in range(4):
                    jx = jg * 4 + jj
                    pb, Bpb = self.bank("proj", [0, 1, 2, 3])
                    for kc in range(8):
                        self.mm(pb[:, 0:n], sv[:, kc, jj * 128:(jj + 1) * 128], a2[:, kc, 0:n], kc == 0, kc == 7,
                                [Bs, self.Ba2[par]], [Bpb])
                    t, Bt = self.ftmp()
                    self.act(t[:, 0:n], pb[:, 0:n], AF.Relu, [Bpb], [Bt])
                    self.tt("dve", hid[:, jx, 0:n], t[:, 0:n], t[:, 0:n], ALU.mult, [Bt], [self.Bhid[jg]])
            for jg in range(8):
                slot, Bs = self.sload(f"wout{l}", jg)
                sv = slot[:].rearrange("p (j n) -> p j n", j=4)
                for fo in range(8):
                    for jj in range(4):
                        self.mm(self.ps[fo][:, 0:n], sv[:, jj, fo * 128:(fo + 1) * 128], hid[:, jg * 4 + jj, 0:n],
                                jg == 0 and jj == 0, jg == 7 and jj == 3, [Bs, self.Bhid[jg]], [self.Bps[fo]])
            for fo in range(8):
                if fo % 2 == 0:
                    self.act(fbuf[:, fo, 0:n], self.ps[fo][:, 0:n], AF.Identity, [self.Bps[fo]], [self.Bfst])
                else:
                    self.cp("dve", fbuf[:, fo, 0:n], self.ps[fo][:, 0:n], [self.Bps[fo]], [self.Bfst])
            pending_post = blk
        self.postnorm_residual(b, l, 1, pending_post, fbuf, self.Bfst)
        P.barrier()


def input_shapes(nb):
    return {
        "x": (nb, SEQ, D), "ctx": (nb, CTX, D), "cvecT": (D, 3), "ident": (128, 128),
        "w_ada": (DEPTH, D, 6 * D), "b_adaT": (DEPTH, 128, 48), "norm_gT": (128, DEPTH * 4 * NCH),
        "w_mlp_in": (DEPTH, D, HID), "w_mlp_out": (DEPTH, HID, D),
        "w_da_qkv": (2, D, 3 * D), "da_lambda": (2, 4, 64), "da_sublnT": (128, 2), "w_da_out": (2, D, D),
        "w_mla_down": (1, D, 416), "mla_q_normT": (128, 2), "w_mla_uq": (1, 256, 1536),
        "mla_kv_normT": (128, 1), "w_mla_ukv": (1, 128, 2048), "w_mla_out": (1, D, D),
        "w_sc_in": (1, D, 3 * D), "sc_convT": (128, 24), "w_sc_out": (1, D, D),
        "rope_da": (2, 128, SEQ), "rope_mla": (2, 128, SEQ),
    }


def rope_tables():
    t = np.arange(SEQ)
    row = (t // GRID_W).astype(np.float32)
    col = (t % GRID_W).astype(np.float32)

    def tab(rot_dim):
        quarter = rot_dim // 4
        inv = (ROPE_THETA ** (-np.arange(quarter, dtype=np.float32) / quarter)).astype(np.float32)
        ang = np.concatenate([row[:, None] * inv, col[:, None] * inv], axis=-1).astype(np.float32)
        return np.cos(ang).T.astype(np.float32), np.sin(ang).T.astype(np.float32)

    c32, s32 = tab(64)
    da = np.zeros((2, 128, SEQ), np.float32)
    for g in range(4):
        da[0, g * 32:(g + 1) * 32] = c32
        da[1, g * 32:(g + 1) * 32] = -s32 if g % 2 == 0 else s32
    c16, s16 = tab(32)
    mla = np.zeros((2, 128, SEQ), np.float32)
    mla[0, 64:80] = c16
    mla[0, 80:96] = c16
    mla[1, 64:80] = -s16
    mla[1, 80:96] = s16
    return da, mla


def make_in_maps(inputs, nb, n_cores):
    f = lambda a: np.ascontiguousarray(np.asarray(a, dtype=np.float32))
    da, mla = rope_tables()
    shared = {
        "ident": np.eye(128, dtype=np.float32),
        "w_ada": f(inputs["w_ada"]),
        "b_adaT": f(np.asarray(inputs["b_ada"]).reshape(DEPTH, 48, 128).transpose(0, 2, 1)),
        "norm_gT": f(np.asarray(inputs["norm_g"]).reshape(DEPTH * 4 * NCH, 128).T),
        "w_mlp_in": f(inputs["w_mlp_in"]), "w_mlp_out": f(inputs["w_mlp_out"]),
        "w_da_qkv": f(inputs["w_da_qkv"]), "da_lambda": f(inputs["da_lambda"]),
        "da_sublnT": f(np.asarray(inputs["da_subln"]).T), "w_da_out": f(inputs["w_da_out"]),
        "w_mla_down": f(inputs["w_mla_down"]),
        "mla_q_normT": f(np.asarray(inputs["mla_q_norm"]).reshape(2, 128).T),
        "w_mla_uq": f(inputs["w_mla_uq"]),
        "mla_kv_normT": f(np.asarray(inputs["mla_kv_norm"]).reshape(1, 128).T),
        "w_mla_ukv": f(inputs["w_mla_ukv"]), "w_mla_out": f(inputs["w_mla_out"]),
        "w_sc_in": f(inputs["w_sc_in"]),
        "sc_convT": f(np.asarray(inputs["sc_conv"]).reshape(3 * NCH, 128).T),
        "w_sc_out": f(inputs["w_sc_out"]),
        "rope_da": da, "rope_mla": mla,
    }
    x, c, ctx, c_ctx = (np.asarray(inputs[k], dtype=np.float32) for k in ("x", "c", "ctx", "c_ctx"))
    maps = []
    for i in range(n_cores):
        sl = slice(i * nb, (i + 1) * nb)
        cols = [c[i * nb + r] for r in range(nb)]
        while len(cols) < 2:
            cols.append(np.zeros(D, np.float32))
        cv = np.stack(cols + [c_ctx], axis=1)
        m = dict(shared)
        m["x"] = f(x[sl])
        m["ctx"] = f(ctx[sl])
        m["cvecT"] = f(cv)
        maps.append(m)
    return maps


_CACHE = {}


def run(inputs, nb, n_cores, n_layers=DEPTH, trace=False, dbg=""):
    key = (nb, n_layers, dbg)
    if key not in _CACHE:
        _CACHE[key] = Builder(nb, n_layers, dbg).build()
    nc = _CACHE[key]
    maps = make_in_maps(inputs, nb, n_cores)
    res = run_bass_kernel_spmd(nc, maps, core_ids=list(range(n_cores)), trace=trace)
    out = np.concatenate([r["y"] for r in res.results], axis=0)
    if dbg:
        return out, res.results[0]
    return out, res


def kernel(**inputs):
    out, _ = run(inputs, nb=2, n_cores=8)
    return out.astype(np.float32)
```

```python
import math
import contextlib
import numpy as np
import concourse.bass as bass
import concourse.mybir as mybir
from concourse.bass_utils import run_bass_kernel_spmd

F32 = mybir.dt.float32
BF16 = mybir.dt.bfloat16
AF = mybir.ActivationFunctionType
ALU = mybir.AluOpType

D = 1024
SEQ = 2048
CTX = 256
T = SEQ + CTX
DEPTH = 4
HID = 4096
NCH = 8
EPS = 1e-6
GRID_W = 64
ROPE_THETA = 10000.0
DA_SCALE = 64 ** -0.5
MLA_SCALE = 96 ** -0.5
ENGS = ("pe", "act", "dve", "pool", "sp")

BLOCKS = [(0, 256, "ctx")] + [(256 + 512 * i, 512, "lat") for i in range(4)]


class Buf:
    __slots__ = ("name", "last_w", "readers", "excl", "sem", "dcount")

    def __init__(self, name, excl=False):
        self.name = name
        self.last_w = None
        self.readers = {}
        self.excl = excl
        self.sem = None
        self.dcount = 0


class Instr:
    __slots__ = ("eng", "fn", "deps", "signal", "val", "is_dma", "dsem", "dval", "small")

    def __init__(self, eng, fn, is_dma=False):
        self.small = False
        self.eng = eng
        self.fn = fn
        self.deps = []
        self.signal = False
        self.val = None
        self.is_dma = is_dma
        self.dsem = None
        self.dval = None


class Prog:
    def __init__(self, nc):
        self.nc = nc
        self.streams = {e: [] for e in ENGS}
        self.bufs = []
        self.pending = {e: [] for e in ENGS}
        self.last = {e: None for e in ENGS}
        self.dmas_since_barrier = []
        self.strict = ()

    def buf(self, name, excl=False):
        b = Buf(name, excl)
        self.bufs.append(b)
        return b

    def _dep(self, ins, other):
        if other is None or other is ins:
            return
        if (not other.is_dma) and (not ins.is_dma) and other.eng == ins.eng:
            if ins.eng == "pe" or not (other.small or (self.strict and ins.eng in self.strict)):
                return
        ins.deps.append(other)
        if not other.is_dma:
            other.signal = True

    def _all_readers(self, b):
        for k, r in b.readers.items():
            if k == "dma":
                for x in r:
                    yield x
            else:
                yield r

    def _track(self, ins, reads, writes):
        for d in self.pending[ins.eng]:
            self._dep(ins, d)
        self.pending[ins.eng] = []
        for b in reads:
            if b.excl:
                self._dep(ins, b.last_w)
                for r in self._all_readers(b):
                    self._dep(ins, r)
                b.readers = {}
                b.last_w = ins
            else:
                self._dep(ins, b.last_w)
                if ins.is_dma:
                    b.readers.setdefault("dma", []).append(ins)
                else:
                    b.readers[ins.eng] = ins
        for b in writes:
            self._dep(ins, b.last_w)
            for r in self._all_readers(b):
                self._dep(ins, r)
            b.readers = {}
            b.last_w = ins

    def op(self, eng, fn, reads=(), writes=(), small=False):
        ins = Instr(eng, fn)
        ins.small = small
        self._track(ins, reads, writes)
        self.streams[eng].append(ins)
        self.last[eng] = ins
        return ins

    def dma(self, queue, fn, reads=(), writes=(), sem_buf=None, in_barrier=True):
        ins = Instr(queue, fn, is_dma=True)
        if sem_buf is None:
            sem_buf = writes[0] if writes else reads[0]
        sem_buf.dcount += 1
        ins.dsem = sem_buf
        ins.dval = 16 * sem_buf.dcount
        self._track(ins, reads, writes)
        self.streams[queue].append(ins)
        if in_barrier:
            self.dmas_since_barrier.append(ins)
        return ins

    def barrier(self):
        deps = [self.last[e] for e in ENGS if self.last[e] is not None]
        deps += self.dmas_since_barrier
        self.dmas_since_barrier = []
        for e in ENGS:
            if e == "pool":
                continue
            self.pending[e] = list(self.pending[e]) + deps

    def emit(self, final_waits=()):
        nc = self.nc
        with contextlib.ExitStack() as es:
            esem = {e: es.enter_context(nc.semaphore("s_" + e)) for e in ENGS}
            for b in self.bufs:
                if b.dcount > 0:
                    b.sem = es.enter_context(nc.semaphore("d_" + b.name))
            for e in ENGS:
                c = 0
                for ins in self.streams[e]:
                    if (not ins.is_dma) and ins.signal:
                        c += 1
                        ins.val = c
            block = es.enter_context(nc.Block())

            def run(ename, e):
                waited = {}
                for ins in self.streams[ename]:
                    for d in ins.deps:
                        if d.is_dma:
                            sem, val = d.dsem.sem, d.dval
                        else:
                            sem, val = esem[d.eng], d.val
                        key = sem.num
                        if waited.get(key, 0) >= val:
                            continue
                        waited[key] = val
                        e.wait_ge(sem, val)
                    r = ins.fn(e)
                    if ins.is_dma:
                        r.then_inc(ins.dsem.sem, 16)
                    elif ins.signal:
                        r.then_inc(esem[ename], 1)
                if ename == "sp":
                    for d in final_waits:
                        e.wait_ge(d.dsem.sem, d.dval)

            @block.tensor
            def _(e):
                run("pe", e)

            @block.scalar
            def _(e):
                run("act", e)

            @block.vector
            def _(e):
                run("dve", e)

            @block.gpsimd
            def _(e):
                run("pool", e)

            @block.sync
            def _(e):
                run("sp", e)


class Builder:
    def __init__(self, nb, n_layers, dbg=""):
        self.dbg = dbg
        self.nb = nb
        self.n_layers = n_layers
        self.nc = bass.Bass("TRN2", target_bir_lowering=False)
        self.P = Prog(self.nc)
        if "strict" in dbg:
            self.P.strict = ("dve", "act", "pool")
        self.es = contextlib.ExitStack()
        self.out_dmas = []
        self.conv_gate = None
        self.rr = {}
        self.scr = {}

    @staticmethod
    def _small(ap):
        n = 1
        for d in ap.shape[1:]:
            n *= int(d)
        return n < 512

    def mm(self, out, lhsT, rhs, start, stop, rd, wr):
        self.P.op("pe", lambda e: e.matmul(out, lhsT=lhsT, rhs=rhs, start=start, stop=stop), reads=rd, writes=wr)

    def tr(self, out, in_, rd, wr):
        ident = self.ident
        self.P.op("pe", lambda e: e.transpose(out, in_, ident[:]), reads=list(rd) + [self.Bconst], writes=wr)

    def act(self, out, in_, func, rd, wr, scale=1.0, bias=None):
        if bias is None:
            self.P.op("act", lambda e: e.activation(out=out, in_=in_, func=func, scale=scale), reads=rd, writes=wr, small=self._small(out))
        else:
            self.P.op("act", lambda e: e.activation(out=out, in_=in_, func=func, scale=scale, bias=bias), reads=rd, writes=wr, small=self._small(out))

    def tt(self, eng, out, in0, in1, op, rd, wr):
        self.P.op(eng, lambda e: e.tensor_tensor(out=out, in0=in0, in1=in1, op=op), reads=rd, writes=wr, small=self._small(out))

    def stt(self, eng, out, in0, scalar, in1, op0, op1, rd, wr):
        self.P.op(eng, lambda e: e.scalar_tensor_tensor(out=out, in0=in0, scalar=scalar, in1=in1, op0=op0, op1=op1), reads=rd, writes=wr, small=self._small(out))

    def ts(self, eng, out, in0, s1, s2, op0, op1, rd, wr):
        if s2 is None:
            self.P.op(eng, lambda e: e.tensor_scalar(out=out, in0=in0, scalar1=s1, scalar2=None, op0=op0), reads=rd, writes=wr, small=self._small(out))
        else:
            self.P.op(eng, lambda e: e.tensor_scalar(out=out, in0=in0, scalar1=s1, scalar2=s2, op0=op0, op1=op1), reads=rd, writes=wr, small=self._small(out))

    def cp(self, eng, out, in_, rd, wr):
        self.P.op(eng, lambda e: e.tensor_copy(out=out, in_=in_), reads=rd, writes=wr, small=self._small(out))

    def memset(self, eng, ap, val, wr):
        self.P.op(eng, lambda e: e.memset(ap, val), writes=wr, small=True)

    def recip(self, out, in_, rd, wr):
        self.P.op("dve", lambda e: e.reciprocal(out=out, in_=in_), reads=rd, writes=wr, small=self._small(out))

    def dma(self, queue, out, in_, rd, wr, sem_buf=None, in_barrier=True):
        return self.P.dma(queue, lambda e: e.dma_start(out=out, in_=in_), reads=rd, writes=wr, sem_buf=sem_buf,
                          in_barrier=in_barrier)

    def dump(self, tag, ap, bufs, dt=None):
        if "dump" not in self.dbg:
            return
        if not hasattr(self, "dumps"):
            self.dumps = {}
        if tag in self.dumps:
            return
        dt = dt or ap.dtype
        d = self.nc.dram_tensor("dbg_" + tag, list(ap.shape), dt, kind="ExternalOutput").ap()
        self.dumps[tag] = d
        x = self.dma("sp", d, ap, list(bufs), [], sem_buf=bufs[0])
        self.out_dmas.append(x)

    def rot(self, key, n):
        v = self.rr.get(key, 0)
        self.rr[key] = v + 1
        return v % n

    def sb(self, name, shape, dt):
        return self.es.enter_context(self.nc.sbuf_tensor("sb_" + name, shape, dt))

    def dram_in(self, name, shape):
        return self.nc.dram_tensor(name, list(shape), F32, kind="ExternalInput").ap()

    def build(self):
        nc, P = self.nc, self.P
        nb = self.nb
        I = {}
        for name, shape in input_shapes(nb).items():
            I[name] = self.dram_in(name, shape)
        self.I = I
        self.y = nc.dram_tensor("y", [nb, SEQ, D], F32, kind="ExternalOutput").ap()

        self.h = self.sb("h", [128, NCH, T], F32)
        self.Bh = [[P.buf(f"h{c}_{k}") for k in range(len(BLOCKS))] for c in range(NCH)]
        self.arA = self.sb("arA", [128, NCH * T // 2], F32)
        self.arA_bf = self.arA.bitcast(BF16) if hasattr(self.arA, "bitcast") else None
        self.arB = self.sb("arB", [128, NCH * T // 2], F32)
        self.NSLOT = 3
        self.ring = [self.sb(f"ring{i}", [128, 4096], BF16) for i in range(self.NSLOT)]
        self.Bring = [P.buf(f"ring{i}") for i in range(self.NSLOT)]
        self.Bringsw = [P.buf(f"ringsw{i}") for i in range(self.NSLOT)]
        self.hk = self.sb("hk", [128, T], BF16)
        self.hv = self.sb("hv", [128, 18 * 128], BF16)
        self.hqs = [self.sb(f"hq{i}", [128, 512], BF16) for i in range(4)]
        self.Bhk, self.Bhv = P.buf("hk"), P.buf("hv")
        self.Bhqs = [P.buf(f"hq{i}") for i in range(4)]
        self.NROPE = 1
        self.rope = [self.sb(f"rope{i}", [128, 2, 512], F32) for i in range(self.NROPE)]
        self.Brope = [P.buf(f"rope{i}") for i in range(self.NROPE)]
        self.NBT = 4
        self.bt = [self.sb(f"bt{i}", [128, 512], BF16) for i in range(self.NBT)]
        self.Bbt = [P.buf(f"bt{i}") for i in range(self.NBT)]
        self.NFT = 3
        self.ft = [self.sb(f"ft{i}", [128, 512], F32) for i in range(self.NFT)]
        self.Bft = [P.buf(f"ft{i}") for i in range(self.NFT)]
        self.rs = self.sb("rs", [128, 512], F32)
        self.Brs = P.buf("rs")
        self.comb = self.sb("comb", [128, 512], F32)
        self.Bcomb = P.buf("comb")
        self.sqd = self.sb("sqd", [128, 512], BF16)
        self.Bsqd = P.buf("sqd")
        self.ident = self.sb("ident", [128, 128], F32)
        self.ones = {k: self.sb(f"ones{k}", [128, 128], BF16) for k in (1024, 256, 128, 1)}
        self.epsc = self.sb("epsc", [128, 1], F32)
        self.onesf = self.sb("onesf", [128, 64], F32)
        self.scT = self.sb("scT", [128, NCH, 3], BF16)
        self.cv = self.sb("cv", [128, NCH, 3], F32)
        self.mvec = self.sb("mvec", [128, DEPTH, 48, 3], F32)
        self.badaT = self.sb("badaT", [128, DEPTH, 48], F32)
        self.ngT = self.sb("ngT", [128, DEPTH, 4, NCH], F32)
        self.der = self.sb("der", [128, DEPTH, 4, NCH, 3], F32)
        self.small = self.sb("small", [128, 64], F32)
        self.Bconst = P.buf("consts")
        self.Bmvec_l = [P.buf(f"mvec{l}") for l in range(DEPTH)]
        self.Bder_l = [P.buf(f"der{l}") for l in range(DEPTH)]
        self.Bsmall = P.buf("small")
        self.Bgate = P.buf("gate")
        self.ps = [self.es.enter_context(nc.psum_tensor(f"ps{i}", [128, 512], F32)) for i in range(8)]
        self.Bps = [P.buf(f"ps{i}", excl=True) for i in range(8)]

        self.prologue()
        for b in range(nb):
            self.load_inputs(b)
            for l in range(self.n_layers):
                self.layer(b, l)
            self.store_output(b)
        P.emit(final_waits=self.out_dmas)
        return nc

    def wload(self, src_ap, view):
        s = self.rot("ring", self.NSLOT)
        dst = view(self.ring[s])
        self.dma("pool", dst, src_ap, rd=[], wr=[self.Bring[s]], sem_buf=self.Bringsw[s], in_barrier=False)
        return self.ring[s], self.Bring[s]

    def scratch_tensor(self, key, n_slots, width):
        t = self.nc.dram_tensor("ws_" + key, [n_slots, 128, width], BF16).ap()
        self.scr[key] = (t, self.P.buf("ws_" + key))
        return t, self.scr[key][1]

    def conv(self, dst, src, Bdst):
        gate = []
        if self.conv_gate is not None:
            gate, self.conv_gate = [self.conv_gate], None
        self.dma("pool", dst, src, gate, [], sem_buf=Bdst, in_barrier=False)
        Bdst.last_w = self.P.streams["pool"][-1]

    def sload(self, key, idx, width=None):
        t, Bt = self.scr[key]
        s = self.rot("ring", self.NSLOT)
        w = t.shape[2] if width is None else width
        self.dma("sp", self.ring[s][:, 0:w], t[idx, :, 0:w], [Bt], [self.Bring[s]], in_barrier=False)
        return self.ring[s], self.Bring[s]

    def prep_weights(self, l):
        I = self.I
        kind, j = l % 3, l // 3
        if "force_kind=" in self.dbg:
            kind, j = int(self.dbg.split("force_kind=")[1][0]), 0
        def halves(key, w_ap):
            t, Bt = self.scratch_tensor(key, 2, 4096)
            wv = w_ap.rearrange("(k p) n -> p k n", p=128)
            for half in range(2):
                self.conv(t[half].rearrange("p (k n) -> p k n", k=8), wv[:, :, half * 512:(half + 1) * 512], Bt)
        if kind == 0:
            t, Bt = self.scratch_tensor(f"daqkv{j}", 8, 3072)
            wq = I["w_da_qkv"][j].rearrange("(k p) n -> p k n", p=128)
            for hd in range(8):
                tv = t[hd].rearrange("p (t k n) -> p t k n", t=3, k=8)
                for t3 in range(3):
                    self.conv(tv[:, t3], wq[:, :, t3 * 1024 + hd * 128: t3 * 1024 + (hd + 1) * 128], Bt)
            halves(f"daout{j}", I["w_da_out"][j])
        elif kind == 1:
            t, Bt = self.scratch_tensor("mladown", 1, 4096)
            wd = I["w_mla_down"][j].rearrange("(k p) n -> p k n", p=128)
            tv = t[0].rearrange("p (k n) -> p k n", k=8)
            self.conv(tv[:, :, 0:416], wd, Bt)
            self.conv(tv[:, :, 480:496], wd[:, :, 400:416], Bt)
            self.conv(tv[:, :, 496:512], wd[:, :, 384:400], Bt)
            t, Bt = self.scratch_tensor("mlahead", 16, 512)
            wuq = I["w_mla_uq"][j].rearrange("(k p) n -> p k n", p=128)
            wukv = I["w_mla_ukv"][j]
            for hd in range(16):
                wqn = t[hd][:, 0:192].rearrange("p (k n) -> p k n", k=2)
                wqs = t[hd][:, 192:384].rearrange("p (k n) -> p k n", k=2)
                self.conv(wqn, wuq[:, :, hd * 96:(hd + 1) * 96], Bt)
                self.conv(wqs[:, :, 64:80], wuq[:, :, hd * 96 + 80:hd * 96 + 96], Bt)
                self.conv(wqs[:, :, 80:96], wuq[:, :, hd * 96 + 64:hd * 96 + 80], Bt)
                self.conv(t[hd][:, 384:512], wukv[:, hd * 128:(hd + 1) * 128], Bt)
            halves("mlaout", I["w_mla_out"][j])
        else:
            t, Bt = self.scratch_tensor("scin", 8, 3072)
            wi = I["w_sc_in"][j].rearrange("(k p) n -> p k n", p=128)
            for c in range(8):
                tv = t[c].rearrange("p (t k n) -> p t k n", t=3, k=8)
                for t3 in range(3):
                    self.conv(tv[:, t3], wi[:, :, t3 * 1024 + c * 128: t3 * 1024 + (c + 1) * 128], Bt)
            halves("scout", I["w_sc_out"][j])
        if l + 1 < self.n_layers:
            t, Bt = self.scratch_tensor(f"ada{l + 1}", 12, 4096)
            wa = I["w_ada"][l + 1].rearrange("(k p) n -> p k n", p=128)
            for g in range(12):
                self.conv(t[g].rearrange("p (k n) -> p k n", k=8), wa[:, :, g * 512:(g + 1) * 512], Bt)
        t, Bt = self.scratch_tensor(f"win{l}", 8, 4096)
        w_in = I["w_mlp_in"][l].rearrange("(k p) n -> p k n", p=128)
        for jg in range(8):
            self.conv(t[jg].rearrange("p (k n) -> p k n", k=8), w_in[:, :, jg * 512:(jg + 1) * 512], Bt)
        t, Bt = self.scratch_tensor(f"wout{l}", 8, 4096)
        w_out = I["w_mlp_out"][l].rearrange("(j p) n -> p j n", p=128)
        for jg in range(8):
            self.conv(t[jg].rearrange("p (j n) -> p j n", j=4), w_out[:, jg * 4:(jg + 1) * 4, :], Bt)

    def ftmp(self):
        i = self.rot("ft", self.NFT)
        return self.ft[i], self.Bft[i]

    def btmp(self):
        i = self.rot("bt", self.NBT)
        return self.bt[i], self.Bbt[i]

    def bank(self, key, banks):
        i = banks[self.rot(key, len(banks))]
        return self.ps[i], self.Bps[i]

    def prologue(self):
        P, I = self.P, self.I
        Bc = self.Bconst
        self.dma("sp", self.ident[:], I["ident"], [], [Bc])
        self.dma("sp", self.cv[:], I["cvecT"].rearrange("(k p) r -> p k r", p=128), [], [Bc])
        self.dma("sp", self.badaT[:], I["b_adaT"].rearrange("l p c -> p l c"), [], [Bc])
        self.dma("sp", self.ngT[:], I["norm_gT"].rearrange("p (l s c) -> p l s c", l=DEPTH, s=4), [], [Bc])
        self.dma("sp", self.small[:, 0:2], I["da_sublnT"], [], [Bc])
        self.dma("sp", self.small[:, 2:4], I["mla_q_normT"], [], [Bc])
        self.dma("sp", self.small[:, 4:5], I["mla_kv_normT"], [], [Bc])
        self.dma("sp", self.small[:, 8:32], I["sc_convT"], [], [Bc])
        lam_src = bass.AP(I["da_lambda"].tensor, 0, [[0, 128], [1, 512]])
        self.dma("sp", self.rope[0][:, 0, :], lam_src, [], [self.Brope[0]])
        for k, t in self.ones.items():
            self.memset("pool", t[:], 1.0 / k, [Bc])
        self.memset("pool", self.epsc[:], EPS, [Bc])
        self.memset("pool", self.onesf[:], 1.0, [Bc])
        self.act(self.scT[:], self.cv[:], AF.Silu, [Bc], [Bc])
        self.adaln(0)
        self.prep_weights(0)
        for j in range(2):
            li = 3 * j
            if li >= self.n_layers:
                break
            lam_init = 0.8 - 0.6 * math.exp(-0.3 * li)
            lv = self.rope[0][:, 0, j * 256:(j + 1) * 256].rearrange("p (a d) -> p a d", a=4)
            t, Bt = self.ftmp()
            self.tt("dve", t[:, 0:64], lv[:, 0], lv[:, 1], ALU.mult, [self.Brope[0]], [Bt])
            self.tt("dve", t[:, 64:128], lv[:, 2], lv[:, 3], ALU.mult, [self.Brope[0]], [Bt])
            self.P.op("dve", lambda e, t=t: e.reduce_sum(out=t[:, 128:130], in_=t[:, 0:128].rearrange("p (a d) -> p a d", a=2),
                                                       axis=mybir.AxisListType.X), reads=[Bt], writes=[Bt], small=True)
            self.act(t[:, 130:132], t[:, 128:130], AF.Exp, [Bt], [Bt])
            self.stt("dve", self.small[:, 32 + j:33 + j], t[:, 131:132], -lam_init, t[:, 130:131], ALU.add, ALU.subtract,
                     [Bt], [self.Bsmall])
            self.ts("dve", self.small[:, 34 + j:35 + j], self.small[:, j:j + 1], 1.0 - lam_init, None, ALU.mult, None,
                    [Bc], [self.Bsmall])

    def adaln(self, l):
        I, Bc = self.I, self.Bconst
        if True:
            pm, Bpm = self.bank("gen", [6, 7])
            for g in range(12):
                if l == 0:
                    src = I["w_ada"][l].rearrange("(k p) n -> p k n", p=128)[:, :, g * 512:(g + 1) * 512]
                    slot, Bs = self.wload(src, lambda t: t[:].rearrange("p (k n) -> p k n", k=8))
                else:
                    slot, Bs = self.sload(f"ada{l}", g)
                sv = slot[:].rearrange("p (k n) -> p k n", k=8)
                for cc in range(4):
                    ch = g * 4 + cc
                    for kc in range(8):
                        self.mm(pm[:, ch * 3:ch * 3 + 3], sv[:, kc, cc * 128:(cc + 1) * 128], self.scT[:, kc, :],
                                kc == 0, kc == 7, [Bs, Bc], [Bpm])
            self.tt("dve", self.mvec[:, l], pm[:, 0:144].rearrange("p (c r) -> p c r", r=3),
                    self.badaT[:, l].unsqueeze(2).to_broadcast([128, 48, 3]), ALU.add, [Bpm, Bc], [self.Bmvec_l[l]])
            mv = self.mvec[:, l].rearrange("p (n c) r -> p n c r", n=6)
            for idx, (mi, gi, plus1) in enumerate([(1, 0, True), (2, 1, False), (4, 2, True), (5, 3, False)]):
                gb = self.ngT[:, l, gi].unsqueeze(2).to_broadcast([128, NCH, 3])
                dst = self.der[:, l, idx]
                if plus1:
                    self.stt("dve", dst, mv[:, mi], 1.0, gb, ALU.add, ALU.mult, [self.Bmvec_l[l], Bc], [self.Bder_l[l]])
                else:
                    self.tt("dve", dst, mv[:, mi], gb, ALU.mult, [self.Bmvec_l[l], Bc], [self.Bder_l[l]])

    def load_inputs(self, b):
        P, I = self.P, self.I
        P.barrier()
        stage = [self.arA[:, 0:1024], self.arA[:, 1024:2048]]
        Bst = getattr(self, "Bstage_in", None)
        if Bst is None:
            Bst = self.Bstage_in = [P.buf("stin0"), P.buf("stin1")]
        for tt_ in range(T // 128):
            s = tt_ % 2
            if tt_ < 2:
                src = I["ctx"][b, tt_ * 128:(tt_ + 1) * 128, :]
            else:
                src = I["x"][b, (tt_ - 2) * 128:(tt_ - 1) * 128, :]
            self.dma("sp", stage[s], src, [], [Bst[s]])
            blk = self.tok2blk(tt_ * 128)
            for half in range(2):
                pb, Bpb = self.bank("gen", [6, 7])
                for q in range(4):
                    c = half * 4 + q
                    self.tr(pb[:, q * 128:(q + 1) * 128], stage[s][:, c * 128:(c + 1) * 128], [Bst[s]], [Bpb])
                for q in range(4):
                    c = half * 4 + q
                    eng = "act" if q % 2 == 0 else "dve"
                    if eng == "act":
                        self.act(self.h[:, c, tt_ * 128:(tt_ + 1) * 128], pb[:, q * 128:(q + 1) * 128], AF.Identity,
                                 [Bpb], [self.Bh[c][blk]])
                    else:
                        self.cp("dve", self.h[:, c, tt_ * 128:(tt_ + 1) * 128], pb[:, q * 128:(q + 1) * 128],
                                [Bpb], [self.Bh[c][blk]])
        P.barrier()

    def tok2blk(self, tok):
        for i, (s, n, _) in enumerate(BLOCKS):
            if s <= tok < s + n:
                return i
        raise ValueError

    def store_output(self, b):
        P = self.P
        P.barrier()
        stage = [self.arA[:, 0:1024], self.arA[:, 1024:2048]]
        Bst = getattr(self, "Bstage_out", None)
        if Bst is None:
            Bst = self.Bstage_out = [P.buf("stout0"), P.buf("stout1")]
        for tt_ in range(2, T // 128):
            s = tt_ % 2
            blk = self.tok2blk(tt_ * 128)
            for half in range(2):
                pb, Bpb = self.bank("gen", [6, 7])
                for q in range(4):
                    c = half * 4 + q
                    self.tr(pb[:, q * 128:(q + 1) * 128], self.h[:, c, tt_ * 128:(tt_ + 1) * 128], [self.Bh[c][blk]], [Bpb])
                if half == 0:
                    self.act(stage[s][:, 0:512], pb[:], AF.Identity, [Bpb], [Bst[s]])
                else:
                    self.cp("dve", stage[s][:, 512:1024], pb[:], [Bpb], [Bst[s]])
            d = self.dma("sp", self.y[b, (tt_ - 2) * 128:(tt_ - 1) * 128, :], stage[s], [Bst[s]], [], sem_buf=Bst[s])
            self.out_dmas.append(d)
        P.barrier()

    def mrow(self, b, kind):
        return 2 if kind == "ctx" else b

    def rstd_from_psum(self, pm, Bpm, n):
        r, Br = self.rs, self.Brs
        self.act(r[:, 0:n], pm[:, 0:n], AF.Ln, [Bpm, self.Bconst], [Br], bias=self.epsc[:, 0:1])
        self.act(r[:, 0:n], r[:, 0:n], AF.Exp, [Br], [Br], scale=-0.5)
        return r, Br

    def prenorm(self, b, l, site, blk, dst_fn, Bdst):
        s0, n, kind = BLOCKS[blk]
        r_ = self.mrow(b, kind)
        pm, Bpm = self.bank("gen", [6, 7])
        for c in range(NCH):
            sq, Bsq = self.btmp()
            self.act(sq[:, 0:n], self.h[:, c, s0:s0 + n], AF.Square, [self.Bh[c][blk]], [Bsq])
            self.mm(pm[:, 0:n], self.ones[1024][:], sq[:, 0:n], c == 0, c == NCH - 1, [Bsq, self.Bconst], [Bpm])
        rstd, Br = self.rstd_from_psum(pm, Bpm, n)
        aidx = 0 if site == 0 else 2
        midx = 0 if site == 0 else 3
        for c in range(NCH):
            t, Bt = self.ftmp()
            self.stt("dve", t[:, 0:n], self.h[:, c, s0:s0 + n], self.der[:, l, aidx, c, r_:r_ + 1], rstd[:, 0:n],
                     ALU.mult, ALU.mult, [self.Bh[c][blk], self.Bder_l[l], Br], [Bt])
            self.act(dst_fn(c)[:, 0:n], t[:, 0:n], AF.Identity, [Bt, self.Bmvec_l[l]], [Bdst],
                     bias=self.mvec[:, l, midx * 8 + c, r_:r_ + 1])

    def postnorm_residual(self, b, l, site, blk, ybuf, By):
        s0, n, kind = BLOCKS[blk]
        r_ = self.mrow(b, kind)
        pm, Bpm = self.bank("gen", [6, 7])
        for c in range(NCH):
            sq, Bsq = self.btmp()
            self.act(sq[:, 0:n], ybuf[:, c, 0:n], AF.Square, [By], [Bsq])
            self.mm(pm[:, 0:n], self.ones[1024][:], sq[:, 0:n], c == 0, c == NCH - 1, [Bsq, self.Bconst], [Bpm])
        rstd, Br = self.rstd_from_psum(pm, Bpm, n)
        gidx = 1 if site == 0 else 3
        for c in range(NCH):
            t, Bt = self.ftmp()
            self.stt("dve", t[:, 0:n], ybuf[:, c, 0:n], self.der[:, l, gidx, c, r_:r_ + 1], rstd[:, 0:n],
                     ALU.mult, ALU.mult, [By, self.Bder_l[l], Br], [Bt])
            self.tt("dve", self.h[:, c, s0:s0 + n], self.h[:, c, s0:s0 + n], t[:, 0:n], ALU.add,
                    [Bt, self.Bh[c][blk]], [self.Bh[c][blk]])

    def layer(self, b, l):
        kind, j = l % 3, l // 3
        ctx_out = l < DEPTH - 1
        if b == 0 and l + 1 < self.n_layers:
            self.memset("dve", self.small[:, 63:64], 0.0, [self.Bgate])
        if "force_kind=" in self.dbg:
            kind, j = int(self.dbg.split("force_kind=")[1][0]), 0
        if kind == 0:
            self.mixer_da(b, l, j, ctx_out)
        elif kind == 1:
            self.mixer_mla(b, l, j, ctx_out)
        else:
            self.mixer_sc(b, l, j, ctx_out)
        if "nomlp" in self.dbg and l == self.n_layers - 1:
            return
        if b == 0 and l + 1 < self.n_layers:
            self.adaln(l + 1)
            self.conv_gate = self.Bgate
            self.prep_weights(l + 1)
        self.mlp(b, l, ctx_out)

    def a_view(self):
        return self.arA_bf[:].rearrange("p (c t) -> p c t", c=NCH)

    def compute_a_all(self, b, l):
        P = self.P
        if not hasattr(self, "Ba"):
            self.Ba = [P.buf(f"a_{k}") for k in range(len(BLOCKS))]
        av = self.a_view()
        for blk, (s0, n, kind) in enumerate(BLOCKS):
            self.prenorm(b, l, 0, blk, lambda c, s0=s0, n=n: av[:, c, s0:s0 + n], self.Ba[blk])
        return av

    def out_proj_residual(self, b, l, wkey, Oall, BO, ctx_out, kchunks=NCH):
        P = self.P
        P.barrier()
        if not hasattr(self, "Bystage"):
            self.Bystage = [P.buf("ystage0"), P.buf("ystage1")]
        ybufs = [self.arA[:, i * NCH * 512:(i + 1) * NCH * 512].rearrange("p (c n) -> p c n", c=NCH) for i in range(2)]
        blks = [k for k in range(len(BLOCKS)) if not (BLOCKS[k][2] == "ctx" and not ctx_out)]

        def proj(bi):
            s0, n, kind = BLOCKS[blks[bi]]
            ybuf, By = ybufs[bi % 2], self.Bystage[bi % 2]
            for half in range(2):
                slot, Bs = self.sload(wkey, half)
                sv = slot[:, 0:kchunks * 512].rearrange("p (k n) -> p k n", k=kchunks)
                for q in range(4):
                    fo = half * 4 + q
                    pb, Bpb = self.bank("proj", [0, 1, 2, 3])
                    for k in range(kchunks):
                        self.mm(pb[:, 0:n], sv[:, k, q * 128:(q + 1) * 128], Oall[:, k, s0:s0 + n],
                                k == 0, k == kchunks - 1, [Bs, BO], [Bpb])
                    if q % 2 == 0:
                        self.act(ybuf[:, fo, 0:n], pb[:, 0:n], AF.Identity, [Bpb], [By])
                    else:
                        self.cp("dve", ybuf[:, fo, 0:n], pb[:, 0:n], [Bpb], [By])

        proj(0)
        for bi, blk in enumerate(blks):
            if bi + 1 < len(blks):
                proj(bi + 1)
            self.postnorm_residual(b, l, 0, blk, ybufs[bi % 2], self.Bystage[bi % 2])
        P.barrier()

    def mixer_da(self, b, l, j, ctx_out):
        P, I = self.P, self.I
        P.barrier()
        av = self.compute_a_all(b, l)
        Ba_all = self.Ba
        self.dump("a", av, Ba_all)
        self.dump("mvec", self.mvec[:, l], [self.Bmvec_l[l]])
        self.dump("der", self.der[:, l], [self.Bder_l[l]])
        self.dump("small", self.small[:], [self.Bsmall])
        Oall = self.arB.bitcast(BF16)[:].rearrange("p (c t) -> p c t", c=NCH)
        if not hasattr(self, "BO"):
            self.BO = P.buf("Oall")
        wq = I["w_da_qkv"][j].rearrange("(k p) n -> p k n", p=128)
        nlam = self.small[:, 32 + j:33 + j]
        sg = self.small[:, 34 + j:35 + j]
        self.da_pending_tail = None
        for i4 in range(4):
            m_ = i4 % 2
            self.memset("dve", self.hqs[i4][(1 - m_) * 64:(2 - m_) * 64, :], 0.0, [self.Bhqs[i4]])
        for hd in range(8):
            slot, Bs = self.sload(f"daqkv{j}", hd)
            sv = slot[:, 0:3072].rearrange("p (t k n) -> p t k n", t=3, k=8)
            for blk, (s0, n, kind) in enumerate(BLOCKS):
                self.proj_rope(sv[:, 1], Bs, av, Ba_all[blk], s0, n, kind, self.hk[:, s0:s0 + n], self.Bhk, "rope_da", [4, 5, 6, 7])
            for tt_ in range(T // 128):
                blk = self.tok2blk(tt_ * 128)
                if tt_ % 4 == 0:
                    pb, Bpb = self.bank("pr4567", [4, 5, 6, 7])
                q4 = tt_ % 4
                for kc in range(8):
                    self.mm(pb[:, q4 * 128:(q4 + 1) * 128], av[:, kc, tt_ * 128:(tt_ + 1) * 128], sv[:, 2, kc, :],
                            kc == 0, kc == 7, [Bs, Ba_all[blk]], [Bpb])
                if q4 == 3 or tt_ == T // 128 - 1:
                    t0 = tt_ - q4
                    self.act(self.hv[:, t0 * 128:(tt_ + 1) * 128], pb[:, 0:(q4 + 1) * 128], AF.Identity, [Bpb], [self.Bhv])
            self.dump("hk", self.hk[:], [self.Bhk])
            self.dump("hv", self.hv[:], [self.Bhv])
            qblocks = [k for k in range(len(BLOCKS)) if not (BLOCKS[k][2] == "ctx" and not ctx_out)]

            def qproj(blk):
                s0, n, kind = BLOCKS[blk]
                qi = self.rot("hq", 2)
                hq2 = [self.hqs[2 * qi], self.hqs[2 * qi + 1]]
                Bhq2 = [self.Bhqs[2 * qi], self.Bhqs[2 * qi + 1]]
                self.proj_rope(sv[:, 0], Bs, av, Ba_all[blk], s0, n, kind, None, None, "rope_da", [7],
                               outs=[(0, 64, hq2[0][0:64, 0:n], Bhq2[0]), (64, 128, hq2[1][64:128, 0:n], Bhq2[1])])
                return hq2, Bhq2

            nxt = qproj(qblocks[0])
            for bi, blk in enumerate(qblocks):
                s0, n, kind = BLOCKS[blk]
                hq2, Bhq2 = nxt
                if bi + 1 < len(qblocks):
                    nxt = qproj(qblocks[bi + 1])
                kts = [0, 1] if kind == "ctx" else list(range(18))
                steps = [(kt, m) for kt in kts for m in range(2)]
                Sb = [4, 5, 6]
                pend = []

                def issue_S(step):
                    kt, m = step
                    pS, BpS = self.bank("S3", Sb)
                    self.mm(pS[:, 0:n], self.hk[:, kt * 128:(kt + 1) * 128],
                            hq2[m][:, 0:n], True, True, [self.Bhk, Bhq2[m]], [BpS])
                    E, BE = self.btmp()
                    self.act(E[:, 0:n], pS[:, 0:n], AF.Exp, [BpS], [BE], scale=DA_SCALE)
                    return (kt, m, E, BE)

                def issue_AV(item, first, last):
                    kt, m, E, BE = item
                    self.mm(self.ps[m][:, 0:n], self.hv[:, kt * 128:(kt + 1) * 128], E[:, 0:n], first, last,
                            [self.Bhv, BE], [self.Bps[m]])
                    self.mm(self.ps[2 + m][:, 0:n], self.ones[1][:], E[:, 0:n], first, last,
                            [self.Bconst, BE], [self.Bps[2 + m]])

                LAG = 2
                for i, st in enumerate(steps):
                    pend.append(issue_S(st))
                    if i >= LAG:
                        it = pend[i - LAG]
                        issue_AV(it, it[0] == kts[0], it[0] == kts[-1])
                    if i == min(10, len(steps) - 1) and self.da_pending_tail is not None:
                        self.da_pending_tail()
                        self.da_pending_tail = None
                for i in range(max(0, len(steps) - LAG), len(steps)):
                    it = pend[i]
                    issue_AV(it, it[0] == kts[0], it[0] == kts[-1])
                r0, Br0 = self.ftmp()
                r1, Br1 = self.ftmp()
                self.act(r0[:, 0:n], self.ps[2][:, 0:n], AF.Ln, [self.Bps[2]], [Br0])
                self.act(r1[:, 0:n], self.ps[3][:, 0:n], AF.Ln, [self.Bps[3]], [Br1])
                self.cp("dve", self.comb[:, 0:n], self.ps[0][:, 0:n], [self.Bps[0]], [self.Bcomb])
                self.cp("dve", self.rs[:, 0:n], self.ps[1][:, 0:n], [self.Bps[1]], [self.Brs])
                self.act(r0[:, 0:n], r0[:, 0:n], AF.Exp, [Br0], [Br0], scale=-1.0)
                self.act(r1[:, 0:n], r1[:, 0:n], AF.Exp, [Br1], [Br1], scale=-1.0)
                self.tt("dve", r0[:, 0:n], self.comb[:, 0:n], r0[:, 0:n], ALU.mult, [self.Bcomb, Br0], [Br0])
                self.tt("dve", r1[:, 0:n], self.rs[:, 0:n], r1[:, 0:n], ALU.mult, [self.Brs, Br1], [Br1])
                self.stt("dve", self.comb[:, 0:n], r1[:, 0:n], nlam, r0[:, 0:n], ALU.mult, ALU.add,
                         [Br0, Br1, self.Bsmall], [self.Bcomb])
                self.act(self.sqd[:, 0:n], self.comb[:, 0:n], AF.Square, [self.Bcomb], [self.Bsqd])

                def tail_b(hd=hd, s0=s0, n=n):
                    pm, Bpm = self.bank("S3", [4, 5, 6])
                    self.mm(pm[:, 0:n], self.ones[128][:], self.sqd[:, 0:n], True, True, [self.Bsqd, self.Bconst], [Bpm])
                    rstd, Brs = self.rstd_from_psum(pm, Bpm, n)
                    self.stt("dve", Oall[:, hd, s0:s0 + n], self.comb[:, 0:n], sg, rstd[:, 0:n], ALU.mult, ALU.mult,
                             [self.Bcomb, Brs, self.Bsmall], [self.BO])

                self.da_pending_tail = tail_b
        if self.da_pending_tail is not None:
            self.da_pending_tail()
            self.da_pending_tail = None
        self.dump("Oall", Oall, [self.BO])
        self.out_proj_residual(b, l, f"daout{j}", Oall, self.BO, ctx_out)

    def proj_rope(self, w, Bw, av, Ba, s0, n, kind, dst, Bdst, rope_name, banks, outs=None):
        pb, Bpb = self.bank("pr" + "".join(str(x) for x in banks), banks)
        for kc in range(8):
            self.mm(pb[:, 0:n], w[:, kc, :], av[:, kc, s0:s0 + n], kc == 0, kc == 7, [Bw, Ba], [Bpb])
        if outs is None:
            outs = [(0, 128, dst, Bdst)]
        if kind == "ctx":
            for (r0, r1, d_, Bd_) in outs:
                self.act(d_, pb[r0:r1, 0:n], AF.Identity, [Bpb], [Bd_])
            return
        ri = self.rot("rope", self.NROPE)
        rt, Brt = self.rope[ri], self.Brope[ri]
        l0 = s0 - CTX
        self.dma("sp", rt[:, 0, 0:n], self.I[rope_name][0, :, l0:l0 + n], [], [Brt], in_barrier=False)
        self.dma("sp", rt[:, 1, 0:n], self.I[rope_name][1, :, l0:l0 + n], [], [Brt], in_barrier=False)
        qf, Bqf = self.ftmp()
        self.act(qf[:, 0:n], pb[:, 0:n], AF.Identity, [Bpb], [Bqf])
        sw, Bsw = self.ftmp()
        for g in range(4):
            src = g ^ 1
            self.cp("dve", sw[g * 32:(g + 1) * 32, 0:n], qf[src * 32:(src + 1) * 32, 0:n], [Bqf], [Bsw])
        self.tt("dve", sw[:, 0:n], sw[:, 0:n], rt[:, 1, 0:n], ALU.mult, [Bsw, Brt], [Bsw])
        self.tt("dve", qf[:, 0:n], qf[:, 0:n], rt[:, 0, 0:n], ALU.mult, [Bqf, Brt], [Bqf])
        for (r0, r1, d_, Bd_) in outs:
            self.tt("dve", d_, qf[r0:r1, 0:n], sw[r0:r1, 0:n], ALU.add, [Bqf, Bsw], [Bd_])


    def mixer_mla(self, b, l, j, ctx_out):
        P, I = self.P, self.I
        P.barrier()
        arA_bf = self.arA_bf
        cqn = arA_bf[:, 0:2 * T].rearrange("p (c t) -> p c t", c=2)
        ckvn = arA_bf[:, 2 * T:3 * T]
        krT = arA_bf[:, 3 * T:4 * T]
        ablk = [arA_bf[:, 4 * T + i * 4096:4 * T + (i + 1) * 4096].rearrange("p (c n) -> p c n", c=NCH) for i in range(2)]
        if not hasattr(self, "Bmla"):
            self.Bmla = {k: P.buf("mla_" + k) for k in ("a0", "a1", "cqn", "ckvn", "kr")}
        if not hasattr(self, "BO"):
            self.BO = P.buf("Oall")
        Bm = self.Bmla
        Oall = self.arB.bitcast(BF16)[:].rearrange("p (c t) -> p c t", c=NCH)
        wd = I["w_mla_down"][j].rearrange("(k p) n -> p k n", p=128)
        qg = self.small[:, 2:4]
        kvg = self.small[:, 4:5]
        hvv = self.hv[:, 0:18 * 128].rearrange("p (t e) -> p t e", e=128)
        self.memset("dve", hvv[:, :, 64:128], 1.0, [self.Bhv])
        self.memset("dve", self.hk[96:128, :], 0.0, [self.Bhk])
        for i4 in range(2):
            self.memset("dve", self.hqs[i4][96:128, :], 0.0, [self.Bhqs[i4]])
        PB6 = [0, 1, 2, 3, 4, 5]
        for blk, (s0, n, kind) in enumerate(BLOCKS):
            par = blk % 2
            a2 = ablk[par]
            Ba2 = Bm["a%d" % par]
            self.prenorm(b, l, 0, blk, lambda c, a2=a2: a2[:, c, :], Ba2)
            slot, Bs = self.sload("mladown", 0)
            sv = slot[:].rearrange("p (k n) -> p k n", k=8)
            outs = []
            for (c0, c1) in [(0, 128), (128, 256), (256, 384), (320, 416), (416, 512)]:
                if c0 == 416 and kind == "ctx":
                    outs.append(None)
                    continue
                pb, Bpb = self.bank("proj6", PB6)
                M = c1 - c0
                for kc in range(8):
                    self.mm(pb[0:M, 0:n], sv[:, kc, c0:c1], a2[:, kc, 0:n], kc == 0, kc == 7, [Bs, Ba2], [Bpb])
                outs.append((pb, Bpb))
            (pA, BpA), (pB, BpB), (pC, BpC), (pD, BpD) = outs[0:4]
            pm, Bpm = self.bank("gen", [6, 7])
            for i_, (pp, Bpp) in enumerate([(pA, BpA), (pB, BpB)]):
                sq, Bsq = self.btmp()
                self.act(sq[:, 0:n], pp[:, 0:n], AF.Square, [Bpp], [Bsq])
                self.mm(pm[:, 0:n], self.ones[256][:], sq[:, 0:n], i_ == 0, i_ == 1, [Bsq, self.Bconst], [Bpm])
            rstd, Br = self.rstd_from_psum(pm, Bpm, n)
            for i_, (pp, Bpp) in enumerate([(pA, BpA), (pB, BpB)]):
                self.stt("dve", cqn[:, i_, s0:s0 + n], pp[:, 0:n], qg[:, i_:i_ + 1], rstd[:, 0:n], ALU.mult, ALU.mult,
                         [Bpp, Br, self.Bconst], [Bm["cqn"]])
            pm, Bpm = self.bank("gen", [6, 7])
            sq, Bsq = self.btmp()
            self.act(sq[:, 0:n], pC[:, 0:n], AF.Square, [BpC], [Bsq])
            self.mm(pm[:, 0:n], self.ones[128][:], sq[:, 0:n], True, True, [Bsq, self.Bconst], [Bpm])
            rstd, Br = self.rstd_from_psum(pm, Bpm, n)
            self.stt("dve", ckvn[:, s0:s0 + n], pC[:, 0:n], kvg, rstd[:, 0:n], ALU.mult, ALU.mult,
                     [BpC, Br, self.Bconst], [Bm["ckvn"]])
            if kind == "ctx":
                self.act(krT[64:96, s0:s0 + n], pD[64:96, 0:n], AF.Identity, [BpD], [Bm["kr"]])
            else:
                pE, BpE = outs[4]
                self.rope_rows(pD, BpD, pE, BpE, s0, n, krT[64:96, s0:s0 + n], Bm["kr"])
        wuq = I["w_mla_uq"][j].rearrange("(k p) n -> p k n", p=128)
        wukv = I["w_mla_ukv"][j]
        for hd in range(16):
            slot, Bs = self.sload("mlahead", hd)
            wqn = slot[:, 0:192].rearrange("p (k n) -> p k n", k=2)
            wqs = slot[:, 192:384].rearrange("p (k n) -> p k n", k=2)
            wkv = slot[:, 384:512]
            for blk, (s0, n, kind) in enumerate(BLOCKS):
                pb, Bpb = self.bank("S", [4, 5, 6, 7])
                self.mm(pb[0:64, 0:n], wkv[:, 0:64], ckvn[:, s0:s0 + n], True, True, [Bs, Bm["ckvn"]], [Bpb])
                self.cp("dve", self.hk[0:64, s0:s0 + n], pb[0:64, 0:n], [Bpb], [self.Bhk])
            self.cp("dve", self.hk[64:96, :], krT[64:96, :], [Bm["kr"]], [self.Bhk])
            for t0 in (0, 8, 16):
                k = min(8, 18 - t0)
                pb, Bpb = self.bank("S", [4, 5, 6, 7])
                for q in range(k):
                    tt_ = t0 + q
                    self.mm(pb[:, q * 64:(q + 1) * 64], ckvn[:, tt_ * 128:(tt_ + 1) * 128], wkv[:, 64:128], True, True,
                            [Bs, Bm["ckvn"]], [Bpb])
                self.cp("dve", hvv[:, t0:t0 + k, 0:64], pb[:, 0:k * 64].rearrange("p (t e) -> p t e", e=64),
                        [Bpb], [self.Bhv])
            qblocks = [k for k in range(len(BLOCKS)) if not (BLOCKS[k][2] == "ctx" and not ctx_out)]

            def qproj(blk):
                s0, n, kind = BLOCKS[blk]
                qi = self.rot("hq", 2)
                hq, Bhq = self.hqs[qi], self.Bhqs[qi]
                pA, BpA = self.ps[2], self.Bps[2]
                for kc in range(2):
                    self.mm(pA[0:96, 0:n], wqn[:, kc, :], cqn[:, kc, s0:s0 + n], kc == 0, kc == 1, [Bs, Bm["cqn"]], [BpA])
                self.cp("dve", hq[0:64, 0:n], pA[0:64, 0:n], [BpA], [Bhq])
                if kind == "ctx":
                    self.cp("dve", hq[64:96, 0:n], pA[64:96, 0:n], [BpA], [Bhq])
                else:
                    pB, BpB = self.ps[3], self.Bps[3]
                    for kc in range(2):
                        self.mm(pB[0:96, 0:n], wqs[:, kc, :], cqn[:, kc, s0:s0 + n], kc == 0, kc == 1, [Bs, Bm["cqn"]], [BpB])
                    self.rope_rows(pA, BpA, pB, BpB, s0, n, hq[64:96, 0:n], Bhq)
                return hq, Bhq

            nxt = qproj(qblocks[0])
            for bi, blk in enumerate(qblocks):
                s0, n, kind = BLOCKS[blk]
                hq, Bhq = nxt
                if bi + 1 < len(qblocks):
                    nxt = qproj(qblocks[bi + 1])
                kts = [0, 1] if kind == "ctx" else list(range(18))
                oi = self.rot("mlaO", 2)
                pO, BpO = self.ps[oi], self.Bps[oi]
                pend = []
                LAG = 2

                def issue_S(kt):
                    pS, BpS = self.bank("S", [4, 5, 6, 7])
                    self.mm(pS[:, 0:n], self.hk[:, kt * 128:(kt + 1) * 128], hq[:, 0:n], True, True,
                            [self.Bhk, Bhq], [BpS])
                    E, BE = self.btmp()
                    self.act(E[:, 0:n], pS[:, 0:n], AF.Exp, [BpS], [BE], scale=MLA_SCALE)
                    return (kt, E, BE)

                def issue_AV(item):
                    kt, E, BE = item
                    self.mm(pO[:, 0:n], hvv[:, kt, :], E[:, 0:n], kt == kts[0], kt == kts[-1], [self.Bhv, BE], [BpO])

                for i, kt in enumerate(kts):
                    pend.append(issue_S(kt))
                    if i >= LAG:
                        issue_AV(pend[i - LAG])
                for i in range(max(0, len(kts) - LAG), len(kts)):
                    issue_AV(pend[i])
                r, Brr = self.ftmp()
                self.recip(r[64:128, 0:n], pO[64:128, 0:n], [BpO], [Brr])
                rb, Brb = self.ftmp()
                self.cp("dve", rb[0:64, 0:n], r[64:128, 0:n], [Brr], [Brb])
                on, Bon = self.ftmp()
                self.tt("dve", on[0:64, 0:n], pO[0:64, 0:n], rb[0:64, 0:n], ALU.mult, [BpO, Brb], [Bon])
                po = (hd % 2) * 64
                self.cp("dve", Oall[po:po + 64, hd // 2, s0:s0 + n], on[0:64, 0:n], [Bon], [self.BO])
        self.out_proj_residual(b, l, "mlaout", Oall, self.BO, ctx_out)

    def rope_rows(self, pA, BpA, pB, BpB, s0, n, dst, Bdst):
        ri = self.rot("rope", self.NROPE)
        rt, Brt = self.rope[ri], self.Brope[ri]
        l0 = s0 - CTX
        self.dma("sp", rt[64:96, 0, 0:n], self.I["rope_mla"][0, 64:96, l0:l0 + n], [], [Brt], in_barrier=False)
        self.dma("sp", rt[64:96, 1, 0:n], self.I["rope_mla"][1, 64:96, l0:l0 + n], [], [Brt], in_barrier=False)
        t1, Bt1 = self.ftmp()
        t2, Bt2 = self.ftmp()
        self.tt("dve", t1[64:96, 0:n], pA[64:96, 0:n], rt[64:96, 0, 0:n], ALU.mult, [BpA, Brt], [Bt1])
        self.tt("dve", t2[64:96, 0:n], pB[64:96, 0:n], rt[64:96, 1, 0:n], ALU.mult, [BpB, Brt], [Bt2])
        self.tt("dve", dst, t1[64:96, 0:n], t2[64:96, 0:n], ALU.add, [Bt1, Bt2], [Bdst])


    def mixer_sc(self, b, l, j, ctx_out):
        P, I = self.P, self.I
        P.barrier()
        av = self.compute_a_all(b, l)
        zT = self.arB.bitcast(BF16)[:].rearrange("p (c t) -> p c t", c=NCH)
        if not hasattr(self, "BO"):
            self.BO = P.buf("Oall")
        wi = I["w_sc_in"][j].rearrange("(k p) n -> p k n", p=128)
        seqs = [(CTX, SEQ)]
        if ctx_out:
            seqs = [(0, CTX), (CTX, SEQ)]
        for c in range(NCH):
            slot, Bs = self.sload("scin", c)
            sv = slot[:, 0:3072].rearrange("p (t k n) -> p t k n", t=3, k=8)
            k0 = self.small[:, 8 + c:9 + c]
            k1 = self.small[:, 16 + c:17 + c]
            k2 = self.small[:, 24 + c:25 + c]
            for (q0, qlen) in seqs:
                t0 = 0
                while t0 < qlen:
                    m = min(510, qlen - t0)
                    lo, hi = max(t0 - 1, 0), min(t0 + m + 1, qlen)
                    N = hi - lo
                    reads_a = [self.Ba[k] for k in range(len(BLOCKS))
                               if BLOCKS[k][0] < q0 + hi and BLOCKS[k][0] + BLOCKS[k][1] > q0 + lo]
                    pbg, Bpbg = self.bank("proj", [0, 1, 2, 3])
                    pcg, Bpcg = self.bank("proj", [0, 1, 2, 3])
                    phh, Bphh = self.bank("proj", [0, 1, 2, 3])
                    for kc in range(8):
                        self.mm(pbg[:, 0:m], sv[:, 0, kc, :], av[:, kc, q0 + t0:q0 + t0 + m], kc == 0, kc == 7, [Bs] + reads_a, [Bpbg])
                    for kc in range(8):
                        self.mm(pcg[:, 0:N], sv[:, 1, kc, :], av[:, kc, q0 + lo:q0 + hi], kc == 0, kc == 7, [Bs] + reads_a, [Bpcg])
                    for kc in range(8):
                        self.mm(phh[:, 0:N], sv[:, 2, kc, :], av[:, kc, q0 + lo:q0 + hi], kc == 0, kc == 7, [Bs] + reads_a, [Bphh])
                    cg, Bcg = self.ftmp()
                    xt, Bxt = self.ftmp()
                    u, Bu = self.ftmp()
                    self.act(cg[:, 0:N], pcg[:, 0:N], AF.Identity, [Bpcg], [Bcg])
                    self.tt("dve", xt[:, 0:N], phh[:, 0:N], cg[:, 0:N], ALU.mult, [Bphh, Bcg], [Bxt])
                    cen = t0 - lo
                    self.ts("dve", u[:, 0:m], xt[:, cen:cen + m], k1, None, ALU.mult, None, [Bxt, self.Bconst], [Bu])
                    ta = max(t0, 1)
                    cnt = t0 + m - ta
                    if cnt > 0:
                        self.stt("dve", u[:, ta - t0:ta - t0 + cnt], xt[:, ta - 1 - lo:ta - 1 - lo + cnt], k0,
                                 u[:, ta - t0:ta - t0 + cnt], ALU.mult, ALU.add, [Bxt, Bu, self.Bconst], [Bu])
                    tb = min(t0 + m - 1, qlen - 2)
                    cnt = tb - t0 + 1
                    if cnt > 0:
                        self.stt("dve", u[:, 0:cnt], xt[:, t0 + 1 - lo:t0 + 1 - lo + cnt], k2,
                                 u[:, 0:cnt], ALU.mult, ALU.add, [Bxt, Bu, self.Bconst], [Bu])
                    self.tt("dve", zT[:, c, q0 + t0:q0 + t0 + m], pbg[:, 0:m], u[:, 0:m], ALU.mult, [Bpbg, Bu], [self.BO])
                    t0 += m
        self.out_proj_residual(b, l, "scout", zT, self.BO, ctx_out)

    def mlp(self, b, l, ctx_out):
        P, I = self.P, self.I
        P.barrier()
        if not hasattr(self, "Ba2"):
            self.Ba2 = [P.buf("a2_0"), P.buf("a2_1")]
            self.Bhid = [P.buf(f"hid{jg}") for jg in range(8)]
            self.Bfst = P.buf("fstage")
        arA_bf = self.arA_bf
        a2v = [arA_bf[:, 0:4096].rearrange("p (c n) -> p c n", c=NCH),
               arA_bf[:, 4096:8192].rearrange("p (c n) -> p c n", c=NCH)]
        fbuf = self.arA[:, 4096:4096 + NCH * 512].rearrange("p (c n) -> p c n", c=NCH)
        hid = self.arB.bitcast(BF16)[:, 0:32 * 512].rearrange("p (j n) -> p j n", j=32)
        w_in = I["w_mlp_in"][l].rearrange("(k p) n -> p k n", p=128)
        w_out = I["w_mlp_out"][l].rearrange("(j p) n -> p j n", p=128)
        blks = [k for k in range(len(BLOCKS)) if not (BLOCKS[k][2] == "ctx" and not ctx_out)]

        def pre(i):
            a2_ = a2v[i % 2]
            self.prenorm(b, l, 1, blks[i], lambda c, a2_=a2_: a2_[:, c, :], self.Ba2[i % 2])

        pre(0)
        pending_post = None
        for bi, blk in enumerate(blks):
            s0, n, kind = BLOCKS[blk]
            par = bi % 2
            a2 = a2v[par]
            for jg in range(8):
                if jg == 1 and pending_post is not None:
                    self.postnorm_residual(b, l, 1, pending_post, fbuf, self.Bfst)
                    pending_post = None
                if jg == 3 and bi + 1 < len(blks):
                    pre(bi + 1)
                slot, Bs = self.sload(f"win{l}", jg)
                sv = slot[:].rearrange("p (k n) -> p k n", k=8)
                for jj in range(4):
                    jx = jg * 4 + jj
                    pb, Bpb = self.bank("proj", [0, 1, 2, 3])
                    for kc in range(8):
                        self.mm(pb[:, 0:n], sv[:, kc, jj * 128:(jj + 1) * 128], a2[:, kc, 0:n], kc == 0, kc == 7,
                                [Bs, self.Ba2[par]], [Bpb])
                    t, Bt = self.ftmp()
                    self.act(t[:, 0:n], pb[:, 0:n], AF.Relu, [Bpb], [Bt])
                    self.tt("dve", hid[:, jx, 0:n], t[:, 0:n], t[:, 0:n], ALU.mult, [Bt], [self.Bhid[jg]])
            for jg in range(8):
                slot, Bs = self.sload(f"wout{l}", jg)
                sv = slot[:].rearrange("p (j n) -> p j n", j=4)
                for fo in range(8):
                    for jj in range(4):
                        self.mm(self.ps[fo][:, 0:n], sv[:, jj, fo * 128:(fo + 1) * 128], hid[:, jg * 4 + jj, 0:n],
                                jg == 0 and jj == 0, jg == 7 and jj == 3, [Bs, self.Bhid[jg]], [self.Bps[fo]])
            for fo in range(8):
                if fo % 2 == 0:
                    self.act(fbuf[:, fo, 0:n], self.ps[fo][:, 0:n], AF.Identity, [self.Bps[fo]], [self.Bfst])
                else:
                    self.cp("dve", fbuf[:, fo, 0:n], self.ps[fo][:, 0:n], [self.Bps[fo]], [self.Bfst])
            pending_post = blk
        self.postnorm_residual(b, l, 1, pending_post, fbuf, self.Bfst)
        P.barrier()


def input_shapes(nb):
    return {
        "x": (nb, SEQ, D), "ctx": (nb, CTX, D), "cvecT": (D, 3), "ident": (128, 128),
        "w_ada": (DEPTH, D, 6 * D), "b_adaT": (DEPTH, 128, 48), "norm_gT": (128, DEPTH * 4 * NCH),
        "w_mlp_in": (DEPTH, D, HID), "w_mlp_out": (DEPTH, HID, D),
        "w_da_qkv": (2, D, 3 * D), "da_lambda": (2, 4, 64), "da_sublnT": (128, 2), "w_da_out": (2, D, D),
        "w_mla_down": (1, D, 416), "mla_q_normT": (128, 2), "w_mla_uq": (1, 256, 1536),
        "mla_kv_normT": (128, 1), "w_mla_ukv": (1, 128, 2048), "w_mla_out": (1, D, D),
        "w_sc_in": (1, D, 3 * D), "sc_convT": (128, 24), "w_sc_out": (1, D, D),
        "rope_da": (2, 128, SEQ), "rope_mla": (2, 128, SEQ),
    }


def rope_tables():
    t = np.arange(SEQ)
    row = (t // GRID_W).astype(np.float32)
    col = (t % GRID_W).astype(np.float32)

    def tab(rot_dim):
        quarter = rot_dim // 4
        inv = (ROPE_THETA ** (-np.arange(quarter, dtype=np.float32) / quarter)).astype(np.float32)
        ang = np.concatenate([row[:, None] * inv, col[:, None] * inv], axis=-1).astype(np.float32)
        return np.cos(ang).T.astype(np.float32), np.sin(ang).T.astype(np.float32)

    c32, s32 = tab(64)
    da = np.zeros((2, 128, SEQ), np.float32)
    for g in range(4):
        da[0, g * 32:(g + 1) * 32] = c32
        da[1, g * 32:(g + 1) * 32] = -s32 if g % 2 == 0 else s32
    c16, s16 = tab(32)
    mla = np.zeros((2, 128, SEQ), np.float32)
    mla[0, 64:80] = c16
    mla[0, 80:96] = c16
    mla[1, 64:80] = -s16
    mla[1, 80:96] = s16
    return da, mla


def make_in_maps(inputs, nb, n_cores):
    f = lambda a: np.ascontiguousarray(np.asarray(a, dtype=np.float32))
    da, mla = rope_tables()
    shared = {
        "ident": np.eye(128, dtype=np.float32),
        "w_ada": f(inputs["w_ada"]),
        "b_adaT": f(np.asarray(inputs["b_ada"]).reshape(DEPTH, 48, 128).transpose(0, 2, 1)),
        "norm_gT": f(np.asarray(inputs["norm_g"]).reshape(DEPTH * 4 * NCH, 128).T),
        "w_mlp_in": f(inputs["w_mlp_in"]), "w_mlp_out": f(inputs["w_mlp_out"]),
        "w_da_qkv": f(inputs["w_da_qkv"]), "da_lambda": f(inputs["da_lambda"]),
        "da_sublnT": f(np.asarray(inputs["da_subln"]).T), "w_da_out": f(inputs["w_da_out"]),
        "w_mla_down": f(inputs["w_mla_down"]),
        "mla_q_normT": f(np.asarray(inputs["mla_q_norm"]).reshape(2, 128).T),
        "w_mla_uq": f(inputs["w_mla_uq"]),
        "mla_kv_normT": f(np.asarray(inputs["mla_kv_norm"]).reshape(1, 128).T),
        "w_mla_ukv": f(inputs["w_mla_ukv"]), "w_mla_out": f(inputs["w_mla_out"]),
        "w_sc_in": f(inputs["w_sc_in"]),
        "sc_convT": f(np.asarray(inputs["sc_conv"]).reshape(3 * NCH, 128).T),
        "w_sc_out": f(inputs["w_sc_out"]),
        "rope_da": da, "rope_mla": mla,
    }
    x, c, ctx, c_ctx = (np.asarray(inputs[k], dtype=np.float32) for k in ("x", "c", "ctx", "c_ctx"))
    maps = []
    for i in range(n_cores):
        sl = slice(i * nb, (i + 1) * nb)
        cols = [c[i * nb + r] for r in range(nb)]
        while len(cols) < 2:
            cols.append(np.zeros(D, np.float32))
        cv = np.stack(cols + [c_ctx], axis=1)
        m = dict(shared)
        m["x"] = f(x[sl])
        m["ctx"] = f(ctx[sl])
        m["cvecT"] = f(cv)
        maps.append(m)
    return maps


_CACHE = {}


def run(inputs, nb, n_cores, n_layers=DEPTH, trace=False, dbg=""):
    key = (nb, n_layers, dbg)
    if key not in _CACHE:
        _CACHE[key] = Builder(nb, n_layers, dbg).build()
    nc = _CACHE[key]
    maps = make_in_maps(inputs, nb, n_cores)
    res = run_bass_kernel_spmd(nc, maps, core_ids=list(range(n_cores)), trace=trace)
    out = np.concatenate([r["y"] for r in res.results], axis=0)
    if dbg:
        return out, res.results[0]
    return out, res


def kernel(**inputs):
    out, _ = run(inputs, nb=2, n_cores=8)
    return out.astype(np.float32)
```
